# Optimizing a Trainium2 kernel written in Bass

```python
import jax, jax.numpy as jnp
from jax import lax
import numpy as np

D_MODEL = 1024
BATCH = 4
SEQ = 4096
DEPTH = 1

CHUNK = 64
Q_BLOCK = 128
HEAD_DIM = 64
N_SB_HEADS = 8
N_FOX_HEADS = 8
SB_WIDTH = N_SB_HEADS * HEAD_DIM
FOX_WIDTH = N_FOX_HEADS * HEAD_DIM
D_FF = -(-(8 * D_MODEL) // (3 * 256)) * 256
N_COND = 6
LN_EPS = 1e-5
DEEPNORM_ALPHA = (2 * DEPTH) ** 0.25
DEEPNORM_BETA = (8 * DEPTH) ** -0.25
OFF_SB = 0
OFF_FOX = OFF_SB + 3 * SB_WIDTH
OFF_FGATE = OFF_FOX + 3 * FOX_WIDTH
OFF_BGATE = OFF_FGATE + N_FOX_HEADS
IN_COLS = OFF_BGATE + 2 * D_MODEL

kernel_name = "hybrid_stickbreak_fox_gated_block"


def layer_norm(x, gain=None, bias=None):
    xf = x.astype(jnp.float32)
    mu = jnp.mean(xf, axis=-1, keepdims=True)
    var = jnp.mean(jnp.square(xf - mu), axis=-1, keepdims=True)
    y = (xf - mu) * lax.rsqrt(var + LN_EPS)
    if gain is not None:
        y = y * gain.astype(jnp.float32) + bias.astype(jnp.float32)
    return y.astype(x.dtype)


def modulate(x, shift, scale):
    return layer_norm(x) * (1.0 + scale[:, None, :]) + shift[:, None, :]


def split_heads(t, n_heads):
    b, s, _ = t.shape
    return t.reshape(b, s, n_heads, HEAD_DIM).transpose(0, 2, 1, 3)


def merge_heads(t):
    b, h, s, d = t.shape
    return t.transpose(0, 2, 1, 3).reshape(b, s, h * d)


def stick_breaking_attention(q, k, v):
    seq = q.shape[2]
    scale = HEAD_DIM ** -0.5
    outs = []
    for i in range(seq // Q_BLOCK):
        q0 = i * Q_BLOCK
        kv_len = q0 + Q_BLOCK
        qb = q[:, :, q0:kv_len]
        kb = k[:, :, :kv_len]
        vb = v[:, :, :kv_len]
        z = jnp.einsum('bhqd,bhkd->bhqk', qb, kb, preferred_element_type=jnp.float32) * scale
        qpos = q0 + jnp.arange(Q_BLOCK)[:, None]
        kpos = jnp.arange(kv_len)[None, :]
        mask = kpos < qpos
        log_1m = jnp.where(mask, jax.nn.log_sigmoid(-z), 0.0)
        suffix = lax.cumsum(log_1m, axis=3, reverse=True) - log_1m
        a = jnp.where(mask, jnp.exp(jax.nn.log_sigmoid(z) + suffix), 0.0)
        outs.append(jnp.einsum('bhqk,bhkd->bhqd', a.astype(v.dtype), vb))
    return jnp.concatenate(outs, axis=2)


def forgetting_attention(q, k, v, f_cum):
    seq = q.shape[2]
    scale = HEAD_DIM ** -0.5
    outs = []
    for i in range(seq // Q_BLOCK):
        q0 = i * Q_BLOCK
        kv_len = q0 + Q_BLOCK
        qb = q[:, :, q0:kv_len]
        kb = k[:, :, :kv_len]
        vb = v[:, :, :kv_len]
        z = (jnp.einsum('bhqd,bhkd->bhqk', qb, kb, preferred_element_type=jnp.float32) * scale
             + f_cum[:, :, q0:kv_len, None] - f_cum[:, :, None, :kv_len])
        qpos = q0 + jnp.arange(Q_BLOCK)[:, None]
        kpos = jnp.arange(kv_len)[None, :]
        p = jax.nn.softmax(jnp.where(kpos <= qpos, z, -jnp.inf), axis=-1)
        outs.append(jnp.einsum('bhqk,bhkd->bhqd', p.astype(v.dtype), vb))
    return jnp.concatenate(outs, axis=2)


def setup_inputs(seed: int = 0) -> dict:
    key = jax.random.key(seed)
    ks = jax.random.split(key, 20)
    f32 = jnp.float32
    nrm = lambda k, shape, s: (jax.random.normal(k, shape, f32) * s).astype(f32)
    return {
        "x": nrm(ks[0], (BATCH, SEQ, D_MODEL), 1.0),
        "c": nrm(ks[1], (BATCH, D_MODEL), 1.0),
        "w_ada": nrm(ks[2], (DEPTH, D_MODEL, N_COND * D_MODEL), 0.5 * D_MODEL ** -0.5),
        "b_ada": nrm(ks[3], (DEPTH, N_COND * D_MODEL), 0.02),
        "w_in": nrm(ks[4], (DEPTH, D_MODEL, IN_COLS), D_MODEL ** -0.5),
        "b_gate": nrm(ks[5], (DEPTH, 2 * D_MODEL), 0.02),
        "b_forget": 3.0 + nrm(ks[6], (DEPTH, N_FOX_HEADS), 1.0),
        "w_sb_out": nrm(ks[7], (DEPTH, SB_WIDTH, D_MODEL), SB_WIDTH ** -0.5),
        "w_fox_out": nrm(ks[8], (DEPTH, FOX_WIDTH, D_MODEL), FOX_WIDTH ** -0.5),
        "w_o": nrm(ks[9], (DEPTH, D_MODEL, D_MODEL), DEEPNORM_BETA * D_MODEL ** -0.5),
        "ln1_g": 1.0 + nrm(ks[10], (DEPTH, D_MODEL), 0.02),
        "ln1_b": nrm(ks[11], (DEPTH, D_MODEL), 0.02),
        "w_ffn_gate": nrm(ks[12], (DEPTH, D_MODEL, D_FF), D_MODEL ** -0.5),
        "w_ffn_up": nrm(ks[13], (DEPTH, D_MODEL, D_FF), D_MODEL ** -0.5),
        "w_ffn_down": nrm(ks[14], (DEPTH, D_FF, D_MODEL), DEEPNORM_BETA * D_FF ** -0.5),
        "ln2_g": 1.0 + nrm(ks[15], (DEPTH, D_MODEL), 0.02),
        "ln2_b": nrm(ks[16], (DEPTH, D_MODEL), 0.02),
    }


def reference(x, c, w_ada, b_ada, w_in, b_gate, b_forget, w_sb_out, w_fox_out, w_o,
              ln1_g, ln1_b, w_ffn_gate, w_ffn_up, w_ffn_down, ln2_g, ln2_b):
    c_act = jax.nn.silu(c)
    for l in range(DEPTH):
        ada = c_act @ w_ada[l] + b_ada[l]
        sh1, sc1, g1, sh2, sc2, g2 = jnp.split(ada, N_COND, axis=-1)

        u = modulate(x, sh1, sc1)
        proj = u @ w_in[l]
        q_sb = split_heads(proj[..., OFF_SB:OFF_SB + SB_WIDTH], N_SB_HEADS)
        k_sb = split_heads(proj[..., OFF_SB + SB_WIDTH:OFF_SB + 2 * SB_WIDTH], N_SB_HEADS)
        v_sb = split_heads(proj[..., OFF_SB + 2 * SB_WIDTH:OFF_FOX], N_SB_HEADS)
        q_fx = split_heads(proj[..., OFF_FOX:OFF_FOX + FOX_WIDTH], N_FOX_HEADS)
        k_fx = split_heads(proj[..., OFF_FOX + FOX_WIDTH:OFF_FOX + 2 * FOX_WIDTH], N_FOX_HEADS)
        v_fx = split_heads(proj[..., OFF_FOX + 2 * FOX_WIDTH:OFF_FGATE], N_FOX_HEADS)
        f_logit = proj[..., OFF_FGATE:OFF_BGATE].astype(jnp.float32) + b_forget[l].astype(jnp.float32)
        f_cum = jnp.cumsum(jax.nn.log_sigmoid(f_logit), axis=1).transpose(0, 2, 1)
        gate_logit = proj[..., OFF_BGATE:] + b_gate[l]
        g_sb = jax.nn.sigmoid(gate_logit[..., :D_MODEL])
        g_fx = jax.nn.sigmoid(gate_logit[..., D_MODEL:])

        y_sb = merge_heads(stick_breaking_attention(q_sb, k_sb, v_sb)) @ w_sb_out[l]
        y_fx = merge_heads(forgetting_attention(q_fx, k_fx, v_fx, f_cum)) @ w_fox_out[l]
        mix = (g_sb * y_sb + g_fx * y_fx) @ w_o[l]
        x = layer_norm(DEEPNORM_ALPHA * x + g1[:, None, :] * mix, ln1_g[l], ln1_b[l])

        u = modulate(x, sh2, sc2)
        h = (jax.nn.silu(u @ w_ffn_gate[l]) * (u @ w_ffn_up[l])) @ w_ffn_down[l]
        x = layer_norm(DEEPNORM_ALPHA * x + g2[:, None, :] * h, ln2_g[l], ln2_b[l])
    return x
```

```python
import numpy as np
from contextlib import ExitStack
import concourse.bass as bass
import concourse.mybir as mybir
from concourse.bass_utils import run_bass_kernel_spmd

F32 = mybir.dt.float32
BF16 = mybir.dt.bfloat16
AF = mybir.ActivationFunctionType
ALU = mybir.AluOpType

D = 1024
KC = 8
S = 4096
NB = 32
NOWN = 16
DFF = 2816
FC = 22
IN_COLS = 5128
OFF_SB = 0
OFF_FOX = 1536
OFF_FG = 3072
OFF_BG = 3080
LN_EPS = 1e-5
ALPHA = 2.0 ** 0.25
NEG = -30000.0

COMPUTE = ("pe", "act", "dve", "pool")


class Sched:
    def __init__(self, nc, stack, n_dma_sems=64):
        self.nc = nc
        self.prog = {e: [] for e in ("pe", "act", "dve", "pool", "sp")}
        self.cnt = {e: 0 for e in COMPUTE}
        self.sem = {e: stack.enter_context(nc.semaphore("s_" + e)) for e in COMPUTE}
        self.dma_pool = [stack.enter_context(nc.semaphore("d%d" % i)) for i in range(n_dma_sems)]
        self.dma_key = {}
        self.dma_cnt = {}
        self.known = {e: {} for e in self.prog}
        self.lastw = {}
        self.reads = {}

    def _semh(self, k):
        return self.sem[k] if k in self.sem else self.dma_pool[self.dma_key[k]]

    def _deps(self, eng, reads, writes):
        deps = set()
        for b in reads:
            ev = self.lastw.get(b)
            if ev:
                deps.add(ev)
        for b in writes:
            ev = self.lastw.get(b)
            if ev:
                deps.add(ev)
            for ev in self.reads.get(b, ()):
                deps.add(ev)
        waits = {}
        kn = self.known[eng]
        for (k, v) in deps:
            if k == eng and eng == "pe":
                continue
            if kn.get(k, 0) >= v:
                continue
            if waits.get(k, 0) < v:
                waits[k] = v
        for k, v in waits.items():
            kn[k] = v
        return [(self._semh(k), v) for k, v in waits.items()]

    def _record(self, ev, reads, writes):
        for b in reads:
            self.reads.setdefault(b, []).append(ev)
        for b in writes:
            self.lastw[b] = ev
            self.reads[b] = []

    def op(self, eng, fn, reads=(), writes=(), inc=True):
        waits = self._deps(eng, reads, writes)
        if inc:
            self.cnt[eng] += 1
            ev = (eng, self.cnt[eng])
        else:
            ev = (eng, self.cnt[eng] + 1)
        self._record(ev, reads, writes)
        self.prog[eng].append((waits, fn, (self.sem[eng], 1) if inc else None))

    def dma(self, queue, out, in_, reads=(), writes=(), key=None, **kw):
        assert key is not None
        if key not in self.dma_key:
            idx = len(self.dma_key)
            assert idx < len(self.dma_pool), "out of dma semaphores"
            self.dma_key[key] = idx
            self.dma_cnt[key] = 0
        waits = self._deps(queue, reads, writes)
        self.dma_cnt[key] += 16
        ev = (key, self.dma_cnt[key])
        self._record(ev, reads, writes)
        fn = lambda e, out=out, in_=in_, kw=kw: e.dma_start(out=out, in_=in_, **kw)
        self.prog[queue].append((waits, fn, (self._semh(key), 16)))

    def group_final(self, key, bufs):
        for b in bufs:
            self.lastw[b] = (key, self.dma_cnt[key])

    def wait_all(self, eng, keys):
        waits = []
        for k in keys:
            v = self.dma_cnt[k] if k in self.dma_cnt else self.cnt[k]
            if v > 0:
                waits.append((self._semh(k), v))
        self.prog[eng].append((waits, None, None))

    def barrier(self):
        keys = list(COMPUTE) + list(self.dma_cnt.keys())
        for eng in self.prog:
            waits = []
            for k in keys:
                v = self.dma_cnt[k] if k in self.dma_cnt else self.cnt[k]
                if k == eng or v == 0:
                    continue
                if self.known[eng].get(k, 0) >= v:
                    continue
                self.known[eng][k] = v
                waits.append((self._semh(k), v))
            if waits:
                self.prog[eng].append((waits, None, None))

    def emit(self, eng, e):
        for waits, fn, inc in self.prog[eng]:
            for (h, v) in waits:
                e.wait_ge(h, v)
            if fn is None:
                continue
            ins = fn(e)
            if inc is not None:
                ins.then_inc(inc[0], inc[1])


class Arena:
    def __init__(self, nc, stack, nbytes):
        self.h32 = stack.enter_context(nc.sbuf_tensor("arena", [128, nbytes // 4], F32))
        self.h16 = self.h32.bitcast(BF16)
        self.top = 0
        self.limit = nbytes
        self.peak = 0

    def alloc(self, dtype, *free):
        n = 1
        for f in free:
            n *= f
        sz = 4 if dtype == F32 else 2
        nb = (n * sz + 63) // 64 * 64
        off = self.top
        self.top += nb
        self.peak = max(self.peak, self.top)
        assert self.top <= self.limit, "arena overflow %d > %d" % (self.top, self.limit)
        base = self.h32 if dtype == F32 else self.h16
        e0 = off // sz
        ap = base[:, e0:e0 + n]
        if len(free) == 2:
            ap = ap.rearrange("p (a b) -> p a b", a=free[0], b=free[1])
        elif len(free) == 3:
            ap = ap.rearrange("p (a b c) -> p a b c", a=free[0], b=free[1], c=free[2])
        return ap

    def mark(self):
        return self.top

    def release(self, m):
        self.top = m


def sb_of(g):
    return g // 2 if g % 2 == 0 else 16 + g // 2


def build_nc(debug=False):
    nc = bass.Bass("TRN2", target_bir_lowering=False)
    dt = nc.dram_tensor
    xall = dt("xall", [S, D], F32, kind="ExternalInput").ap()
    c_col = dt("c_col", [128, KC], F32, kind="ExternalInput").ap()
    w_ada = dt("w_ada", [128, KC, 6 * D], F32, kind="ExternalInput").ap()
    b_ada = dt("b_ada", [1, 6 * D], F32, kind="ExternalInput").ap()
    w_in = dt("w_in", [128, KC, IN_COLS], F32, kind="ExternalInput").ap()
    b_gate = dt("b_gate", [128, 16], F32, kind="ExternalInput").ap()
    b_forget = dt("b_forget", [8, 1], F32, kind="ExternalInput").ap()
    w_sb = dt("w_sb", [128, 4, D], F32, kind="ExternalInput").ap()
    w_fx = dt("w_fx", [128, 4, D], F32, kind="ExternalInput").ap()
    w_o = dt("w_o", [128, KC, D], F32, kind="ExternalInput").ap()
    lnp = dt("lnp", [4, D], F32, kind="ExternalInput").ap()
    w_g = dt("w_g", [128, KC, DFF], F32, kind="ExternalInput").ap()
    w_u = dt("w_u", [128, KC, DFF], F32, kind="ExternalInput").ap()
    w_d = dt("w_d", [128, FC, D], F32, kind="ExternalInput").ap()
    consts = dt("consts", [128, 4 * 128 + 512], F32, kind="ExternalInput").ap()
    out = dt("out", [NOWN * 128, D], F32, kind="ExternalOutput").ap()
    fc_scr = dt("fc_scr", [8, 3, S], BF16, kind="Internal").ap()
    x1_scr = dt("x1_scr", [NOWN * 128, D], F32, kind="Internal").ap()
    dbg = {}
    if debug:
        dbg["uT"] = dt("dbg_uT", [128, KC, S], BF16, kind="ExternalOutput").ap()
        dbg["attnT"] = dt("dbg_attnT", [128, KC, NOWN * 128], BF16, kind="ExternalOutput").ap()
        dbg["ada"] = dt("dbg_ada", [128, 48], F32, kind="ExternalOutput").ap()
        dbg["fcn"] = dt("dbg_fcn", [8, S], F32, kind="ExternalOutput").ap()
        dbg["x1"] = dt("dbg_x1", [NOWN * 128, D], F32, kind="ExternalOutput").ap()
        dbg["sml"] = dt("dbg_sml", [128, 4], F32, kind="ExternalOutput").ap()
        dbg["stt"] = dt("dbg_stt", [128, 16], F32, kind="ExternalOutput").ap()
        dbg["xn"] = dt("dbg_xn", [128, D], F32, kind="ExternalOutput").ap()
        dbg["s1col"] = dt("dbg_s1col", [128, 8], F32, kind="ExternalOutput").ap()

    with ExitStack() as stack:
        S_ = Sched(nc, stack)
        AR = Arena(nc, stack, 206 * 1024)
        ps = stack.enter_context(nc.psum_tensor("ps", [128, 8 * 512], F32))

        def bank(i):
            return ps[:, i * 512:(i + 1) * 512]

        op, dma = S_.op, S_.dma

        ident_f = AR.alloc(F32, 128)
        cst_b = AR.alloc(BF16, 4 * 128 + 512)
        ident_b = cst_b[:, 0:128]
        triS_b = cst_b[:, 128:256]
        triI_b = cst_b[:, 256:384]
        negU_b = cst_b[:, 384:512]
        dm_b = cst_b[:, 512:1024]
        negones_b = AR.alloc(BF16, 128)
        adacol = AR.alloc(F32, 48)
        s1col = AR.alloc(F32, 8)
        s2col = AR.alloc(F32, 8)
        bgcol = AR.alloc(F32, 16)
        fcncol = AR.alloc(F32, 256)
        one_f = AR.alloc(F32, 128)
        GB = AR.alloc(F32, 2, D)
        epsc = AR.alloc(F32, 1)
        uT_own = AR.alloc(BF16, KC, NOWN * 128)
        m_attnT = AR.mark()
        attnT = AR.alloc(BF16, KC, NOWN * 128)
        op("dve", lambda e: e.memset(epsc, LN_EPS), writes=["epsc"])

        dma("sp", ident_f, consts[:, 0:128], writes=["ident_f"], key="c0")
        dma("pool", cst_b, consts, writes=["cst_b"], key="c1")
        dma("sp", bgcol, b_gate, writes=["bgcol"], key="c2")
        op("dve", lambda e: e.memset(negones_b, -1.0), writes=["negones"])
        op("dve", lambda e: e.memset(one_f, 1.0), writes=["one_f"])

        m0 = AR.mark()
        ccol = AR.alloc(F32, KC)
        cact = AR.alloc(F32, KC)
        badar = AR.alloc(F32, 6 * D)
        adar = AR.alloc(F32, 6 * D)
        wst = [AR.alloc(F32, KC, 512) for _ in range(2)]
        dma("sp", ccol, c_col, writes=["ccol"], key="c3")
        dma("sp", badar[0:1, :], b_ada, writes=["badar"], key="c4")
        op("act", lambda e: e.activation(out=cact, in_=ccol, func=AF.Silu), reads=["ccol"], writes=["cact"])
        for ng in range(12):
            sl = ng % 2
            dma("sp", wst[sl], w_ada[:, :, ng * 512:(ng + 1) * 512], writes=[("wst", sl)], key=("wst", sl))
            pb = bank(sl)
            for kc in range(KC):
                op("pe", lambda e, pb=pb, kc=kc, sl=sl: e.matmul(pb[0:1, :], cact[:, kc:kc + 1], wst[sl][:, kc, :],
                                                              start=(kc == 0), stop=(kc == KC - 1)),
                   reads=["cact", ("wst", sl)], writes=[("ps", sl)], inc=(kc == KC - 1))
            op("dve", lambda e, pb=pb, ng=ng: e.tensor_tensor(out=adar[0:1, ng * 512:(ng + 1) * 512], in0=pb[0:1, :],
                                                              in1=badar[0:1, ng * 512:(ng + 1) * 512], op=ALU.add),
               reads=[("ps", sl), "badar"], writes=[("adar", ng)])
        for c in range(48):
            op("pe", lambda e, c=c: e.matmul(bank(2)[:, c:c + 1], adar[0:1, c * 128:(c + 1) * 128], one_f[0:1, 0:1],
                                             start=True, stop=True),
               reads=[("adar", c // 4), "one_f"], writes=[("ps", 2)], inc=(c == 47))
        op("dve", lambda e: e.tensor_copy(out=adacol, in_=bank(2)[:, 0:48]), reads=[("ps", 2)], writes=["adacol"])
        op("dve", lambda e: e.tensor_scalar(out=s1col, in0=adacol[:, 8:16], scalar1=1.0, scalar2=None, op0=ALU.add),
           reads=["adacol"], writes=["s1col"])
        op("dve", lambda e: e.tensor_scalar(out=s2col, in0=adacol[:, 32:40], scalar1=1.0, scalar2=None, op0=ALU.add),
           reads=["adacol"], writes=["s2col"])
        for i, src in enumerate((2 * D, 5 * D)):
            for hf in range(2):
                b_ = 3 + (i * 2 + hf) % 2
                op("pe", lambda e, b_=b_, src=src, hf=hf: e.matmul(bank(b_), one_f[0:1, :], adar[0:1, src + hf * 512: src + hf * 512 + 512],
                                                                  start=True, stop=True),
                   reads=[("adar", (src + hf * 512) // 512), "one_f"], writes=[("ps", b_)])
                op("dve", lambda e, b_=b_, i=i, hf=hf: e.tensor_copy(out=GB[:, i, hf * 512:(hf + 1) * 512], in_=bank(b_)),
                   reads=[("ps", b_)], writes=[("GB", i, hf)])
        if debug:
            dma("sp", dbg["ada"], adacol, reads=["adacol"], key="dbg1")
        S_.barrier()
        AR.release(m0)

        m_att = AR.mark()
        uT_oth = AR.alloc(BF16, KC, NOWN * 128)

        def uT_blk(g, kc):
            t = uT_own if g % 2 == 1 else uT_oth
            j = g // 2
            return t[:, kc, j * 128:(j + 1) * 128]

        mA = AR.mark()
        xb = [AR.alloc(F32, D) for _ in range(3)]
        xn = [AR.alloc(F32, D) for _ in range(2)]
        sttA = [AR.alloc(F32, 16) for _ in range(2)]
        smlA = [AR.alloc(F32, 4) for _ in range(2)]

        def ln_stats(src, k, rk, tag):
            st, sm = sttA[k], smlA[k]
            op("dve", lambda e: e.bn_stats(out=st[:, 0:6], in_=src[:, 0:512]), reads=[rk], writes=[("st", k, 0)])
            op("dve", lambda e: e.bn_stats(out=st[:, 6:12], in_=src[:, 512:1024]), reads=[rk], writes=[("st", k, 1)])
            op("dve", lambda e: e.bn_aggr(out=st[:, 12:14], in_=st[:, 0:12]), reads=[("st", k, 0), ("st", k, 1)],
               writes=[("mv", k)])
            op("act", lambda e: e.activation(out=sm[:, 0:1], in_=st[:, 13:14], func=AF.Sqrt, bias=epsc[:, 0:1], scale=1.0),
               reads=[("mv", k), "epsc"], writes=[("sd", k)])
            op("dve", lambda e: e.reciprocal(out=sm[:, 1:2], in_=sm[:, 0:1]), reads=[("sd", k)], writes=[("rstd", k)])
            op("dve", lambda e: e.tensor_scalar(out=sm[:, 2:3], in0=st[:, 12:13], scalar1=-1.0, scalar2=sm[:, 1:2],
                                                op0=ALU.mult, op1=ALU.mult),
               reads=[("mv", k), ("rstd", k)], writes=[("nmr", k)])

        for g in range(NB):
            xt = xb[g % 3]
            k = g % 2
            dma("sp", xt, xall[g * 128:(g + 1) * 128, :], writes=[("xb", g % 3)], key=("xb", g % 3))
            ln_stats(xt, k, ("xb", g % 3), "A")
            op("act", lambda e, xt=xt, k=k: e.activation(out=xn[k], in_=xt, func=AF.Identity, bias=smlA[k][:, 2:3],
                                                         scale=smlA[k][:, 1:2]),
               reads=[("xb", g % 3), ("rstd", k), ("nmr", k)], writes=[("xn", k)])
            for kc in range(KC):
                bk = 4 + 2 * k + kc // 4
                op("pe", lambda e, bk=bk, kc=kc, k=k: e.transpose(bank(bk)[:, (kc % 4) * 128:(kc % 4 + 1) * 128],
                                                              xn[k][:, kc * 128:(kc + 1) * 128], ident_f),
                   reads=[("xn", k), "ident_f"], writes=[("ps", bk)], inc=(kc % 4 == 3))
            for kc in range(KC):
                bk = 4 + 2 * k + kc // 4
                op("dve", lambda e, bk=bk, kc=kc, g=g: e.tensor_scalar(
                    out=uT_blk(g, kc), in0=bank(bk)[:, (kc % 4) * 128:(kc % 4 + 1) * 128],
                    scalar1=s1col[:, kc:kc + 1], scalar2=adacol[:, kc:kc + 1], op0=ALU.mult, op1=ALU.add),
                   reads=[("ps", bk), "s1col", "adacol"], writes=[("uT", g)])
        if debug:
            dma("sp", dbg["sml"], smlA[1], reads=[("nmr", 1)], key="dbg2")
            dma("sp", dbg["stt"], sttA[1], reads=[("mv", 1)], key="dbg3")
            dma("sp", dbg["xn"], xn[1], reads=[("xn", 1)], key="dbg4")
            dma("sp", dbg["s1col"], s1col, reads=["s1col"], key="dbg5")
            dma("sp", dbg["uT"][:, :, 0:2048], uT_oth, reads=[("uT", g) for g in range(0, NB, 2)], key="dbg6")
            dma("sp", dbg["uT"][:, :, 2048:4096], uT_own, reads=[("uT", g) for g in range(1, NB, 2)], key="dbg7")
        S_.barrier()
        AR.release(mA)

        mF = AR.mark()
        wf_b = AR.alloc(BF16, KC, 8)
        nbf = AR.alloc(F32, 2)
        spf = AR.alloc(F32, S)
        fcn = AR.alloc(F32, S)
        fc3 = AR.alloc(BF16, 3, S)
        etmp = [AR.alloc(F32, 512) for _ in range(2)]
        dma("pool", wf_b, w_in[:, :, OFF_FG:OFF_FG + 8], writes=["wf_b"], key="c5")
        dma("sp", nbf[0:8, 0:1], b_forget, writes=["nbf0"], key="c6")
        op("dve", lambda e: e.tensor_scalar(out=nbf[0:8, 1:2], in0=nbf[0:8, 0:1], scalar1=-1.0, scalar2=None, op0=ALU.mult),
           reads=["nbf0"], writes=["nbf"])
        for pg in range(8):
            bk = pg % 2
            for blk in range(4):
                g = pg * 4 + blk
                for kc in range(KC):
                    op("pe", lambda e, bk=bk, blk=blk, g=g, kc=kc: e.matmul(
                        bank(bk)[0:8, blk * 128:(blk + 1) * 128], wf_b[:, kc, :], uT_blk(g, kc),
                        start=(kc == 0), stop=(kc == KC - 1)),
                       reads=["wf_b", ("uT", g)], writes=[("ps", bk)], inc=(blk == 3 and kc == KC - 1))
            op("act", lambda e, bk=bk: e.activation(out=etmp[bk][0:8, :], in_=bank(bk)[0:8, :], func=AF.Exp,
                                                    bias=nbf[0:8, 1:2], scale=-1.0),
               reads=[("ps", bk), "nbf"], writes=[("etmp", bk)])
            op("act", lambda e, bk=bk, pg=pg: e.activation(out=spf[0:8, pg * 512:(pg + 1) * 512], in_=etmp[bk][0:8, :],
                                                           func=AF.Ln, bias=1.0, scale=1.0),
               reads=[("etmp", bk)], writes=["spf"])
        op("dve", lambda e: e.tensor_tensor_scan(out=fcn[0:8, :], data0=spf[0:8, :], data1=spf[0:8, :], initial=0.0,
                                                 op0=ALU.add, op1=ALU.max),
           reads=["spf"], writes=["fcn"])
        if debug:
            dma("sp", dbg["fcn"], fcn[0:8, :], reads=["fcn"], key="dbg8")
        for g in range(NB):
            op("pe", lambda e, g=g: e.transpose(bank(2)[:, g * 8:(g + 1) * 8], fcn[0:8, g * 128:(g + 1) * 128], ident_f[0:8, 0:8]),
               reads=["fcn", "ident_f"], writes=[("ps", 2)], inc=(g == NB - 1))
        op("dve", lambda e: e.tensor_copy(out=fcncol, in_=bank(2)[:, 0:256]), reads=[("ps", 2)], writes=["fcncol"])
        op("dve", lambda e: e.tensor_copy(out=fc3[0:8, 0, :], in_=fcn[0:8, :]), reads=["fcn"], writes=[("fc3", 0)])
        op("dve", lambda e: e.tensor_tensor(out=spf[0:8, :], in0=fcn[0:8, :], in1=fc3[0:8, 0, :], op=ALU.subtract),
           reads=["fcn", ("fc3", 0), ("ps", 2)], writes=["spf"])
        op("dve", lambda e: e.tensor_copy(out=fc3[0:8, 1, :], in_=spf[0:8, :]), reads=["spf"], writes=[("fc3", 1)])
        op("dve", lambda e: e.tensor_tensor(out=fcn[0:8, :], in0=spf[0:8, :], in1=fc3[0:8, 1, :], op=ALU.subtract),
           reads=["spf", ("fc3", 1), "fcncol"], writes=["fcn"])
        op("dve", lambda e: e.tensor_copy(out=fc3[0:8, 2, :], in_=fcn[0:8, :]), reads=["fcn"], writes=[("fc3", 2)])
        dma("sp", fc_scr, fc3[0:8, :, :], reads=[("fc3", 0), ("fc3", 1), ("fc3", 2)], writes=["fc_scr"], key="c7")
        S_.barrier()
        AR.release(mF)

        wp = [AR.alloc(BF16, KC, 384) for _ in range(2)]
        KT = AR.alloc(BF16, S)
        KTB = AR.alloc(BF16, S)
        QT = AR.alloc(BF16, NOWN * 128)
        QTB = AR.alloc(BF16, NOWN * 128)
        VT = AR.alloc(BF16, NB, 2, 128)
        eb = [[AR.alloc(F32, 512) for _ in range(2)] for _ in range(2)]
        spb = [[AR.alloc(BF16, 512) for _ in range(2)] for _ in range(2)]
        Ab = [[AR.alloc(BF16, 512) for _ in range(2)] for _ in range(2)]
        Rb = [AR.alloc(BF16, 512) for _ in range(2)]
        rden = [AR.alloc(F32, 512) for _ in range(2)]

        def proj_pair(pr):
            fox = pr >= 4
            hp = pr % 4
            base = OFF_FOX if fox else OFF_SB
            w = wp[pr % 2]
            wk = ("wp", pr % 2)
            for i in range(3):
                c0 = base + i * 512 + hp * 128
                dma("pool", w[:, :, i * 128:(i + 1) * 128], w_in[:, :, c0:c0 + 128], writes=[(wk, i)], key=(wk, i))
            for sg in range(8):
                src = uT_oth if sg < 4 else uT_own
                cs = (sg % 4) * 512
                bk = sg % 2
                rk = [("uT", g) for g in range(NB) if sb_of(g) // 4 == sg]
                for kc in range(KC):
                    op("pe", lambda e, bk=bk, kc=kc, src=src, cs=cs: e.matmul(bank(bk), w[:, kc, 128:256], src[:, kc, cs:cs + 512],
                                                                          start=(kc == 0), stop=(kc == KC - 1)),
                       reads=[(wk, 1)] + rk, writes=[("ps", bk)], inc=(kc == KC - 1))
                if not fox:
                    op("dve", lambda e, bk=bk, sg=sg: e.tensor_copy(out=KT[:, sg * 512:(sg + 1) * 512], in_=bank(bk)),
                       reads=[("ps", bk)], writes=[("KT", sg)])
                else:
                    op("dve", lambda e, bk=bk, sg=sg: e.tensor_copy(out=KT[0:64, sg * 512:(sg + 1) * 512], in_=bank(bk)[0:64, :]),
                       reads=[("ps", bk)], writes=[("KT", sg)])
                    op("dve", lambda e, bk=bk, sg=sg: e.tensor_copy(out=KTB[64:128, sg * 512:(sg + 1) * 512], in_=bank(bk)[64:128, :]),
                       reads=[("ps", bk)], writes=[("KTB", sg)])
            for G in range(4):
                bk = G % 2
                rk = [("uT", 2 * j + 1) for j in range(4 * G, 4 * G + 4)]
                for kc in range(KC):
                    op("pe", lambda e, bk=bk, kc=kc, G=G: e.matmul(bank(bk), w[:, kc, 0:128], uT_own[:, kc, G * 512:(G + 1) * 512],
                                                                start=(kc == 0), stop=(kc == KC - 1)),
                       reads=[(wk, 0)] + rk, writes=[("ps", bk)], inc=(kc == KC - 1))
                if not fox:
                    op("dve", lambda e, bk=bk, G=G: e.tensor_scalar(out=QT[:, G * 512:(G + 1) * 512], in0=bank(bk), scalar1=0.125,
                                                                    scalar2=None, op0=ALU.mult),
                       reads=[("ps", bk)], writes=[("QT", G)])
                else:
                    op("dve", lambda e, bk=bk, G=G: e.tensor_scalar(out=QT[0:64, G * 512:(G + 1) * 512], in0=bank(bk)[0:64, :],
                                                                    scalar1=0.125, scalar2=None, op0=ALU.mult),
                       reads=[("ps", bk)], writes=[("QT", G)])
                    op("dve", lambda e, bk=bk, G=G: e.tensor_scalar(out=QTB[64:128, G * 512:(G + 1) * 512], in0=bank(bk)[64:128, :],
                                                                    scalar1=0.125, scalar2=None, op0=ALU.mult),
                       reads=[("ps", bk)], writes=[("QTB", G)])
            for sg in range(8):
                bk = sg % 2
                for blk in range(4):
                    sbk = sg * 4 + blk
                    g = 2 * sbk if sbk < 16 else 2 * (sbk - 16) + 1
                    for kc in range(KC):
                        op("pe", lambda e, bk=bk, blk=blk, g=g, kc=kc: e.matmul(
                            bank(bk)[:, blk * 128:(blk + 1) * 128], uT_blk(g, kc), w[:, kc, 256:384],
                            start=(kc == 0), stop=(kc == KC - 1)),
                           reads=[(wk, 2), ("uT", g)], writes=[("ps", bk)], inc=(blk == 3 and kc == KC - 1))
                if not fox:
                    op("dve", lambda e, bk=bk, sg=sg: e.tensor_copy(
                        out=VT[:, sg * 4:(sg + 1) * 4, 0, :], in_=bank(bk).rearrange("p (a b) -> p a b", a=4, b=128)),
                       reads=[("ps", bk)], writes=[("VT", sg)])
                else:
                    op("dve", lambda e, bk=bk, sg=sg: e.tensor_copy(
                        out=VT[:, sg * 4:(sg + 1) * 4, :, 0:64],
                        in_=bank(bk).rearrange("p (a h d) -> p a h d", a=4, h=2, d=64)),
                       reads=[("ps", bk)], writes=[("VT", sg)])
            if fox:
                for h in range(2):
                    hg = hp * 2 + h
                    dst = QT[64:67, :] if h == 0 else QTB[0:3, :]
                    srcap = fc_scr[hg].rearrange("r (j two c) -> r j two c", two=2, c=128)[:, :, 1, :]
                    dma("sp", dst.rearrange("r (j c) -> r j c", c=128), srcap, reads=["fc_scr"],
                        writes=[("QTaug", h)] + [(("QT" if h == 0 else "QTB"), G_) for G_ in range(4)], key=("qaug", h))

        def grp_cols(G, g):
            d = g - 8 * G
            return (d // 2) * 128 if d >= 0 else 0

        def attn_sb(pr):
            for G in range(4):
                gs = list(range(8 * G + 7, -1, -1))
                for h in range(2):
                    op("pool", lambda e, h=h: e.memset(Rb[h], 0.0), writes=[("R", h)])
                for it, g in enumerate(gs):
                    b = sb_of(g)
                    d = g - 8 * G
                    c0 = grp_cols(G, g)
                    par = it % 2
                    for h in range(2):
                        zb = bank(2 + 2 * h + par)
                        zk = ("ps", 2 + 2 * h + par)
                        ob = bank(6 + h)
                        ok_ = ("ps", 6 + h)
                        rs = slice(h * 64, (h + 1) * 64)
                        qsl = slice(G * 512 + c0, G * 512 + 512)
                        has_tri = d >= 1 and d % 2 == 1
                        has_dm = g == 0
                        op("pe", lambda e, zb=zb, rs=rs, b=b, qsl=qsl, c0=c0, st_=not (has_tri or has_dm): e.matmul(
                            zb[:, c0:512], KT[rs, b * 128:(b + 1) * 128], QT[rs, qsl], start=True, stop=st_),
                           reads=[("KT", b // 4), ("QT", G)], writes=[zk], inc=not (has_tri or has_dm))
                        if has_tri:
                            op("pe", lambda e, zb=zb, c0=c0, has_dm=has_dm: e.matmul(zb[:, c0:c0 + 128], ident_b, triS_b, start=False,
                                                                                   stop=not has_dm),
                               reads=["cst_b"], writes=[zk], inc=not has_dm)
                        if has_dm:
                            op("pe", lambda e, zb=zb, c0=c0: e.matmul(zb[:, c0:512], ident_b, dm_b[:, c0:512], start=False, stop=True),
                               reads=["cst_b"], writes=[zk])
                        e_, sp_, A_ = eb[h][par], spb[h][par], Ab[h][par]
                        op("act", lambda e, zb=zb, e_=e_, c0=c0: e.activation(out=e_[:, c0:512], in_=zb[:, c0:512], func=AF.Exp),
                           reads=[zk], writes=[("e", h, par)])
                        op("act", lambda e, e_=e_, sp_=sp_, c0=c0: e.activation(out=sp_[:, c0:512], in_=e_[:, c0:512], func=AF.Ln,
                                                                               bias=1.0, scale=1.0),
                           reads=[("e", h, par)], writes=[("sp", h, par)])
                        op("pe", lambda e, zb=zb, sp_=sp_, c0=c0: e.matmul(zb[:, c0:512], negU_b, sp_[:, c0:512], start=False, stop=True, skip_group_check=True),
                           reads=["cst_b", ("sp", h, par)], writes=[zk], inc=(it == 0))
                        if it > 0:
                            op("pe", lambda e, zb=zb, h=h, c0=c0: e.matmul(zb[:, c0:512], negones_b, Rb[h][:, c0:512], start=False, stop=True, skip_group_check=True),
                               reads=["negones", ("R", h)], writes=[zk])
                        op("act", lambda e, zb=zb, A_=A_, c0=c0: e.activation(out=A_[:, c0:512], in_=zb[:, c0:512], func=AF.Exp),
                           reads=[zk], writes=[("A", h, par)])
                        op("pe", lambda e, ob=ob, b=b, h=h, A_=A_, c0=c0, it=it, g=g: e.matmul(
                            ob[0:64, c0:512], VT[:, b, 0, h * 64:(h + 1) * 64], A_[:, c0:512], start=(it == 0), stop=(g == 0),
                            skip_group_check=True),
                           reads=[("VT", b // 4), ("A", h, par)], writes=[ok_], inc=True)
                        if it < len(gs) - 1:
                            op("pool", lambda e, h=h, sp_=sp_, c0=c0: e.tensor_tensor(out=Rb[h][:, c0:512], in0=Rb[h][:, c0:512],
                                                                                      in1=sp_[:, c0:512], op=ALU.add),
                               reads=[("R", h), ("sp", h, par)], writes=[("R", h)])
                for h in range(2):
                    op("act", lambda e, h=h, G=G: e.activation(out=attnT[h * 64:(h + 1) * 64, pr, G * 512:(G + 1) * 512],
                                                               in_=bank(6 + h)[0:64, :], func=AF.Copy),
                       reads=[("ps", 6 + h)], writes=[("attnT", pr, G)])

        def attn_fox(pr):
            hp = pr % 4
            for G in range(4):
                gs = list(range(8 * G + 7, -1, -1))
                for it, g in enumerate(gs):
                    b = sb_of(g)
                    d = g - 8 * G
                    c0 = grp_cols(G, g)
                    par = it % 2
                    for h in range(2):
                        hg = hp * 2 + h
                        zb = bank(2 + 2 * h + par)
                        zk = ("ps", 2 + 2 * h + par)
                        ob = bank(6 + h)
                        ok_ = ("ps", 6 + h)
                        qsl = slice(G * 512 + c0, G * 512 + 512)
                        has_tri = d >= 1 and d % 2 == 1
                        has_dm = g == 0
                        if h == 0:
                            kap, qap = KT[0:67, b * 128:(b + 1) * 128], QT[0:67, qsl]
                            rds = [("KT", b // 4), ("QT", G), ("QTaug", 0), "kaug"]
                        else:
                            kap, qap = KTB[:, b * 128:(b + 1) * 128], QTB[:, qsl]
                            rds = [("KTB", b // 4), ("QTB", G), ("QTaug", 1), "kaug"]
                        op("pe", lambda e, zb=zb, kap=kap, qap=qap, c0=c0, st_=not (has_tri or has_dm): e.matmul(
                            zb[:, c0:512], kap, qap, start=True, stop=st_),
                           reads=rds, writes=[zk], inc=not (has_tri or has_dm))
                        if has_tri:
                            op("pe", lambda e, zb=zb, c0=c0, has_dm=has_dm: e.matmul(zb[:, c0:c0 + 128], ident_b, triI_b, start=False,
                                                                                   stop=not has_dm),
                               reads=["cst_b"], writes=[zk], inc=not has_dm)
                        if has_dm:
                            op("pe", lambda e, zb=zb, c0=c0: e.matmul(zb[:, c0:512], ident_b, dm_b[:, c0:512], start=False, stop=True),
                               reads=["cst_b"], writes=[zk])
                        A_ = Ab[h][par]
                        op("act", lambda e, zb=zb, A_=A_, c0=c0, g=g, hg=hg: e.activation(
                            out=A_[:, c0:512], in_=zb[:, c0:512], func=AF.Exp, bias=fcncol[:, g * 8 + hg:g * 8 + hg + 1], scale=1.0),
                           reads=[zk, "fcncol"], writes=[("A", h, par)])
                        op("pe", lambda e, ob=ob, b=b, h=h, A_=A_, c0=c0, it=it, g=g: e.matmul(
                            ob[:, c0:512], VT[:, b, h, :], A_[:, c0:512], start=(it == 0), stop=(g == 0), skip_group_check=True),
                           reads=[("VT", b // 4), ("A", h, par), "vones"], writes=[ok_], inc=True)
                for h in range(2):
                    op("dve", lambda e, h=h: e.reciprocal(out=rden[h][64:128, :], in_=bank(6 + h)[64:128, :]),
                       reads=[("ps", 6 + h)], writes=[("rden", h)])
                    op("dve", lambda e, h=h, G=G: e.tensor_tensor(out=attnT[h * 64:(h + 1) * 64, pr, G * 512:(G + 1) * 512],
                                                                  in0=bank(6 + h)[0:64, :], in1=rden[h][64:128, :], op=ALU.mult),
                       reads=[("ps", 6 + h), ("rden", h)], writes=[("attnT", pr, G)])

        for pr in range(8):
            if pr == 4:
                op("pool", lambda e: e.memset(KT[64:67, :], -1.0), writes=[("KT", i) for i in range(8)] + ["kaug"])
                op("pool", lambda e: e.memset(KTB[0:64, :], 0.0), writes=["kaug0"])
                op("pool", lambda e: e.memset(KTB[0:3, :], -1.0), reads=["kaug0"], writes=["kaug"])
                op("pool", lambda e: e.memset(QTB[0:64, :], 0.0), writes=[("QTaug", 1)])
                op("pool", lambda e: e.memset(VT[:, :, :, 64:128], 1.0), writes=[("VT", i) for i in range(8)] + ["vones"])
            proj_pair(pr)
            if pr < 4:
                attn_sb(pr)
            else:
                attn_fox(pr)
        if debug:
            dma("sp", dbg["attnT"], attnT, reads=[("attnT", pr, G) for pr in range(8) for G in range(4)], key="dbg9")
        S_.barrier()
        AR.release(m_att)

        mC1 = AR.mark()
        wbg = AR.alloc(BF16, KC, 2 * D)
        wsb_b = AR.alloc(BF16, 4, D)
        wfx_b = AR.alloc(BF16, 4, D)
        wo_b = AR.alloc(BF16, KC, D)
        lnB1 = AR.alloc(F32, 2, D)
        gT = [[AR.alloc(BF16, 512) for _ in range(2)] for _ in range(2)]
        tmpa = [AR.alloc(F32, 512) for _ in range(2)]
        tmpb = [AR.alloc(F32, 512) for _ in range(2)]
        mT = AR.alloc(BF16, KC, 512)
        xinB = [AR.alloc(F32, D) for _ in range(2)]
        r1 = [AR.alloc(F32, D) for _ in range(2)]
        x1b = [AR.alloc(F32, D) for _ in range(2)]
        sttB = [AR.alloc(F32, 16) for _ in range(2)]
        smlB = [AR.alloc(F32, 4) for _ in range(2)]

        for hf in range(2):
            dma("pool", wbg[:, :, hf * D:(hf + 1) * D], w_in[:, :, OFF_BG + hf * D:OFF_BG + (hf + 1) * D], writes=["wbg"], key="w0")
        dma("pool", wsb_b, w_sb, writes=["wsb"], key="w1")
        dma("pool", wfx_b, w_fx, writes=["wfx"], key="w2")
        for i in range(2):
            dma("sp", lnB1[:, i, :], lnp[i:i + 1, :].partition_broadcast(128), writes=[("lnB1", i)], key=("lnB1", i))
        for kc in range(KC):
            st_ = xinB[kc % 2]
            dma("sp", st_, w_o[:, kc, :], writes=[("xinB", kc % 2)], key=("xinB", kc % 2))
            op("pool", lambda e, st_=st_, kc=kc: e.tensor_tensor(out=wo_b[:, kc, :], in0=st_, in1=GB[:, 0, :], op=ALU.mult),
               reads=[("xinB", kc % 2), ("GB", 0, 0), ("GB", 0, 1)], writes=[("wo", kc)])

        def own_rows(j):
            g = 2 * j + 1
            return slice(g * 128, (g + 1) * 128)

        for G in range(4):
            ukeys = [("uT", 2 * j + 1) for j in range(4 * G, 4 * G + 4)]
            for n in range(8):
                par = n % 2
                for w_ in range(2):
                    bk = w_ * 2 + par
                    col = w_ * D + n * 128
                    for kc in range(KC):
                        op("pe", lambda e, bk=bk, kc=kc, col=col, G=G: e.matmul(bank(bk), wbg[:, kc, col:col + 128],
                                                                             uT_own[:, kc, G * 512:(G + 1) * 512],
                                                                             start=(kc == 0), stop=(kc == KC - 1)),
                           reads=["wbg"] + ukeys, writes=[("ps", bk)], inc=(kc == KC - 1))
                    op("act", lambda e, bk=bk, w_=w_, par=par, n=n: e.activation(out=gT[w_][par], in_=bank(bk), func=AF.Sigmoid,
                                                                                bias=bgcol[:, w_ * 8 + n:w_ * 8 + n + 1], scale=1.0),
                       reads=[("ps", bk), "bgcol"], writes=[("gT", w_, par)])
                for w_, wt in enumerate((wsb_b, wfx_b)):
                    bk = 4 + w_ * 2 + par
                    for kc in range(4):
                        op("pe", lambda e, bk=bk, kc=kc, wt=wt, w_=w_, n=n, G=G: e.matmul(
                            bank(bk), wt[:, kc, n * 128:(n + 1) * 128], attnT[:, w_ * 4 + kc, G * 512:(G + 1) * 512],
                            start=(kc == 0), stop=(kc == 3)),
                           reads=["wsb", "wfx"] + [("attnT", w_ * 4 + kc, G)], writes=[("ps", bk)], inc=(kc == 3))
                op("dve", lambda e, par=par: e.tensor_tensor(out=tmpa[par], in0=bank(4 + par), in1=gT[0][par], op=ALU.mult),
                   reads=[("ps", 4 + par), ("gT", 0, par)], writes=[("tmpa", par)])
                op("dve", lambda e, par=par: e.tensor_tensor(out=tmpb[par], in0=bank(6 + par), in1=gT[1][par], op=ALU.mult),
                   reads=[("ps", 6 + par), ("gT", 1, par)], writes=[("tmpb", par)])
                op("pool", lambda e, par=par, n=n: e.tensor_tensor(out=mT[:, n, :], in0=tmpa[par], in1=tmpb[par], op=ALU.add),
                   reads=[("tmpa", par), ("tmpb", par)], writes=[("mT", n)])
            for i in range(4):
                j = 4 * G + i
                k = j % 2
                dma("sp", xinB[k], xall[own_rows(j), :], writes=[("xinB", k)], key=("xinB", k))
                for hf in range(2):
                    bk = hf
                    for kc in range(KC):
                        op("pe", lambda e, bk=bk, kc=kc, i=i, hf=hf: e.matmul(bank(bk), mT[:, kc, i * 128:(i + 1) * 128],
                                                                           wo_b[:, kc, hf * 512:(hf + 1) * 512],
                                                                           start=(kc == 0), stop=(kc == KC - 1)),
                           reads=[("mT", kc), ("wo", kc)], writes=[("ps", bk)], inc=(kc == KC - 1))
                    op("dve", lambda e, bk=bk, k=k, hf=hf: e.scalar_tensor_tensor(
                        out=r1[k][:, hf * 512:(hf + 1) * 512], in0=xinB[k][:, hf * 512:(hf + 1) * 512], scalar=ALPHA,
                        in1=bank(bk), op0=ALU.mult, op1=ALU.add),
                       reads=[("xinB", k), ("ps", bk)], writes=[("r1", k, hf)])
                st, sm = sttB[k], smlB[k]

                def ln_stats2(src, rks, k=k, st=st, sm=sm):
                    op("dve", lambda e: e.bn_stats(out=st[:, 0:6], in_=src[:, 0:512]), reads=rks, writes=[("st", k, 0)])
                    op("dve", lambda e: e.bn_stats(out=st[:, 6:12], in_=src[:, 512:1024]), reads=rks, writes=[("st", k, 1)])
                    op("dve", lambda e: e.bn_aggr(out=st[:, 12:14], in_=st[:, 0:12]), reads=[("st", k, 0), ("st", k, 1)],
                       writes=[("mv", k)])
                    op("act", lambda e: e.activation(out=sm[:, 0:1], in_=st[:, 13:14], func=AF.Sqrt, bias=epsc[:, 0:1], scale=1.0),
                       reads=[("mv", k), "epsc"], writes=[("sd", k)])
                    op("dve", lambda e: e.reciprocal(out=sm[:, 1:2], in_=sm[:, 0:1]), reads=[("sd", k)], writes=[("rstd", k)])
                    op("dve", lambda e: e.tensor_scalar(out=sm[:, 2:3], in0=st[:, 12:13], scalar1=-1.0, scalar2=sm[:, 1:2],
                                                        op0=ALU.mult, op1=ALU.mult),
                       reads=[("mv", k), ("rstd", k)], writes=[("nmr", k)])

                ln_stats2(r1[k], [("r1", k, 0), ("r1", k, 1)])
                op("act", lambda e, k=k: e.activation(out=r1[k], in_=r1[k], func=AF.Identity, bias=smlB[k][:, 2:3], scale=smlB[k][:, 1:2]),
                   reads=[("r1", k, 0), ("r1", k, 1), ("rstd", k), ("nmr", k)], writes=[("r1", k, 0), ("r1", k, 1)])
                op("pool", lambda e, k=k: e.tensor_tensor(out=x1b[k], in0=r1[k], in1=lnB1[:, 0, :], op=ALU.mult),
                   reads=[("r1", k, 0), ("r1", k, 1), ("lnB1", 0)], writes=[("x1", k)])
                op("pool", lambda e, k=k: e.tensor_tensor(out=x1b[k], in0=x1b[k], in1=lnB1[:, 1, :], op=ALU.add),
                   reads=[("x1", k), ("lnB1", 1)], writes=[("x1", k)])
                dma("sp", x1_scr[j * 128:(j + 1) * 128, :], x1b[k], reads=[("x1", k)], writes=[("x1scr", j)], key=("x1o", k))
                if debug:
                    dma("sp", dbg["x1"][j * 128:(j + 1) * 128, :], x1b[k], reads=[("x1", k)], key=("dbgx1", k))
                ln_stats2(x1b[k], [("x1", k)])
                op("act", lambda e, k=k: e.activation(out=r1[k], in_=x1b[k], func=AF.Identity, bias=smlB[k][:, 2:3], scale=smlB[k][:, 1:2]),
                   reads=[("x1", k), ("rstd", k), ("nmr", k)], writes=[("r1", k, 0), ("r1", k, 1)])
                for kc in range(KC):
                    bk = 2 + kc // 4
                    op("pe", lambda e, bk=bk, kc=kc, k=k: e.transpose(bank(bk)[:, (kc % 4) * 128:(kc % 4 + 1) * 128],
                                                                  r1[k][:, kc * 128:(kc + 1) * 128], ident_f),
                       reads=[("r1", k, 0), ("r1", k, 1), "ident_f"], writes=[("ps", bk)], inc=(kc % 4 == 3))
                for kc in range(KC):
                    bk = 2 + kc // 4
                    op("dve", lambda e, bk=bk, kc=kc, j=j: e.tensor_scalar(
                        out=uT_own[:, kc, j * 128:(j + 1) * 128], in0=bank(bk)[:, (kc % 4) * 128:(kc % 4 + 1) * 128],
                        scalar1=s2col[:, kc:kc + 1], scalar2=adacol[:, 24 + kc:25 + kc], op0=ALU.mult, op1=ALU.add),
                       reads=[("ps", bk), "s2col", "adacol"], writes=[("uT", 2 * j + 1)])
        S_.barrier()
        AR.release(m_attnT)

        wg_b = AR.alloc(BF16, KC, DFF)
        wu_b = AR.alloc(BF16, KC, DFF)
        wd_b = AR.alloc(BF16, FC, D)
        hT = AR.alloc(BF16, FC, 256)
        sgt = [AR.alloc(F32, 256) for _ in range(2)]
        xin = [AR.alloc(F32, D) for _ in range(2)]
        r2 = xin
        lnB = GB
        stt = [AR.alloc(F32, 16) for _ in range(2)]
        sml = [AR.alloc(F32, 4) for _ in range(2)]
        for kc in range(KC):
            for hf in range(2):
                dma("pool", wg_b[:, kc, hf * 1408:(hf + 1) * 1408], w_g[:, kc, hf * 1408:(hf + 1) * 1408], writes=[("wg", kc)], key="w3")
                dma("pool", wu_b[:, kc, hf * 1408:(hf + 1) * 1408], w_u[:, kc, hf * 1408:(hf + 1) * 1408], writes=[("wu", kc)], key="w4")
        S_.group_final("w3", [("wg", kc) for kc in range(KC)])
        S_.group_final("w4", [("wu", kc) for kc in range(KC)])
        for fc in range(FC):
            st_ = xin[fc % 2]
            dma("sp", st_, w_d[:, fc, :], writes=[("xin2", fc % 2)], key=("xin", fc % 2))
            op("pool", lambda e, st_=st_, fc=fc: e.tensor_tensor(out=wd_b[:, fc, :], in0=st_, in1=GB[:, 1, :], op=ALU.mult),
               reads=[("xin2", fc % 2), ("GB", 1, 0), ("GB", 1, 1)], writes=[("wd", fc)])
        for i in range(2):
            dma("sp", lnB[:, i, :], lnp[2 + i:3 + i, :].partition_broadcast(128), writes=[("GB", i, 0), ("GB", i, 1)],
                key=("lnB", i))
        wgk = [("wg", kc) for kc in range(KC)]
        wuk = [("wu", kc) for kc in range(KC)]
        for sub in range(8):
            cs = sub * 256
            ukeys = [("uT", 2 * j + 1) for j in (2 * sub, 2 * sub + 1)]
            for fc in range(FC):
                par = fc % 2
                bg, bu = bank(par), bank(2 + par)
                for kc in range(KC):
                    op("pe", lambda e, bg=bg, kc=kc, fc=fc, cs=cs: e.matmul(bg[:, 0:256], wg_b[:, kc, fc * 128:(fc + 1) * 128],
                                                                       uT_own[:, kc, cs:cs + 256], start=(kc == 0), stop=(kc == KC - 1)),
                       reads=wgk + ukeys, writes=[("ps", par)], inc=(kc == KC - 1))
                for kc in range(KC):
                    op("pe", lambda e, bu=bu, kc=kc, fc=fc, cs=cs: e.matmul(bu[:, 0:256], wu_b[:, kc, fc * 128:(fc + 1) * 128],
                                                                       uT_own[:, kc, cs:cs + 256], start=(kc == 0), stop=(kc == KC - 1)),
                       reads=wuk + ukeys, writes=[("ps", 2 + par)], inc=(kc == KC - 1))
                op("act", lambda e, bg=bg, par=par: e.activation(out=sgt[par], in_=bg[:, 0:256], func=AF.Silu),
                   reads=[("ps", par)], writes=[("sgt", par)])
                op("dve", lambda e, bu=bu, par=par, fc=fc: e.tensor_tensor(out=hT[:, fc, :], in0=bu[:, 0:256], in1=sgt[par], op=ALU.mult),
                   reads=[("ps", 2 + par), ("sgt", par)], writes=[("hT", fc)])
            for i in range(2):
                j = 2 * sub + i
                k = j % 2
                dma("sp", xin[k], x1_scr[j * 128:(j + 1) * 128, :], reads=[("x1scr", j)], writes=[("xin2", k)], key=("xin", k))
                for hf in range(2):
                    bk = 4 + 2 * k + hf
                    for fc in range(FC):
                        op("pe", lambda e, bk=bk, fc=fc, i=i, hf=hf: e.matmul(bank(bk), hT[:, fc, i * 128:(i + 1) * 128],
                                                                           wd_b[:, fc, hf * 512:(hf + 1) * 512],
                                                                           start=(fc == 0), stop=(fc == FC - 1)),
                           reads=[("hT", fc), ("wd", fc)], writes=[("ps", bk)], inc=(fc == FC - 1))
                    op("dve", lambda e, bk=bk, k=k, hf=hf: e.scalar_tensor_tensor(
                        out=r2[k][:, hf * 512:(hf + 1) * 512], in0=xin[k][:, hf * 512:(hf + 1) * 512], scalar=ALPHA,
                        in1=bank(bk), op0=ALU.mult, op1=ALU.add),
                       reads=[("xin2", k), ("ps", bk)], writes=[("xin2", k)])
                st, sm = stt[k], sml[k]
                op("dve", lambda e, st=st, k=k: e.bn_stats(out=st[:, 0:6], in_=r2[k][:, 0:512]), reads=[("xin2", k)], writes=[("st", k, 0)])
                op("dve", lambda e, st=st, k=k: e.bn_stats(out=st[:, 6:12], in_=r2[k][:, 512:1024]), reads=[("xin2", k)], writes=[("st", k, 1)])
                op("dve", lambda e, st=st: e.bn_aggr(out=st[:, 12:14], in_=st[:, 0:12]), reads=[("st", k, 0), ("st", k, 1)],
                   writes=[("mv", k)])
                op("act", lambda e, st=st, sm=sm: e.activation(out=sm[:, 0:1], in_=st[:, 13:14], func=AF.Sqrt, bias=epsc[:, 0:1], scale=1.0),
                   reads=[("mv", k), "epsc"], writes=[("sd", k)])
                op("dve", lambda e, sm=sm: e.reciprocal(out=sm[:, 1:2], in_=sm[:, 0:1]), reads=[("sd", k)], writes=[("rstd", k)])
                op("dve", lambda e, st=st, sm=sm: e.tensor_scalar(out=sm[:, 2:3], in0=st[:, 12:13], scalar1=-1.0, scalar2=sm[:, 1:2],
                                                                  op0=ALU.mult, op1=ALU.mult),
                   reads=[("mv", k), ("rstd", k)], writes=[("nmr", k)])
                op("act", lambda e, k=k, sm=sm: e.activation(out=r2[k], in_=r2[k], func=AF.Identity, bias=sm[:, 2:3], scale=sm[:, 1:2]),
                   reads=[("xin2", k), ("rstd", k), ("nmr", k)], writes=[("xin2", k)])
                op("pool", lambda e, k=k: e.tensor_tensor(out=r2[k], in0=r2[k], in1=lnB[:, 0, :], op=ALU.mult),
                   reads=[("xin2", k), ("GB", 0, 0), ("GB", 0, 1)], writes=[("xin2", k)])
                op("pool", lambda e, k=k: e.tensor_tensor(out=r2[k], in0=r2[k], in1=lnB[:, 1, :], op=ALU.add),
                   reads=[("xin2", k), ("GB", 1, 0), ("GB", 1, 1)], writes=[("xin2", k)])
                dma("sp", out[j * 128:(j + 1) * 128, :], r2[k], reads=[("xin2", k)], writes=[("out", j)], key=("outd", k))
        S_.wait_all("sp", [("outd", 0), ("outd", 1)] + ([k for k in S_.dma_cnt if str(k).startswith("dbg") or (isinstance(k, tuple) and k[0] == "dbgx1")] if debug else []))
        print("arena peak bytes", AR.peak, "pe", S_.cnt["pe"], "act", S_.cnt["act"], "dve", S_.cnt["dve"], "pool", S_.cnt["pool"],
              "instr", {k: len(v) for k, v in S_.prog.items()})

        with nc.Block() as block:
            @block.tensor
            def _(e):
                S_.emit("pe", e)

            @block.scalar
            def _(e):
                S_.emit("act", e)

            @block.vector
            def _(e):
                S_.emit("dve", e)

            @block.gpsimd
            def _(e):
                S_.emit("pool", e)

            @block.sync
            def _(e):
                S_.emit("sp", e)
    return nc


_CONSTS = None


def _consts():
    global _CONSTS
    if _CONSTS is None:
        s = np.arange(128)[:, None]
        t = np.arange(128)[None, :]
        ident = (s == t).astype(np.float32)
        triS = np.where(s < t, 0.0, NEG).astype(np.float32)
        triI = np.where(s <= t, 0.0, NEG).astype(np.float32)
        negU = np.where(s >= t, -1.0, 0.0).astype(np.float32)
        _CONSTS = np.concatenate([ident, triS, triI, negU], axis=1)
    return _CONSTS


def _rk(w, kc):
    n = w.shape[1]
    return np.ascontiguousarray(w.reshape(kc, 128, n).transpose(1, 0, 2))


def make_in_maps(x, c, w_ada, b_ada, w_in, b_gate, b_forget, w_sb_out, w_fox_out, w_o,
                 ln1_g, ln1_b, w_ffn_gate, w_ffn_up, w_ffn_down, ln2_g, ln2_b):
    f = np.float32
    x = np.asarray(x, f)
    shared = {
        "w_ada": _rk(np.asarray(w_ada, f)[0], KC),
        "b_ada": np.ascontiguousarray(np.asarray(b_ada, f)[0][None, :]),
        "w_in": _rk(np.asarray(w_in, f)[0], KC),
        "b_gate": np.ascontiguousarray(np.asarray(b_gate, f)[0].reshape(16, 128).T),
        "b_forget": np.ascontiguousarray(np.asarray(b_forget, f)[0].reshape(8, 1)),
        "w_sb": _rk(np.asarray(w_sb_out, f)[0], 4),
        "w_fx": _rk(np.asarray(w_fox_out, f)[0], 4),
        "w_o": _rk(np.asarray(w_o, f)[0], KC),
        "lnp": np.ascontiguousarray(np.stack([np.asarray(a, f)[0] for a in (ln1_g, ln1_b, ln2_g, ln2_b)])),
        "w_g": _rk(np.asarray(w_ffn_gate, f)[0], KC),
        "w_u": _rk(np.asarray(w_ffn_up, f)[0], KC),
        "w_d": _rk(np.asarray(w_ffn_down, f)[0], FC),
    }
    cb = _consts()
    maps = []
    for core in range(8):
        b, p = core // 2, core % 2
        if p == 1:
            xa = x[b]
        else:
            xa = np.concatenate([np.zeros((128, D), f), x[b][:S - 128]], axis=0)
        dmask = np.full((128, 512), NEG if p == 0 else 0.0, f)
        m = dict(shared)
        m["xall"] = np.ascontiguousarray(xa)
        m["c_col"] = np.ascontiguousarray(np.asarray(c, f)[b].reshape(KC, 128).T)
        m["consts"] = np.ascontiguousarray(np.concatenate([cb, dmask], axis=1))
        maps.append(m)
    return maps


def assemble(results):
    out = np.zeros((4, S, D), np.float32)
    for core in range(8):
        b, p = core // 2, core % 2
        o = np.asarray(results[core]["out"], np.float32).reshape(NOWN, 128, D)
        ov = out[b].reshape(NB, 128, D)
        for j in range(NOWN):
            ov[2 * j + p] = o[j]
    return out


def kernel(**inputs):
    nc = build_nc(debug=False)
    maps = make_in_maps(**inputs)
    res = run_bass_kernel_spmd(nc, maps, core_ids=list(range(8)))
    return assemble(res.results)
```

```python
import numpy as np
from contextlib import ExitStack
import concourse.bass as bass
import concourse.mybir as mybir
from concourse.bass_utils import run_bass_kernel_spmd

F32 = mybir.dt.float32
BF16 = mybir.dt.bfloat16
AF = mybir.ActivationFunctionType
ALU = mybir.AluOpType

D = 1024
KC = 8
S = 4096
NB = 32
NOWN = 16
DFF = 2816
FC = 22
IN_COLS = 5128
OFF_SB = 0
OFF_FOX = 1536
OFF_FG = 3072
OFF_BG = 3080
LN_EPS = 1e-5
ALPHA = 2.0 ** 0.25
NEG = -30000.0

COMPUTE = ("pe", "act", "dve", "pool")


class Sched:
    def __init__(self, nc, stack, n_dma_sems=64):
        self.nc = nc
        self.prog = {e: [] for e in ("pe", "act", "dve", "pool", "sp")}
        self.cnt = {e: 0 for e in COMPUTE}
        self.sem = {e: stack.enter_context(nc.semaphore("s_" + e)) for e in COMPUTE}
        self.dma_pool = [stack.enter_context(nc.semaphore("d%d" % i)) for i in range(n_dma_sems)]
        self.dma_key = {}
        self.dma_cnt = {}
        self.known = {e: {} for e in self.prog}
        self.lastw = {}
        self.reads = {}

    def _semh(self, k):
        return self.sem[k] if k in self.sem else self.dma_pool[self.dma_key[k]]

    def _deps(self, eng, reads, writes):
        deps = set()
        for b in reads:
            ev = self.lastw.get(b)
            if ev:
                deps.add(ev)
        for b in writes:
            ev = self.lastw.get(b)
            if ev and not (ev[0] == eng and eng in COMPUTE):
                deps.add(ev)
            for ev in self.reads.get(b, ()):
                if not (ev[0] == eng and eng in COMPUTE):
                    deps.add(ev)
        waits = {}
        kn = self.known[eng]
        for (k, v) in deps:
            if k == eng and eng == "pe":
                continue
            if kn.get(k, 0) >= v:
                continue
            if waits.get(k, 0) < v:
                waits[k] = v
        for k, v in waits.items():
            kn[k] = v
        return [(self._semh(k), v) for k, v in waits.items()]

    def _record(self, ev, reads, writes):
        for b in reads:
            self.reads.setdefault(b, []).append(ev)
        for b in writes:
            self.lastw[b] = ev
            self.reads[b] = []

    def op(self, eng, fn, reads=(), writes=(), inc=True):
        waits = self._deps(eng, reads, writes)
        if inc:
            self.cnt[eng] += 1
            ev = (eng, self.cnt[eng])
        else:
            ev = (eng, self.cnt[eng] + 1)
        self._record(ev, reads, writes)
        self.prog[eng].append((waits, fn, (self.sem[eng], 1) if inc else None))

    def dma(self, queue, out, in_, reads=(), writes=(), key=None, **kw):
        assert key is not None
        if key not in self.dma_key:
            idx = len(self.dma_key)
            assert idx < len(self.dma_pool), "out of dma semaphores"
            self.dma_key[key] = idx
            self.dma_cnt[key] = 0
        waits = self._deps(queue, reads, writes)
        self.dma_cnt[key] += 16
        ev = (key, self.dma_cnt[key])
        self._record(ev, reads, writes)
        fn = lambda e, out=out, in_=in_, kw=kw: e.dma_start(out=out, in_=in_, **kw)
        self.prog[queue].append((waits, fn, (self._semh(key), 16)))

    def group_final(self, key, bufs):
        for b in bufs:
            self.lastw[b] = (key, self.dma_cnt[key])

    def wait_all(self, eng, keys):
        waits = []
        for k in keys:
            v = self.dma_cnt[k] if k in self.dma_cnt else self.cnt[k]
            if v > 0:
                waits.append((self._semh(k), v))
        self.prog[eng].append((waits, None, None))

    def barrier(self):
        keys = list(COMPUTE) + list(self.dma_cnt.keys())
        for eng in self.prog:
            waits = []
            for k in keys:
                v = self.dma_cnt[k] if k in self.dma_cnt else self.cnt[k]
                if k == eng or v == 0:
                    continue
                if self.known[eng].get(k, 0) >= v:
                    continue
                self.known[eng][k] = v
                waits.append((self._semh(k), v))
            if waits:
                self.prog[eng].append((waits, None, None))

    def emit(self, eng, e):
        for waits, fn, inc in self.prog[eng]:
            for (h, v) in waits:
                e.wait_ge(h, v)
            if fn is None:
                continue
            ins = fn(e)
            if inc is not None:
                ins.then_inc(inc[0], inc[1])


class Arena:
    def __init__(self, nc, stack, nbytes):
        self.h32 = stack.enter_context(nc.sbuf_tensor("arena", [128, nbytes // 4], F32))
        self.h16 = self.h32.bitcast(BF16)
        self.top = 0
        self.limit = nbytes
        self.peak = 0

    def alloc(self, dtype, *free):
        n = 1
        for f in free:
            n *= f
        sz = 4 if dtype == F32 else 2
        nb = (n * sz + 63) // 64 * 64
        off = self.top
        self.top += nb
        self.peak = max(self.peak, self.top)
        assert self.top <= self.limit, "arena overflow %d > %d" % (self.top, self.limit)
        base = self.h32 if dtype == F32 else self.h16
        e0 = off // sz
        ap = base[:, e0:e0 + n]
        if len(free) == 2:
            ap = ap.rearrange("p (a b) -> p a b", a=free[0], b=free[1])
        elif len(free) == 3:
            ap = ap.rearrange("p (a b c) -> p a b c", a=free[0], b=free[1], c=free[2])
        return ap

    def mark(self):
        return self.top

    def release(self, m):
        self.top = m


def sb_of(g):
    return g // 2 if g % 2 == 0 else 16 + g // 2


def build_nc(debug=False):
    nc = bass.Bass("TRN2", target_bir_lowering=False)
    dt = nc.dram_tensor
    xall = dt("xall", [S, D], F32, kind="ExternalInput").ap()
    c_col = dt("c_col", [128, KC], F32, kind="ExternalInput").ap()
    w_ada = dt("w_ada", [128, KC, 6 * D], F32, kind="ExternalInput").ap()
    b_ada = dt("b_ada", [1, 6 * D], F32, kind="ExternalInput").ap()
    w_in = dt("w_in", [128, KC, IN_COLS], F32, kind="ExternalInput").ap()
    b_gate = dt("b_gate", [128, 16], F32, kind="ExternalInput").ap()
    b_forget = dt("b_forget", [8, 1], F32, kind="ExternalInput").ap()
    w_sb = dt("w_sb", [128, 4, D], F32, kind="ExternalInput").ap()
    w_fx = dt("w_fx", [128, 4, D], F32, kind="ExternalInput").ap()
    w_o = dt("w_o", [128, KC, D], F32, kind="ExternalInput").ap()
    lnp = dt("lnp", [4, D], F32, kind="ExternalInput").ap()
    w_g = dt("w_g", [128, KC, DFF], F32, kind="ExternalInput").ap()
    w_u = dt("w_u", [128, KC, DFF], F32, kind="ExternalInput").ap()
    w_d = dt("w_d", [128, FC, D], F32, kind="ExternalInput").ap()
    consts = dt("consts", [128, 4 * 128 + 512], F32, kind="ExternalInput").ap()
    out = dt("out", [NOWN * 128, D], F32, kind="ExternalOutput").ap()
    fc_scr = dt("fc_scr", [8, 3, S], BF16, kind="Internal").ap()
    x1_scr = dt("x1_scr", [NOWN * 128, D], F32, kind="Internal").ap()
    dbg = {}
    if debug:
        dbg["uT"] = dt("dbg_uT", [128, KC, S], BF16, kind="ExternalOutput").ap()
        dbg["attnT"] = dt("dbg_attnT", [128, KC, NOWN * 128], BF16, kind="ExternalOutput").ap()
        dbg["ada"] = dt("dbg_ada", [128, 48], F32, kind="ExternalOutput").ap()
        dbg["fcn"] = dt("dbg_fcn", [8, S], F32, kind="ExternalOutput").ap()
        dbg["x1"] = dt("dbg_x1", [NOWN * 128, D], F32, kind="ExternalOutput").ap()
        dbg["sml"] = dt("dbg_sml", [128, 4], F32, kind="ExternalOutput").ap()
        dbg["stt"] = dt("dbg_stt", [128, 16], F32, kind="ExternalOutput").ap()
        dbg["xn"] = dt("dbg_xn", [128, D], F32, kind="ExternalOutput").ap()
        dbg["s1col"] = dt("dbg_s1col", [128, 8], F32, kind="ExternalOutput").ap()

    with ExitStack() as stack:
        S_ = Sched(nc, stack)
        AR = Arena(nc, stack, 206 * 1024)
        ps = stack.enter_context(nc.psum_tensor("ps", [128, 8 * 512], F32))

        def bank(i):
            return ps[:, i * 512:(i + 1) * 512]

        op, dma = S_.op, S_.dma

        ident_f = AR.alloc(F32, 128)
        cst_b = AR.alloc(BF16, 4 * 128 + 512)
        ident_b = cst_b[:, 0:128]
        triS_b = cst_b[:, 128:256]
        triI_b = cst_b[:, 256:384]
        negU_b = cst_b[:, 384:512]
        dm_b = cst_b[:, 512:1024]
        negones_b = AR.alloc(BF16, 128)
        adacol = AR.alloc(F32, 48)
        s1col = AR.alloc(F32, 8)
        s2col = AR.alloc(F32, 8)
        bgcol = AR.alloc(F32, 16)
        fcncol = AR.alloc(F32, 256)
        one_f = AR.alloc(F32, 128)
        GB = AR.alloc(F32, 2, D)
        epsc = AR.alloc(F32, 1)
        uT_own = AR.alloc(BF16, KC, NOWN * 128)
        m_attnT = AR.mark()
        attnT = AR.alloc(BF16, KC, NOWN * 128)
        op("dve", lambda e: e.memset(epsc, LN_EPS), writes=["epsc"])

        dma("sp", ident_f, consts[:, 0:128], writes=["ident_f"], key="c0")
        dma("pool", cst_b, consts, writes=["cst_b"], key="c1")
        dma("sp", bgcol, b_gate, writes=["bgcol"], key="c2")
        op("dve", lambda e: e.memset(negones_b, -1.0), writes=["negones"])
        op("dve", lambda e: e.memset(one_f, 1.0), writes=["one_f"])

        m0 = AR.mark()
        ccol = AR.alloc(F32, KC)
        cact = AR.alloc(F32, KC)
        badar = AR.alloc(F32, 6 * D)
        adar = AR.alloc(F32, 6 * D)
        wst = [AR.alloc(F32, KC, 512) for _ in range(2)]
        dma("sp", ccol, c_col, writes=["ccol"], key="c3")
        dma("sp", badar[0:1, :], b_ada, writes=["badar"], key="c4")
        op("act", lambda e: e.activation(out=cact, in_=ccol, func=AF.Silu), reads=["ccol"], writes=["cact"])
        for ng in range(12):
            sl = ng % 2
            dma("sp", wst[sl], w_ada[:, :, ng * 512:(ng + 1) * 512], writes=[("wst", sl)], key=("wst", sl))
            pb = bank(sl)
            for kc in range(KC):
                op("pe", lambda e, pb=pb, kc=kc, sl=sl: e.matmul(pb[0:1, :], cact[:, kc:kc + 1], wst[sl][:, kc, :],
                                                              start=(kc == 0), stop=(kc == KC - 1)),
                   reads=["cact", ("wst", sl)], writes=[("ps", sl)], inc=(kc == KC - 1))
            op("dve", lambda e, pb=pb, ng=ng: e.tensor_tensor(out=adar[0:1, ng * 512:(ng + 1) * 512], in0=pb[0:1, :],
                                                              in1=badar[0:1, ng * 512:(ng + 1) * 512], op=ALU.add),
               reads=[("ps", sl), "badar"], writes=[("adar", ng)])
        for c in range(48):
            op("pe", lambda e, c=c: e.matmul(bank(2)[:, c:c + 1], adar[0:1, c * 128:(c + 1) * 128], one_f[0:1, 0:1],
                                             start=True, stop=True),
               reads=[("adar", c // 4), "one_f"], writes=[("ps", 2)], inc=(c == 47))
        op("dve", lambda e: e.tensor_copy(out=adacol, in_=bank(2)[:, 0:48]), reads=[("ps", 2)], writes=["adacol"])
        op("dve", lambda e: e.tensor_scalar(out=s1col, in0=adacol[:, 8:16], scalar1=1.0, scalar2=None, op0=ALU.add),
           reads=["adacol"], writes=["s1col"])
        op("dve", lambda e: e.tensor_scalar(out=s2col, in0=adacol[:, 32:40], scalar1=1.0, scalar2=None, op0=ALU.add),
           reads=["adacol"], writes=["s2col"])
        for i, src in enumerate((2 * D, 5 * D)):
            for hf in range(2):
                b_ = 3 + (i * 2 + hf) % 2
                op("pe", lambda e, b_=b_, src=src, hf=hf: e.matmul(bank(b_), one_f[0:1, :], adar[0:1, src + hf * 512: src + hf * 512 + 512],
                                                                  start=True, stop=True),
                   reads=[("adar", (src + hf * 512) // 512), "one_f"], writes=[("ps", b_)])
                op("dve", lambda e, b_=b_, i=i, hf=hf: e.tensor_copy(out=GB[:, i, hf * 512:(hf + 1) * 512], in_=bank(b_)),
                   reads=[("ps", b_)], writes=[("GB", i, hf)])
        if debug:
            dma("sp", dbg["ada"], adacol, reads=["adacol"], key="dbg1")
        S_.barrier()
        AR.release(m0)

        m_att = AR.mark()
        uT_oth = AR.alloc(BF16, KC, NOWN * 128)

        def uT_blk(g, kc):
            t = uT_own if g % 2 == 1 else uT_oth
            j = g // 2
            return t[:, kc, j * 128:(j + 1) * 128]

        mA = AR.mark()
        xb = [AR.alloc(F32, D) for _ in range(3)]
        xn = [AR.alloc(F32, D) for _ in range(2)]
        sttA = [AR.alloc(F32, 16) for _ in range(2)]
        smlA = [AR.alloc(F32, 4) for _ in range(2)]

        def ln_stats(src, k, rk, tag):
            st, sm = sttA[k], smlA[k]
            op("dve", lambda e: e.bn_stats(out=st[:, 0:6], in_=src[:, 0:512]), reads=[rk], writes=[("st", k, 0)])
            op("dve", lambda e: e.bn_stats(out=st[:, 6:12], in_=src[:, 512:1024]), reads=[rk], writes=[("st", k, 1)])
            op("dve", lambda e: e.bn_aggr(out=st[:, 12:14], in_=st[:, 0:12]), reads=[("st", k, 0), ("st", k, 1)],
               writes=[("mv", k)])
            op("act", lambda e: e.activation(out=sm[:, 0:1], in_=st[:, 13:14], func=AF.Sqrt, bias=epsc[:, 0:1], scale=1.0),
               reads=[("mv", k), "epsc"], writes=[("sd", k)])
            op("dve", lambda e: e.reciprocal(out=sm[:, 1:2], in_=sm[:, 0:1]), reads=[("sd", k)], writes=[("rstd", k)])
            op("dve", lambda e: e.tensor_scalar(out=sm[:, 2:3], in0=st[:, 12:13], scalar1=-1.0, scalar2=sm[:, 1:2],
                                                op0=ALU.mult, op1=ALU.mult),
               reads=[("mv", k), ("rstd", k)], writes=[("nmr", k)])

        for g in range(NB):
            xt = xb[g % 3]
            k = g % 2
            dma("sp", xt, xall[g * 128:(g + 1) * 128, :], writes=[("xb", g % 3)], key=("xb", g % 3))
            ln_stats(xt, k, ("xb", g % 3), "A")
            op("act", lambda e, xt=xt, k=k: e.activation(out=xn[k], in_=xt, func=AF.Identity, bias=smlA[k][:, 2:3],
                                                         scale=smlA[k][:, 1:2]),
               reads=[("xb", g % 3), ("rstd", k), ("nmr", k)], writes=[("xn", k)])
            for kc in range(KC):
                bk = 4 + 2 * k + kc // 4
                op("pe", lambda e, bk=bk, kc=kc, k=k: e.transpose(bank(bk)[:, (kc % 4) * 128:(kc % 4 + 1) * 128],
                                                              xn[k][:, kc * 128:(kc + 1) * 128], ident_f),
                   reads=[("xn", k), "ident_f"], writes=[("ps", bk)], inc=(kc % 4 == 3))
            for kc in range(KC):
                bk = 4 + 2 * k + kc // 4
                op("dve", lambda e, bk=bk, kc=kc, g=g: e.tensor_scalar(
                    out=uT_blk(g, kc), in0=bank(bk)[:, (kc % 4) * 128:(kc % 4 + 1) * 128],
                    scalar1=s1col[:, kc:kc + 1], scalar2=adacol[:, kc:kc + 1], op0=ALU.mult, op1=ALU.add),
                   reads=[("ps", bk), "s1col", "adacol"], writes=[("uT", g)])
        if debug:
            dma("sp", dbg["sml"], smlA[1], reads=[("nmr", 1)], key="dbg2")
            dma("sp", dbg["stt"], sttA[1], reads=[("mv", 1)], key="dbg3")
            dma("sp", dbg["xn"], xn[1], reads=[("xn", 1)], key="dbg4")
            dma("sp", dbg["s1col"], s1col, reads=["s1col"], key="dbg5")
            dma("sp", dbg["uT"][:, :, 0:2048], uT_oth, reads=[("uT", g) for g in range(0, NB, 2)], key="dbg6")
            dma("sp", dbg["uT"][:, :, 2048:4096], uT_own, reads=[("uT", g) for g in range(1, NB, 2)], key="dbg7")
        S_.barrier()
        AR.release(mA)

        mF = AR.mark()
        wf_b = AR.alloc(BF16, KC, 8)
        nbf = AR.alloc(F32, 2)
        spf = AR.alloc(F32, S)
        fcn = AR.alloc(F32, S)
        fc3 = AR.alloc(BF16, 3, S)
        etmp = [AR.alloc(F32, 512) for _ in range(2)]
        dma("pool", wf_b, w_in[:, :, OFF_FG:OFF_FG + 8], writes=["wf_b"], key="c5")
        dma("sp", nbf[0:8, 0:1], b_forget, writes=["nbf0"], key="c6")
        op("dve", lambda e: e.tensor_scalar(out=nbf[0:8, 1:2], in0=nbf[0:8, 0:1], scalar1=-1.0, scalar2=None, op0=ALU.mult),
           reads=["nbf0"], writes=["nbf"])
        for pg in range(8):
            bk = pg % 2
            for blk in range(4):
                g = pg * 4 + blk
                for kc in range(KC):
                    op("pe", lambda e, bk=bk, blk=blk, g=g, kc=kc: e.matmul(
                        bank(bk)[0:8, blk * 128:(blk + 1) * 128], wf_b[:, kc, :], uT_blk(g, kc),
                        start=(kc == 0), stop=(kc == KC - 1)),
                       reads=["wf_b", ("uT", g)], writes=[("ps", bk)], inc=(blk == 3 and kc == KC - 1))
            op("act", lambda e, bk=bk: e.activation(out=etmp[bk][0:8, :], in_=bank(bk)[0:8, :], func=AF.Exp,
                                                    bias=nbf[0:8, 1:2], scale=-1.0),
               reads=[("ps", bk), "nbf"], writes=[("etmp", bk)])
            op("act", lambda e, bk=bk, pg=pg: e.activation(out=spf[0:8, pg * 512:(pg + 1) * 512], in_=etmp[bk][0:8, :],
                                                           func=AF.Ln, bias=1.0, scale=1.0),
               reads=[("etmp", bk)], writes=["spf"])
        op("dve", lambda e: e.tensor_tensor_scan(out=fcn[0:8, :], data0=spf[0:8, :], data1=spf[0:8, :], initial=0.0,
                                                 op0=ALU.add, op1=ALU.max),
           reads=["spf"], writes=["fcn"])
        if debug:
            dma("sp", dbg["fcn"], fcn[0:8, :], reads=["fcn"], key="dbg8")
        for g in range(NB):
            op("pe", lambda e, g=g: e.transpose(bank(2)[:, g * 8:(g + 1) * 8], fcn[0:8, g * 128:(g + 1) * 128], ident_f[0:8, 0:8]),
               reads=["fcn", "ident_f"], writes=[("ps", 2)], inc=(g == NB - 1))
        op("dve", lambda e: e.tensor_copy(out=fcncol, in_=bank(2)[:, 0:256]), reads=[("ps", 2)], writes=["fcncol"])
        op("dve", lambda e: e.tensor_copy(out=fc3[0:8, 0, :], in_=fcn[0:8, :]), reads=["fcn"], writes=[("fc3", 0)])
        op("dve", lambda e: e.tensor_tensor(out=spf[0:8, :], in0=fcn[0:8, :], in1=fc3[0:8, 0, :], op=ALU.subtract),
           reads=["fcn", ("fc3", 0), ("ps", 2)], writes=["spf"])
        op("dve", lambda e: e.tensor_copy(out=fc3[0:8, 1, :], in_=spf[0:8, :]), reads=["spf"], writes=[("fc3", 1)])
        op("dve", lambda e: e.tensor_tensor(out=fcn[0:8, :], in0=spf[0:8, :], in1=fc3[0:8, 1, :], op=ALU.subtract),
           reads=["spf", ("fc3", 1), "fcncol"], writes=["fcn"])
        op("dve", lambda e: e.tensor_copy(out=fc3[0:8, 2, :], in_=fcn[0:8, :]), reads=["fcn"], writes=[("fc3", 2)])
        dma("sp", fc_scr, fc3[0:8, :, :], reads=[("fc3", 0), ("fc3", 1), ("fc3", 2)], writes=["fc_scr"], key="c7")
        S_.barrier()
        AR.release(mF)

        wp = [AR.alloc(BF16, KC, 384) for _ in range(2)]
        KT = AR.alloc(BF16, S)
        KTB = AR.alloc(BF16, S)
        QT = AR.alloc(BF16, NOWN * 128)
        QTB = AR.alloc(BF16, NOWN * 128)
        VT = AR.alloc(BF16, NB, 2, 128)
        eb = [[AR.alloc(F32, 512) for _ in range(2)] for _ in range(2)]
        spb = [[AR.alloc(BF16, 512) for _ in range(2)] for _ in range(2)]
        Ab = [[AR.alloc(BF16, 512) for _ in range(2)] for _ in range(2)]
        Rb = [AR.alloc(BF16, 512) for _ in range(2)]
        rden = [AR.alloc(F32, 512) for _ in range(2)]

        def proj_pair(pr):
            fox = pr >= 4
            hp = pr % 4
            base = OFF_FOX if fox else OFF_SB
            w = wp[pr % 2]
            wk = ("wp", pr % 2)
            for i in range(3):
                c0 = base + i * 512 + hp * 128
                dma("pool", w[:, :, i * 128:(i + 1) * 128], w_in[:, :, c0:c0 + 128], writes=[(wk, i)], key=(wk, i))
            for sg in range(8):
                src = uT_oth if sg < 4 else uT_own
                cs = (sg % 4) * 512
                bk = sg % 2
                rk = [("uT", g) for g in range(NB) if sb_of(g) // 4 == sg]
                for kc in range(KC):
                    op("pe", lambda e, bk=bk, kc=kc, src=src, cs=cs: e.matmul(bank(bk), w[:, kc, 128:256], src[:, kc, cs:cs + 512],
                                                                          start=(kc == 0), stop=(kc == KC - 1)),
                       reads=[(wk, 1)] + rk, writes=[("ps", bk)], inc=(kc == KC - 1))
                if not fox:
                    op("dve", lambda e, bk=bk, sg=sg: e.tensor_copy(out=KT[:, sg * 512:(sg + 1) * 512], in_=bank(bk)),
                       reads=[("ps", bk)], writes=[("KT", sg)])
                else:
                    op("dve", lambda e, bk=bk, sg=sg: e.tensor_copy(out=KT[0:64, sg * 512:(sg + 1) * 512], in_=bank(bk)[0:64, :]),
                       reads=[("ps", bk)], writes=[("KT", sg)])
                    op("dve", lambda e, bk=bk, sg=sg: e.tensor_copy(out=KTB[64:128, sg * 512:(sg + 1) * 512], in_=bank(bk)[64:128, :]),
                       reads=[("ps", bk)], writes=[("KTB", sg)])
            for G in range(4):
                bk = G % 2
                rk = [("uT", 2 * j + 1) for j in range(4 * G, 4 * G + 4)]
                for kc in range(KC):
                    op("pe", lambda e, bk=bk, kc=kc, G=G: e.matmul(bank(bk), w[:, kc, 0:128], uT_own[:, kc, G * 512:(G + 1) * 512],
                                                                start=(kc == 0), stop=(kc == KC - 1)),
                       reads=[(wk, 0)] + rk, writes=[("ps", bk)], inc=(kc == KC - 1))
                if not fox:
                    op("dve", lambda e, bk=bk, G=G: e.tensor_scalar(out=QT[:, G * 512:(G + 1) * 512], in0=bank(bk), scalar1=0.125,
                                                                    scalar2=None, op0=ALU.mult),
                       reads=[("ps", bk)], writes=[("QT", G)])
                else:
                    op("dve", lambda e, bk=bk, G=G: e.tensor_scalar(out=QT[0:64, G * 512:(G + 1) * 512], in0=bank(bk)[0:64, :],
                                                                    scalar1=0.125, scalar2=None, op0=ALU.mult),
                       reads=[("ps", bk)], writes=[("QT", G)])
                    op("dve", lambda e, bk=bk, G=G: e.tensor_scalar(out=QTB[64:128, G * 512:(G + 1) * 512], in0=bank(bk)[64:128, :],
                                                                    scalar1=0.125, scalar2=None, op0=ALU.mult),
                       reads=[("ps", bk)], writes=[("QTB", G)])
            for sg in range(8):
                bk = sg % 2
                for blk in range(4):
                    sbk = sg * 4 + blk
                    g = 2 * sbk if sbk < 16 else 2 * (sbk - 16) + 1
                    for kc in range(KC):
                        op("pe", lambda e, bk=bk, blk=blk, g=g, kc=kc: e.matmul(
                            bank(bk)[:, blk * 128:(blk + 1) * 128], uT_blk(g, kc), w[:, kc, 256:384],
                            start=(kc == 0), stop=(kc == KC - 1)),
                           reads=[(wk, 2), ("uT", g)], writes=[("ps", bk)], inc=(blk == 3 and kc == KC - 1))
                if not fox:
                    op("dve", lambda e, bk=bk, sg=sg: e.tensor_copy(
                        out=VT[:, sg * 4:(sg + 1) * 4, 0, :], in_=bank(bk).rearrange("p (a b) -> p a b", a=4, b=128)),
                       reads=[("ps", bk)], writes=[("VT", sg)])
                else:
                    op("dve", lambda e, bk=bk, sg=sg: e.tensor_copy(
                        out=VT[:, sg * 4:(sg + 1) * 4, :, 0:64],
                        in_=bank(bk).rearrange("p (a h d) -> p a h d", a=4, h=2, d=64)),
                       reads=[("ps", bk)], writes=[("VT", sg)])
            if fox:
                for h in range(2):
                    hg = hp * 2 + h
                    dst = QT[64:67, :] if h == 0 else QTB[0:3, :]
                    srcap = fc_scr[hg].rearrange("r (j two c) -> r j two c", two=2, c=128)[:, :, 1, :]
                    dma("sp", dst.rearrange("r (j c) -> r j c", c=128), srcap, reads=["fc_scr"],
                        writes=[("QTaug", h)] + [(("QT" if h == 0 else "QTB"), G_) for G_ in range(4)], key=("qaug", h))

        def grp_cols(G, g):
            d = g - 8 * G
            return (d // 2) * 128 if d >= 0 else 0

        hcount = [0, 0]

        def attn_sb(pr):
            steps = []
            for G in range(4):
                gs = list(range(8 * G + 7, -1, -1))
                for it, g in enumerate(gs):
                    for h in range(2):
                        par = hcount[h] % 2
                        hcount[h] += 1
                        steps.append(dict(G=G, it=it, g=g, h=h, par=par, last=(g == 0), c0=grp_cols(G, g), b=sb_of(g)))
            N = len(steps)

            def bufs(st):
                h, par = st["h"], st["par"]
                return bank(2 + 2 * h + par), ("ps", 2 + 2 * h + par), bank(6 + h), ("ps", 6 + h)

            def QK(st):
                G, g, h, par, c0, b = st["G"], st["g"], st["h"], st["par"], st["c0"], st["b"]
                zb, zk, ob, ok_ = bufs(st)
                d = g - 8 * G
                rs = slice(h * 64, (h + 1) * 64)
                qsl = slice(G * 512 + c0, G * 512 + 512)
                has_tri = d >= 1 and d % 2 == 1
                has_dm = g == 0
                op("pe", lambda e, st_=not (has_tri or has_dm): e.matmul(
                    zb[:, c0:512], KT[rs, b * 128:(b + 1) * 128], QT[rs, qsl], start=True, stop=st_),
                   reads=[("KT", b // 4), ("QT", G)], writes=[zk], inc=not (has_tri or has_dm))
                if has_tri:
                    op("pe", lambda e: e.matmul(zb[:, c0:c0 + 128], ident_b, triS_b, start=False, stop=not has_dm),
                       reads=["cst_b"], writes=[zk], inc=not has_dm)
                if has_dm:
                    op("pe", lambda e: e.matmul(zb[:, c0:512], ident_b, dm_b[:, c0:512], start=False, stop=True),
                       reads=["cst_b"], writes=[zk])

            def EL(st):
                h, par, c0 = st["h"], st["par"], st["c0"]
                zb, zk, ob, ok_ = bufs(st)
                e_, sp_ = eb[h][par], spb[h][par]
                op("act", lambda e: e.activation(out=e_[:, c0:512], in_=zb[:, c0:512], func=AF.Exp),
                   reads=[zk], writes=[("e", h, par)])
                op("act", lambda e: e.activation(out=sp_[:, c0:512], in_=e_[:, c0:512], func=AF.Ln, bias=1.0, scale=1.0),
                   reads=[("e", h, par)], writes=[("sp", h, par)])

            def CUM(st):
                h, par, c0, it = st["h"], st["par"], st["c0"], st["it"]
                zb, zk, ob, ok_ = bufs(st)
                sp_ = spb[h][par]
                op("pe", lambda e: e.matmul(zb[:, c0:512], negU_b, sp_[:, c0:512], start=False, stop=True, skip_group_check=True),
                   reads=["cst_b", ("sp", h, par)], writes=[zk], inc=(it == 0))
                if it > 0:
                    op("pe", lambda e: e.matmul(zb[:, c0:512], negones_b, Rb[h][:, c0:512], start=False, stop=True,
                                                skip_group_check=True),
                       reads=["negones", ("R", h)], writes=[zk])

            def AR_(st):
                h, par, c0, it = st["h"], st["par"], st["c0"], st["it"]
                zb, zk, ob, ok_ = bufs(st)
                A_, sp_ = Ab[h][par], spb[h][par]
                op("act", lambda e: e.activation(out=A_[:, c0:512], in_=zb[:, c0:512], func=AF.Exp),
                   reads=[zk], writes=[("A", h, par)])
                if it == 0:
                    op("pool", lambda e: e.memset(Rb[h], 0.0), writes=[("R", h)])
                if not st["last"]:
                    op("pool", lambda e: e.tensor_tensor(out=Rb[h][:, c0:512], in0=Rb[h][:, c0:512], in1=sp_[:, c0:512], op=ALU.add),
                       reads=[("R", h), ("sp", h, par)], writes=[("R", h)])

            def AV(st):
                G, h, par, c0, it, b = st["G"], st["h"], st["par"], st["c0"], st["it"], st["b"]
                zb, zk, ob, ok_ = bufs(st)
                A_ = Ab[h][par]
                op("pe", lambda e: e.matmul(ob[0:64, c0:512], VT[:, b, 0, h * 64:(h + 1) * 64], A_[:, c0:512],
                                            start=(it == 0), stop=st["last"], skip_group_check=True),
                   reads=[("VT", b // 4), ("A", h, par)], writes=[ok_], inc=True)
                if st["last"]:
                    op("act", lambda e: e.activation(out=attnT[h * 64:(h + 1) * 64, pr, G * 512:(G + 1) * 512],
                                                     in_=ob[0:64, :], func=AF.Copy),
                       reads=[ok_], writes=[("attnT", pr, G)])

            for s in range(-1, N + 2):
                if 0 <= s - 1 < N:
                    CUM(steps[s - 1])
                if 0 <= s - 2 < N:
                    AV(steps[s - 2])
                if 0 <= s + 1 < N:
                    QK(steps[s + 1])
                if 0 <= s < N:
                    EL(steps[s])
                if 0 <= s - 1 < N:
                    AR_(steps[s - 1])

        def attn_fox(pr):
            hp = pr % 4
            steps = []
            for G in range(4):
                gs = list(range(8 * G + 7, -1, -1))
                for it, g in enumerate(gs):
                    for h in range(2):
                        par = hcount[h] % 2
                        hcount[h] += 1
                        steps.append(dict(G=G, it=it, g=g, h=h, par=par, last=(g == 0), c0=grp_cols(G, g), b=sb_of(g)))
            N = len(steps)

            def bufs(st):
                h, par = st["h"], st["par"]
                return bank(2 + 2 * h + par), ("ps", 2 + 2 * h + par), bank(6 + h), ("ps", 6 + h)

            def QK(st):
                G, g, h, par, c0, b = st["G"], st["g"], st["h"], st["par"], st["c0"], st["b"]
                zb, zk, ob, ok_ = bufs(st)
                d = g - 8 * G
                qsl = slice(G * 512 + c0, G * 512 + 512)
                has_tri = d >= 1 and d % 2 == 1
                has_dm = g == 0
                if h == 0:
                    kap, qap = KT[0:67, b * 128:(b + 1) * 128], QT[0:67, qsl]
                    rds = [("KT", b // 4), ("QT", G), ("QTaug", 0), "kaug"]
                else:
                    kap, qap = KTB[:, b * 128:(b + 1) * 128], QTB[:, qsl]
                    rds = [("KTB", b // 4), ("QTB", G), ("QTaug", 1), "kaug"]
                op("pe", lambda e, st_=not (has_tri or has_dm): e.matmul(zb[:, c0:512], kap, qap, start=True, stop=st_),
                   reads=rds, writes=[zk], inc=not (has_tri or has_dm))
                if has_tri:
                    op("pe", lambda e: e.matmul(zb[:, c0:c0 + 128], ident_b, triI_b, start=False, stop=not has_dm),
                       reads=["cst_b"], writes=[zk], inc=not has_dm)
                if has_dm:
                    op("pe", lambda e: e.matmul(zb[:, c0:512], ident_b, dm_b[:, c0:512], start=False, stop=True),
                       reads=["cst_b"], writes=[zk])

            def PX(st):
                g, h, par, c0 = st["g"], st["h"], st["par"], st["c0"]
                zb, zk, ob, ok_ = bufs(st)
                hg = hp * 2 + h
                A_ = Ab[h][par]
                op("act", lambda e: e.activation(out=A_[:, c0:512], in_=zb[:, c0:512], func=AF.Exp,
                                                 bias=fcncol[:, g * 8 + hg:g * 8 + hg + 1], scale=1.0),
                   reads=[zk, "fcncol"], writes=[("A", h, par)])

            def AV(st):
                G, h, par, c0, it, b = st["G"], st["h"], st["par"], st["c0"], st["it"], st["b"]
                zb, zk, ob, ok_ = bufs(st)
                A_ = Ab[h][par]
                op("pe", lambda e: e.matmul(ob[:, c0:512], VT[:, b, h, :], A_[:, c0:512], start=(it == 0), stop=st["last"],
                                            skip_group_check=True),
                   reads=[("VT", b // 4), ("A", h, par), "vones"], writes=[ok_], inc=True)
                if st["last"]:
                    op("dve", lambda e: e.reciprocal(out=rden[h][64:128, :], in_=ob[64:128, :]),
                       reads=[ok_], writes=[("rden", h)])
                    op("dve", lambda e: e.tensor_tensor(out=attnT[h * 64:(h + 1) * 64, pr, G * 512:(G + 1) * 512],
                                                        in0=ob[0:64, :], in1=rden[h][64:128, :], op=ALU.mult),
                       reads=[ok_, ("rden", h)], writes=[("attnT", pr, G)])

            for s in range(-1, N + 1):
                if 0 <= s - 1 < N:
                    AV(steps[s - 1])
                if 0 <= s + 1 < N:
                    QK(steps[s + 1])
                if 0 <= s < N:
                    PX(steps[s])

        for pr in range(8):
            if pr == 4:
                op("pool", lambda e: e.memset(KT[64:67, :], -1.0), writes=[("KT", i) for i in range(8)] + ["kaug"])
                op("pool", lambda e: e.memset(KTB[0:64, :], 0.0), writes=["kaug0"])
                op("pool", lambda e: e.memset(KTB[0:3, :], -1.0), reads=["kaug0"], writes=["kaug"])
                op("pool", lambda e: e.memset(QTB[0:64, :], 0.0), writes=[("QTaug", 1)])
                op("pool", lambda e: e.memset(VT[:, :, :, 64:128], 1.0), writes=[("VT", i) for i in range(8)] + ["vones"])
            proj_pair(pr)
            if pr < 4:
                attn_sb(pr)
            else:
                attn_fox(pr)
        if debug:
            dma("sp", dbg["attnT"], attnT, reads=[("attnT", pr, G) for pr in range(8) for G in range(4)], key="dbg9")
        S_.barrier()
        AR.release(m_att)

        mC1 = AR.mark()
        wbg = AR.alloc(BF16, KC, 2 * D)
        wsb_b = AR.alloc(BF16, 4, D)
        wfx_b = AR.alloc(BF16, 4, D)
        wo_b = AR.alloc(BF16, KC, D)
        lnB1 = AR.alloc(F32, 2, D)
        gT = [[AR.alloc(BF16, 512) for _ in range(2)] for _ in range(2)]
        tmpa = [AR.alloc(F32, 512) for _ in range(2)]
        tmpb = [AR.alloc(F32, 512) for _ in range(2)]
        mT = AR.alloc(BF16, KC, 512)
        xinB = [AR.alloc(F32, D) for _ in range(2)]
        r1 = [AR.alloc(F32, D) for _ in range(2)]
        x1b = [AR.alloc(F32, D) for _ in range(2)]
        sttB = [AR.alloc(F32, 16) for _ in range(2)]
        smlB = [AR.alloc(F32, 4) for _ in range(2)]

        for hf in range(2):
            dma("pool", wbg[:, :, hf * D:(hf + 1) * D], w_in[:, :, OFF_BG + hf * D:OFF_BG + (hf + 1) * D], writes=["wbg"], key="w0")
        dma("pool", wsb_b, w_sb, writes=["wsb"], key="w1")
        dma("pool", wfx_b, w_fx, writes=["wfx"], key="w2")
        for i in range(2):
            dma("sp", lnB1[:, i, :], lnp[i:i + 1, :].partition_broadcast(128), writes=[("lnB1", i)], key=("lnB1", i))
        for kc in range(KC):
            st_ = xinB[kc % 2]
            dma("sp", st_, w_o[:, kc, :], writes=[("xinB", kc % 2)], key=("xinB", kc % 2))
            op("pool", lambda e, st_=st_, kc=kc: e.tensor_tensor(out=wo_b[:, kc, :], in0=st_, in1=GB[:, 0, :], op=ALU.mult),
               reads=[("xinB", kc % 2), ("GB", 0, 0), ("GB", 0, 1)], writes=[("wo", kc)])

        def own_rows(j):
            g = 2 * j + 1
            return slice(g * 128, (g + 1) * 128)

        for G in range(4):
            ukeys = [("uT", 2 * j + 1) for j in range(4 * G, 4 * G + 4)]
            for n in range(8):
                par = n % 2
                for w_ in range(2):
                    bk = w_ * 2 + par
                    col = w_ * D + n * 128
                    for kc in range(KC):
                        op("pe", lambda e, bk=bk, kc=kc, col=col, G=G: e.matmul(bank(bk), wbg[:, kc, col:col + 128],
                                                                             uT_own[:, kc, G * 512:(G + 1) * 512],
                                                                             start=(kc == 0), stop=(kc == KC - 1)),
                           reads=["wbg"] + ukeys, writes=[("ps", bk)], inc=(kc == KC - 1))
                    op("act", lambda e, bk=bk, w_=w_, par=par, n=n: e.activation(out=gT[w_][par], in_=bank(bk), func=AF.Sigmoid,
                                                                                bias=bgcol[:, w_ * 8 + n:w_ * 8 + n + 1], scale=1.0),
                       reads=[("ps", bk), "bgcol"], writes=[("gT", w_, par)])
                for w_, wt in enumerate((wsb_b, wfx_b)):
                    bk = 4 + w_ * 2 + par
                    for kc in range(4):
                        op("pe", lambda e, bk=bk, kc=kc, wt=wt, w_=w_, n=n, G=G: e.matmul(
                            bank(bk), wt[:, kc, n * 128:(n + 1) * 128], attnT[:, w_ * 4 + kc, G * 512:(G + 1) * 512],
                            start=(kc == 0), stop=(kc == 3)),
                           reads=["wsb", "wfx"] + [("attnT", w_ * 4 + kc, G)], writes=[("ps", bk)], inc=(kc == 3))
                op("dve", lambda e, par=par: e.tensor_tensor(out=tmpa[par], in0=bank(4 + par), in1=gT[0][par], op=ALU.mult),
                   reads=[("ps", 4 + par), ("gT", 0, par)], writes=[("tmpa", par)])
                op("dve", lambda e, par=par: e.tensor_tensor(out=tmpb[par], in0=bank(6 + par), in1=gT[1][par], op=ALU.mult),
                   reads=[("ps", 6 + par), ("gT", 1, par)], writes=[("tmpb", par)])
                op("pool", lambda e, par=par, n=n: e.tensor_tensor(out=mT[:, n, :], in0=tmpa[par], in1=tmpb[par], op=ALU.add),
                   reads=[("tmpa", par), ("tmpb", par)], writes=[("mT", n)])
            for i in range(4):
                j = 4 * G + i
                k = j % 2
                dma("sp", xinB[k], xall[own_rows(j), :], writes=[("xinB", k)], key=("xinB", k))
                for hf in range(2):
                    bk = hf
                    for kc in range(KC):
                        op("pe", lambda e, bk=bk, kc=kc, i=i, hf=hf: e.matmul(bank(bk), mT[:, kc, i * 128:(i + 1) * 128],
                                                                           wo_b[:, kc, hf * 512:(hf + 1) * 512],
                                                                           start=(kc == 0), stop=(kc == KC - 1)),
                           reads=[("mT", kc), ("wo", kc)], writes=[("ps", bk)], inc=(kc == KC - 1))
                    op("dve", lambda e, bk=bk, k=k, hf=hf: e.scalar_tensor_tensor(
                        out=r1[k][:, hf * 512:(hf + 1) * 512], in0=xinB[k][:, hf * 512:(hf + 1) * 512], scalar=ALPHA,
                        in1=bank(bk), op0=ALU.mult, op1=ALU.add),
                       reads=[("xinB", k), ("ps", bk)], writes=[("r1", k, hf)])
                st, sm = sttB[k], smlB[k]

                def ln_stats2(src, rks, k=k, st=st, sm=sm):
                    op("dve", lambda e: e.bn_stats(out=st[:, 0:6], in_=src[:, 0:512]), reads=rks, writes=[("st", k, 0)])
                    op("dve", lambda e: e.bn_stats(out=st[:, 6:12], in_=src[:, 512:1024]), reads=rks, writes=[("st", k, 1)])
                    op("dve", lambda e: e.bn_aggr(out=st[:, 12:14], in_=st[:, 0:12]), reads=[("st", k, 0), ("st", k, 1)],
                       writes=[("mv", k)])
                    op("act", lambda e: e.activation(out=sm[:, 0:1], in_=st[:, 13:14], func=AF.Sqrt, bias=epsc[:, 0:1], scale=1.0),
                       reads=[("mv", k), "epsc"], writes=[("sd", k)])
                    op("dve", lambda e: e.reciprocal(out=sm[:, 1:2], in_=sm[:, 0:1]), reads=[("sd", k)], writes=[("rstd", k)])
                    op("dve", lambda e: e.tensor_scalar(out=sm[:, 2:3], in0=st[:, 12:13], scalar1=-1.0, scalar2=sm[:, 1:2],
                                                        op0=ALU.mult, op1=ALU.mult),
                       reads=[("mv", k), ("rstd", k)], writes=[("nmr", k)])

                ln_stats2(r1[k], [("r1", k, 0), ("r1", k, 1)])
                op("act", lambda e, k=k: e.activation(out=r1[k], in_=r1[k], func=AF.Identity, bias=smlB[k][:, 2:3], scale=smlB[k][:, 1:2]),
                   reads=[("r1", k, 0), ("r1", k, 1), ("rstd", k), ("nmr", k)], writes=[("r1", k, 0), ("r1", k, 1)])
                op("pool", lambda e, k=k: e.tensor_tensor(out=x1b[k], in0=r1[k], in1=lnB1[:, 0, :], op=ALU.mult),
                   reads=[("r1", k, 0), ("r1", k, 1), ("lnB1", 0)], writes=[("x1", k)])
                op("pool", lambda e, k=k: e.tensor_tensor(out=x1b[k], in0=x1b[k], in1=lnB1[:, 1, :], op=ALU.add),
                   reads=[("x1", k), ("lnB1", 1)], writes=[("x1", k)])
                dma("sp", x1_scr[j * 128:(j + 1) * 128, :], x1b[k], reads=[("x1", k)], writes=[("x1scr", j)], key=("x1o", k))
                if debug:
                    dma("sp", dbg["x1"][j * 128:(j + 1) * 128, :], x1b[k], reads=[("x1", k)], key=("dbgx1", k))
                ln_stats2(x1b[k], [("x1", k)])
                op("act", lambda e, k=k: e.activation(out=r1[k], in_=x1b[k], func=AF.Identity, bias=smlB[k][:, 2:3], scale=smlB[k][:, 1:2]),
                   reads=[("x1", k), ("rstd", k), ("nmr", k)], writes=[("r1", k, 0), ("r1", k, 1)])
                for kc in range(KC):
                    bk = 2 + kc // 4
                    op("pe", lambda e, bk=bk, kc=kc, k=k: e.transpose(bank(bk)[:, (kc % 4) * 128:(kc % 4 + 1) * 128],
                                                                  r1[k][:, kc * 128:(kc + 1) * 128], ident_f),
                       reads=[("r1", k, 0), ("r1", k, 1), "ident_f"], writes=[("ps", bk)], inc=(kc % 4 == 3))
                for kc in range(KC):
                    bk = 2 + kc // 4
                    op("dve", lambda e, bk=bk, kc=kc, j=j: e.tensor_scalar(
                        out=uT_own[:, kc, j * 128:(j + 1) * 128], in0=bank(bk)[:, (kc % 4) * 128:(kc % 4 + 1) * 128],
                        scalar1=s2col[:, kc:kc + 1], scalar2=adacol[:, 24 + kc:25 + kc], op0=ALU.mult, op1=ALU.add),
                       reads=[("ps", bk), "s2col", "adacol"], writes=[("uT", 2 * j + 1)])
        S_.barrier()
        AR.release(m_attnT)

        wg_b = AR.alloc(BF16, KC, DFF)
        wu_b = AR.alloc(BF16, KC, DFF)
        wd_b = AR.alloc(BF16, FC, D)
        hT = AR.alloc(BF16, FC, 256)
        sgt = [AR.alloc(F32, 256) for _ in range(2)]
        xin = [AR.alloc(F32, D) for _ in range(2)]
        r2 = xin
        lnB = GB
        stt = [AR.alloc(F32, 16) for _ in range(2)]
        sml = [AR.alloc(F32, 4) for _ in range(2)]
        for kc in range(KC):
            for hf in range(2):
                dma("pool", wg_b[:, kc, hf * 1408:(hf + 1) * 1408], w_g[:, kc, hf * 1408:(hf + 1) * 1408], writes=[("wg", kc)], key="w3")
                dma("pool", wu_b[:, kc, hf * 1408:(hf + 1) * 1408], w_u[:, kc, hf * 1408:(hf + 1) * 1408], writes=[("wu", kc)], key="w4")
        S_.group_final("w3", [("wg", kc) for kc in range(KC)])
        S_.group_final("w4", [("wu", kc) for kc in range(KC)])
        for fc in range(FC):
            st_ = xin[fc % 2]
            dma("sp", st_, w_d[:, fc, :], writes=[("xin2", fc % 2)], key=("xin", fc % 2))
            op("pool", lambda e, st_=st_, fc=fc: e.tensor_tensor(out=wd_b[:, fc, :], in0=st_, in1=GB[:, 1, :], op=ALU.mult),
               reads=[("xin2", fc % 2), ("GB", 1, 0), ("GB", 1, 1)], writes=[("wd", fc)])
        for i in range(2):
            dma("sp", lnB[:, i, :], lnp[2 + i:3 + i, :].partition_broadcast(128), writes=[("GB", i, 0), ("GB", i, 1)],
                key=("lnB", i))
        wgk = [("wg", kc) for kc in range(KC)]
        wuk = [("wu", kc) for kc in range(KC)]
        for sub in range(8):
            cs = sub * 256
            ukeys = [("uT", 2 * j + 1) for j in (2 * sub, 2 * sub + 1)]
            for fc in range(FC):
                par = fc % 2
                bg, bu = bank(par), bank(2 + par)
                for kc in range(KC):
                    op("pe", lambda e, bg=bg, kc=kc, fc=fc, cs=cs: e.matmul(bg[:, 0:256], wg_b[:, kc, fc * 128:(fc + 1) * 128],
                                                                       uT_own[:, kc, cs:cs + 256], start=(kc == 0), stop=(kc == KC - 1)),
                       reads=wgk + ukeys, writes=[("ps", par)], inc=(kc == KC - 1))
                for kc in range(KC):
                    op("pe", lambda e, bu=bu, kc=kc, fc=fc, cs=cs: e.matmul(bu[:, 0:256], wu_b[:, kc, fc * 128:(fc + 1) * 128],
                                                                       uT_own[:, kc, cs:cs + 256], start=(kc == 0), stop=(kc == KC - 1)),
                       reads=wuk + ukeys, writes=[("ps", 2 + par)], inc=(kc == KC - 1))
                op("act", lambda e, bg=bg, par=par: e.activation(out=sgt[par], in_=bg[:, 0:256], func=AF.Silu),
                   reads=[("ps", par)], writes=[("sgt", par)])
                op("dve", lambda e, bu=bu, par=par, fc=fc: e.tensor_tensor(out=hT[:, fc, :], in0=bu[:, 0:256], in1=sgt[par], op=ALU.mult),
                   reads=[("ps", 2 + par), ("sgt", par)], writes=[("hT", fc)])
            for i in range(2):
                j = 2 * sub + i
                k = j % 2
                dma("sp", xin[k], x1_scr[j * 128:(j + 1) * 128, :], reads=[("x1scr", j)], writes=[("xin2", k)], key=("xin", k))
                for hf in range(2):
                    bk = 4 + 2 * k + hf
                    for fc in range(FC):
                        op("pe", lambda e, bk=bk, fc=fc, i=i, hf=hf: e.matmul(bank(bk), hT[:, fc, i * 128:(i + 1) * 128],
                                                                           wd_b[:, fc, hf * 512:(hf + 1) * 512],
                                                                           start=(fc == 0), stop=(fc == FC - 1)),
                           reads=[("hT", fc), ("wd", fc)], writes=[("ps", bk)], inc=(fc == FC - 1))
                    op("dve", lambda e, bk=bk, k=k, hf=hf: e.scalar_tensor_tensor(
                        out=r2[k][:, hf * 512:(hf + 1) * 512], in0=xin[k][:, hf * 512:(hf + 1) * 512], scalar=ALPHA,
                        in1=bank(bk), op0=ALU.mult, op1=ALU.add),
                       reads=[("xin2", k), ("ps", bk)], writes=[("xin2", k)])
                st, sm = stt[k], sml[k]
                op("dve", lambda e, st=st, k=k: e.bn_stats(out=st[:, 0:6], in_=r2[k][:, 0:512]), reads=[("xin2", k)], writes=[("st", k, 0)])
                op("dve", lambda e, st=st, k=k: e.bn_stats(out=st[:, 6:12], in_=r2[k][:, 512:1024]), reads=[("xin2", k)], writes=[("st", k, 1)])
                op("dve", lambda e, st=st: e.bn_aggr(out=st[:, 12:14], in_=st[:, 0:12]), reads=[("st", k, 0), ("st", k, 1)],
                   writes=[("mv", k)])
                op("act", lambda e, st=st, sm=sm: e.activation(out=sm[:, 0:1], in_=st[:, 13:14], func=AF.Sqrt, bias=epsc[:, 0:1], scale=1.0),
                   reads=[("mv", k), "epsc"], writes=[("sd", k)])
                op("dve", lambda e, sm=sm: e.reciprocal(out=sm[:, 1:2], in_=sm[:, 0:1]), reads=[("sd", k)], writes=[("rstd", k)])
                op("dve", lambda e, st=st, sm=sm: e.tensor_scalar(out=sm[:, 2:3], in0=st[:, 12:13], scalar1=-1.0, scalar2=sm[:, 1:2],
                                                                  op0=ALU.mult, op1=ALU.mult),
                   reads=[("mv", k), ("rstd", k)], writes=[("nmr", k)])
                op("act", lambda e, k=k, sm=sm: e.activation(out=r2[k], in_=r2[k], func=AF.Identity, bias=sm[:, 2:3], scale=sm[:, 1:2]),
                   reads=[("xin2", k), ("rstd", k), ("nmr", k)], writes=[("xin2", k)])
                op("pool", lambda e, k=k: e.tensor_tensor(out=r2[k], in0=r2[k], in1=lnB[:, 0, :], op=ALU.mult),
                   reads=[("xin2", k), ("GB", 0, 0), ("GB", 0, 1)], writes=[("xin2", k)])
                op("pool", lambda e, k=k: e.tensor_tensor(out=r2[k], in0=r2[k], in1=lnB[:, 1, :], op=ALU.add),
                   reads=[("xin2", k), ("GB", 1, 0), ("GB", 1, 1)], writes=[("xin2", k)])
                dma("sp", out[j * 128:(j + 1) * 128, :], r2[k], reads=[("xin2", k)], writes=[("out", j)], key=("outd", k))
        S_.wait_all("sp", [("outd", 0), ("outd", 1)] + ([k for k in S_.dma_cnt if str(k).startswith("dbg") or (isinstance(k, tuple) and k[0] == "dbgx1")] if debug else []))
        print("arena peak bytes", AR.peak, "pe", S_.cnt["pe"], "act", S_.cnt["act"], "dve", S_.cnt["dve"], "pool", S_.cnt["pool"],
              "instr", {k: len(v) for k, v in S_.prog.items()})

        with nc.Block() as block:
            @block.tensor
            def _(e):
                S_.emit("pe", e)

            @block.scalar
            def _(e):
                S_.emit("act", e)

            @block.vector
            def _(e):
                S_.emit("dve", e)

            @block.gpsimd
            def _(e):
                S_.emit("pool", e)

            @block.sync
            def _(e):
                S_.emit("sp", e)
    return nc


_CONSTS = None


def _consts():
    global _CONSTS
    if _CONSTS is None:
        s = np.arange(128)[:, None]
        t = np.arange(128)[None, :]
        ident = (s == t).astype(np.float32)
        triS = np.where(s < t, 0.0, NEG).astype(np.float32)
        triI = np.where(s <= t, 0.0, NEG).astype(np.float32)
        negU = np.where(s >= t, -1.0, 0.0).astype(np.float32)
        _CONSTS = np.concatenate([ident, triS, triI, negU], axis=1)
    return _CONSTS


def _rk(w, kc):
    n = w.shape[1]
    return np.ascontiguousarray(w.reshape(kc, 128, n).transpose(1, 0, 2))


def make_in_maps(x, c, w_ada, b_ada, w_in, b_gate, b_forget, w_sb_out, w_fox_out, w_o,
                 ln1_g, ln1_b, w_ffn_gate, w_ffn_up, w_ffn_down, ln2_g, ln2_b):
    f = np.float32
    x = np.asarray(x, f)
    shared = {
        "w_ada": _rk(np.asarray(w_ada, f)[0], KC),
        "b_ada": np.ascontiguousarray(np.asarray(b_ada, f)[0][None, :]),
        "w_in": _rk(np.asarray(w_in, f)[0], KC),
        "b_gate": np.ascontiguousarray(np.asarray(b_gate, f)[0].reshape(16, 128).T),
        "b_forget": np.ascontiguousarray(np.asarray(b_forget, f)[0].reshape(8, 1)),
        "w_sb": _rk(np.asarray(w_sb_out, f)[0], 4),
        "w_fx": _rk(np.asarray(w_fox_out, f)[0], 4),
        "w_o": _rk(np.asarray(w_o, f)[0], KC),
        "lnp": np.ascontiguousarray(np.stack([np.asarray(a, f)[0] for a in (ln1_g, ln1_b, ln2_g, ln2_b)])),
        "w_g": _rk(np.asarray(w_ffn_gate, f)[0], KC),
        "w_u": _rk(np.asarray(w_ffn_up, f)[0], KC),
        "w_d": _rk(np.asarray(w_ffn_down, f)[0], FC),
    }
    cb = _consts()
    maps = []
    for core in range(8):
        b, p = core // 2, core % 2
        if p == 1:
            xa = x[b]
        else:
            xa = np.concatenate([np.zeros((128, D), f), x[b][:S - 128]], axis=0)
        dmask = np.full((128, 512), NEG if p == 0 else 0.0, f)
        m = dict(shared)
        m["xall"] = np.ascontiguousarray(xa)
        m["c_col"] = np.ascontiguousarray(np.asarray(c, f)[b].reshape(KC, 128).T)
        m["consts"] = np.ascontiguousarray(np.concatenate([cb, dmask], axis=1))
        maps.append(m)
    return maps


def assemble(results):
    out = np.zeros((4, S, D), np.float32)
    for core in range(8):
        b, p = core // 2, core % 2
        o = np.asarray(results[core]["out"], np.float32).reshape(NOWN, 128, D)
        ov = out[b].reshape(NB, 128, D)
        for j in range(NOWN):
            ov[2 * j + p] = o[j]
    return out


def kernel(**inputs):
    nc = build_nc(debug=False)
    maps = make_in_maps(**inputs)
    res = run_bass_kernel_spmd(nc, maps, core_ids=list(range(8)))
    return assemble(res.results)
```

```python
import numpy as np
from contextlib import ExitStack
import concourse.bass as bass
import concourse.mybir as mybir
from concourse.bass_utils import run_bass_kernel_spmd

F32 = mybir.dt.float32
BF16 = mybir.dt.bfloat16
AF = mybir.ActivationFunctionType
ALU = mybir.AluOpType

D = 1024
KC = 8
S = 4096
NB = 32
NOWN = 16
DFF = 2816
FC = 22
IN_COLS = 5128
OFF_SB = 0
OFF_FOX = 1536
OFF_FG = 3072
OFF_BG = 3080
LN_EPS = 1e-5
ALPHA = 2.0 ** 0.25
NEG = -30000.0

COMPUTE = ("pe", "act", "dve", "pool")


class Sched:
    def __init__(self, nc, stack, n_dma_sems=64):
        self.nc = nc
        self.prog = {e: [] for e in ("pe", "act", "dve", "pool", "sp")}
        self.cnt = {e: 0 for e in COMPUTE}
        self.sem = {e: stack.enter_context(nc.semaphore("s_" + e)) for e in COMPUTE}
        self.dma_pool = [stack.enter_context(nc.semaphore("d%d" % i)) for i in range(n_dma_sems)]
        self.dma_key = {}
        self.dma_cnt = {}
        self.known = {e: {} for e in self.prog}
        self.lastw = {}
        self.reads = {}

    def _semh(self, k):
        return self.sem[k] if k in self.sem else self.dma_pool[self.dma_key[k]]

    def _deps(self, eng, reads, writes):
        deps = set()
        for b in reads:
            ev = self.lastw.get(b)
            if ev:
                deps.add(ev)
        for b in writes:
            ev = self.lastw.get(b)
            if ev and not (ev[0] == eng and eng in COMPUTE):
                deps.add(ev)
            for ev in self.reads.get(b, ()):
                if not (ev[0] == eng and eng in COMPUTE):
                    deps.add(ev)
        waits = {}
        kn = self.known[eng]
        for (k, v) in deps:
            if k == eng and eng == "pe":
                continue
            if kn.get(k, 0) >= v:
                continue
            if waits.get(k, 0) < v:
                waits[k] = v
        for k, v in waits.items():
            kn[k] = v
        return [(self._semh(k), v) for k, v in waits.items()]

    def _record(self, ev, reads, writes):
        for b in reads:
            self.reads.setdefault(b, []).append(ev)
        for b in writes:
            self.lastw[b] = ev
            self.reads[b] = []

    def op(self, eng, fn, reads=(), writes=(), inc=True):
        waits = self._deps(eng, reads, writes)
        if inc:
            self.cnt[eng] += 1
            ev = (eng, self.cnt[eng])
        else:
            ev = (eng, self.cnt[eng] + 1)
        self._record(ev, reads, writes)
        self.prog[eng].append((waits, fn, (self.sem[eng], 1) if inc else None))

    def dma(self, queue, out, in_, reads=(), writes=(), key=None, **kw):
        assert key is not None
        if key not in self.dma_key:
            idx = len(self.dma_key)
            assert idx < len(self.dma_pool), "out of dma semaphores"
            self.dma_key[key] = idx
            self.dma_cnt[key] = 0
        waits = self._deps(queue, reads, writes)
        self.dma_cnt[key] += 16
        ev = (key, self.dma_cnt[key])
        self._record(ev, reads, writes)
        fn = lambda e, out=out, in_=in_, kw=kw: e.dma_start(out=out, in_=in_, **kw)
        self.prog[queue].append((waits, fn, (self._semh(key), 16)))

    def group_final(self, key, bufs):
        for b in bufs:
            self.lastw[b] = (key, self.dma_cnt[key])

    def wait_all(self, eng, keys):
        waits = []
        for k in keys:
            v = self.dma_cnt[k] if k in self.dma_cnt else self.cnt[k]
            if v > 0:
                waits.append((self._semh(k), v))
        self.prog[eng].append((waits, None, None))

    def barrier(self):
        keys = list(COMPUTE) + list(self.dma_cnt.keys())
        for eng in self.prog:
            waits = []
            for k in keys:
                v = self.dma_cnt[k] if k in self.dma_cnt else self.cnt[k]
                if k == eng or v == 0:
                    continue
                if self.known[eng].get(k, 0) >= v:
                    continue
                self.known[eng][k] = v
                waits.append((self._semh(k), v))
            if waits:
                self.prog[eng].append((waits, None, None))

    def emit(self, eng, e):
        for waits, fn, inc in self.prog[eng]:
            for (h, v) in waits:
                e.wait_ge(h, v)
            if fn is None:
                continue
            ins = fn(e)
            if inc is not None:
                ins.then_inc(inc[0], inc[1])


class Arena:
    def __init__(self, nc, stack, nbytes):
        self.h32 = stack.enter_context(nc.sbuf_tensor("arena", [128, nbytes // 4], F32))
        self.h16 = self.h32.bitcast(BF16)
        self.top = 0
        self.limit = nbytes
        self.peak = 0

    def alloc(self, dtype, *free):
        n = 1
        for f in free:
            n *= f
        sz = 4 if dtype == F32 else 2
        nb = (n * sz + 63) // 64 * 64
        off = self.top
        self.top += nb
        self.peak = max(self.peak, self.top)
        assert self.top <= self.limit, "arena overflow %d > %d" % (self.top, self.limit)
        base = self.h32 if dtype == F32 else self.h16
        e0 = off // sz
        ap = base[:, e0:e0 + n]
        if len(free) == 2:
            ap = ap.rearrange("p (a b) -> p a b", a=free[0], b=free[1])
        elif len(free) == 3:
            ap = ap.rearrange("p (a b c) -> p a b c", a=free[0], b=free[1], c=free[2])
        return ap

    def mark(self):
        return self.top

    def release(self, m):
        self.top = m


def sb_of(g):
    return g // 2 if g % 2 == 0 else 16 + g // 2


def build_nc(debug=False):
    nc = bass.Bass("TRN2", target_bir_lowering=False)
    dt = nc.dram_tensor
    xall = dt("xall", [S, D], F32, kind="ExternalInput").ap()
    c_col = dt("c_col", [128, KC], F32, kind="ExternalInput").ap()
    w_ada = dt("w_ada", [128, KC, 6 * D], F32, kind="ExternalInput").ap()
    b_ada = dt("b_ada", [1, 6 * D], F32, kind="ExternalInput").ap()
    w_in = dt("w_in", [128, KC, IN_COLS], F32, kind="ExternalInput").ap()
    b_gate = dt("b_gate", [128, 16], F32, kind="ExternalInput").ap()
    b_forget = dt("b_forget", [8, 1], F32, kind="ExternalInput").ap()
    w_sb = dt("w_sb", [128, 4, D], F32, kind="ExternalInput").ap()
    w_fx = dt("w_fx", [128, 4, D], F32, kind="ExternalInput").ap()
    w_o = dt("w_o", [128, KC, D], F32, kind="ExternalInput").ap()
    lnp = dt("lnp", [4, D], F32, kind="ExternalInput").ap()
    w_g = dt("w_g", [128, KC, DFF], F32, kind="ExternalInput").ap()
    w_u = dt("w_u", [128, KC, DFF], F32, kind="ExternalInput").ap()
    w_d = dt("w_d", [128, FC, D], F32, kind="ExternalInput").ap()
    consts = dt("consts", [128, 4 * 128 + 512], F32, kind="ExternalInput").ap()
    out = dt("out", [NOWN * 128, D], F32, kind="ExternalOutput").ap()
    fc_scr = dt("fc_scr", [8, 3, S], BF16, kind="Internal").ap()
    x1_scr = dt("x1_scr", [NOWN * 128, D], F32, kind="Internal").ap()
    dbg = {}
    if debug:
        dbg["uT"] = dt("dbg_uT", [128, KC, S], BF16, kind="ExternalOutput").ap()
        dbg["attnT"] = dt("dbg_attnT", [128, KC, NOWN * 128], BF16, kind="ExternalOutput").ap()
        dbg["ada"] = dt("dbg_ada", [128, 48], F32, kind="ExternalOutput").ap()
        dbg["fcn"] = dt("dbg_fcn", [8, S], F32, kind="ExternalOutput").ap()
        dbg["x1"] = dt("dbg_x1", [NOWN * 128, D], F32, kind="ExternalOutput").ap()
        dbg["sml"] = dt("dbg_sml", [128, 4], F32, kind="ExternalOutput").ap()
        dbg["stt"] = dt("dbg_stt", [128, 16], F32, kind="ExternalOutput").ap()
        dbg["xn"] = dt("dbg_xn", [128, D], F32, kind="ExternalOutput").ap()
        dbg["s1col"] = dt("dbg_s1col", [128, 8], F32, kind="ExternalOutput").ap()

    with ExitStack() as stack:
        S_ = Sched(nc, stack)
        AR = Arena(nc, stack, 206 * 1024)
        ps = stack.enter_context(nc.psum_tensor("ps", [128, 8 * 512], F32))

        def bank(i):
            return ps[:, i * 512:(i + 1) * 512]

        op, dma = S_.op, S_.dma

        ident_f = AR.alloc(F32, 128)
        cst_b = AR.alloc(BF16, 4 * 128 + 512)
        ident_b = cst_b[:, 0:128]
        triS_b = cst_b[:, 128:256]
        triI_b = cst_b[:, 256:384]
        negU_b = cst_b[:, 384:512]
        dm_b = cst_b[:, 512:1024]
        negones_b = AR.alloc(BF16, 128)
        adacol = AR.alloc(F32, 48)
        s1col = AR.alloc(F32, 8)
        s2col = AR.alloc(F32, 8)
        bgcol = AR.alloc(F32, 16)
        fcncol = AR.alloc(F32, 256)
        one_f = AR.alloc(F32, 128)
        GB = AR.alloc(F32, 2, D)
        epsc = AR.alloc(F32, 1)
        uT_own = AR.alloc(BF16, KC, NOWN * 128)
        m_attnT = AR.mark()
        attnT = AR.alloc(BF16, KC, NOWN * 128)
        op("dve", lambda e: e.memset(epsc, LN_EPS), writes=["epsc"])

        dma("sp", ident_f, consts[:, 0:128], writes=["ident_f"], key="c0")
        dma("pool", cst_b, consts, writes=["cst_b"], key="c1")
        dma("sp", bgcol, b_gate, writes=["bgcol"], key="c2")
        op("dve", lambda e: e.memset(negones_b, -1.0), writes=["negones"])
        op("dve", lambda e: e.memset(one_f, 1.0), writes=["one_f"])

        m_att = AR.mark()
        uT_oth = AR.alloc(BF16, KC, NOWN * 128)

        def uT_blk(g, kc):
            t = uT_own if g % 2 == 1 else uT_oth
            j = g // 2
            return t[:, kc, j * 128:(j + 1) * 128]

        mA = AR.mark()
        ccol = AR.alloc(F32, KC)
        cact = AR.alloc(F32, KC)
        badg = [AR.alloc(F32, 512) for _ in range(2)]
        adag = [AR.alloc(F32, 512) for _ in range(2)]
        wst = [AR.alloc(F32, KC, 512) for _ in range(2)]
        xb = [AR.alloc(F32, D) for _ in range(4)]
        xn = [AR.alloc(F32, D) for _ in range(3)]
        sttA = [AR.alloc(F32, 16) for _ in range(4)]
        smlA = [AR.alloc(F32, 4) for _ in range(4)]
        dma("sp", ccol, c_col, writes=["ccol"], key="c3")
        op("act", lambda e: e.activation(out=cact, in_=ccol, func=AF.Silu), reads=["ccol"], writes=["cact"])

        def ada_group(ng):
            sl = ng % 2
            dma("act", wst[sl], w_ada[:, :, ng * 512:(ng + 1) * 512], writes=[("wst", sl)], key=("wst", sl))
            dma("act", badg[sl][0:1, :], b_ada[:, ng * 512:(ng + 1) * 512], writes=[("badg", sl)], key=("badg", sl))
            pb = bank(sl)
            for kc in range(KC):
                op("pe", lambda e, pb=pb, kc=kc, sl=sl: e.matmul(pb[0:1, :], cact[:, kc:kc + 1], wst[sl][:, kc, :],
                                                              start=(kc == 0), stop=(kc == KC - 1)),
                   reads=["cact", ("wst", sl)], writes=[("ps", sl)], inc=(kc == KC - 1))
            op("dve", lambda e, pb=pb, sl=sl: e.tensor_tensor(out=adag[sl][0:1, :], in0=pb[0:1, :], in1=badg[sl][0:1, :], op=ALU.add),
               reads=[("ps", sl), ("badg", sl)], writes=[("adag", sl)])
            for c4 in range(4):
                c = ng * 4 + c4
                op("pe", lambda e, c=c, c4=c4, sl=sl: e.matmul(bank(2)[:, c:c + 1], adag[sl][0:1, c4 * 128:(c4 + 1) * 128], one_f[0:1, 0:1],
                                                            start=True, stop=True),
                   reads=[("adag", sl), "one_f"], writes=[("ps", 2)], inc=(c4 == 3))
            if ng in (4, 5, 10, 11):
                i, hf = (0 if ng < 6 else 1), ng % 2
                op("pe", lambda e, sl=sl: e.matmul(bank(3), one_f[0:1, :], adag[sl][0:1, :], start=True, stop=True),
                   reads=[("adag", sl), "one_f"], writes=[("ps", 3)])
                op("dve", lambda e, i=i, hf=hf: e.tensor_copy(out=GB[:, i, hf * 512:(hf + 1) * 512], in_=bank(3)),
                   reads=[("ps", 3)], writes=[("GB", i, hf)])

        for ng in range(4):
            ada_group(ng)
        op("dve", lambda e: e.tensor_copy(out=adacol[:, 0:16], in_=bank(2)[:, 0:16]), reads=[("ps", 2)], writes=["adacol"])
        op("dve", lambda e: e.tensor_scalar(out=s1col, in0=adacol[:, 8:16], scalar1=1.0, scalar2=None, op0=ALU.add),
           reads=["adacol"], writes=["s1col"])

        def A1(g):
            k4 = g % 4
            xt, st = xb[k4], sttA[k4]
            dma("sp", xt, xall[g * 128:(g + 1) * 128, :], writes=[("xb", k4)], key=("xb", k4))
            op("dve", lambda e: e.bn_stats(out=st[:, 0:6], in_=xt[:, 0:512]), reads=[("xb", k4)], writes=[("st", k4, 0)])
            op("dve", lambda e: e.bn_stats(out=st[:, 6:12], in_=xt[:, 512:1024]), reads=[("xb", k4)], writes=[("st", k4, 1)])
            op("dve", lambda e: e.bn_aggr(out=st[:, 12:14], in_=st[:, 0:12]), reads=[("st", k4, 0), ("st", k4, 1)],
               writes=[("mv", k4)])
            sm = smlA[k4]
            op("act", lambda e: e.activation(out=sm[:, 0:1], in_=st[:, 13:14], func=AF.Sqrt, bias=epsc[:, 0:1], scale=1.0),
               reads=[("mv", k4), "epsc"], writes=[("sd", k4)])

        def A3(g):
            k4 = g % 4
            st, sm, xt = sttA[k4], smlA[k4], xb[k4]
            xo = xn[g % 3]
            op("dve", lambda e: e.reciprocal(out=sm[:, 1:2], in_=sm[:, 0:1]), reads=[("sd", k4)], writes=[("rstd", k4)])
            op("dve", lambda e: e.tensor_scalar(out=sm[:, 2:3], in0=st[:, 12:13], scalar1=-1.0, scalar2=sm[:, 1:2],
                                                op0=ALU.mult, op1=ALU.mult),
               reads=[("mv", k4), ("rstd", k4)], writes=[("nmr", k4)])
            op("act", lambda e: e.activation(out=xo, in_=xt, func=AF.Identity, bias=sm[:, 2:3], scale=sm[:, 1:2]),
               reads=[("xb", k4), ("rstd", k4), ("nmr", k4)], writes=[("xn", g % 3)])

        def A5(g):
            k = g % 2
            xo = xn[g % 3]
            for kc in range(KC):
                bk = 4 + 2 * k + kc // 4
                op("pe", lambda e, bk=bk, kc=kc: e.transpose(bank(bk)[:, (kc % 4) * 128:(kc % 4 + 1) * 128],
                                                         xo[:, kc * 128:(kc + 1) * 128], ident_f),
                   reads=[("xn", g % 3), "ident_f"], writes=[("ps", bk)], inc=(kc % 4 == 3))

        def A6(g):
            k = g % 2
            for kc in range(KC):
                bk = 4 + 2 * k + kc // 4
                src = bank(bk)[:, (kc % 4) * 128:(kc % 4 + 1) * 128]
                if kc % 2 == 0:
                    op("dve", lambda e, src=src, kc=kc: e.tensor_scalar(
                        out=uT_blk(g, kc), in0=src, scalar1=s1col[:, kc:kc + 1], scalar2=adacol[:, kc:kc + 1],
                        op0=ALU.mult, op1=ALU.add),
                       reads=[("ps", bk), "s1col", "adacol"], writes=[("uT", g)])
                else:
                    op("act", lambda e, src=src, kc=kc: e.activation(out=uT_blk(g, kc), in_=src, func=AF.Identity,
                                                                 bias=adacol[:, kc:kc + 1], scale=s1col[:, kc:kc + 1]),
                       reads=[("ps", bk), "s1col", "adacol"], writes=[("uT", g)])

        for s_ in range(NB + 3):
            if s_ < NB:
                A1(s_)
            if 0 <= s_ - 1 < NB:
                A3(s_ - 1)
            if 0 <= s_ - 2 < NB:
                A5(s_ - 2)
            if 0 <= s_ - 3 < NB:
                A6(s_ - 3)
            if s_ % 4 == 1 and s_ // 4 < 8:
                ada_group(4 + s_ // 4)
        op("dve", lambda e: e.tensor_copy(out=adacol[:, 16:48], in_=bank(2)[:, 16:48]), reads=[("ps", 2)], writes=["adacol2"])
        op("dve", lambda e: e.tensor_scalar(out=s2col, in0=adacol[:, 32:40], scalar1=1.0, scalar2=None, op0=ALU.add),
           reads=["adacol2"], writes=["s2col"])
        if debug:
            dma("sp", dbg["ada"], adacol, reads=["adacol", "adacol2"], key="dbg1")
            dma("sp", dbg["uT"][:, :, 0:2048], uT_oth, reads=[("uT", g) for g in range(0, NB, 2)], key="dbg6")
            dma("sp", dbg["uT"][:, :, 2048:4096], uT_own, reads=[("uT", g) for g in range(1, NB, 2)], key="dbg7")
        S_.barrier()
        AR.release(mA)

        mF = AR.mark()
        wf_b = AR.alloc(BF16, KC, 8)
        nbf = AR.alloc(F32, 2)
        spf = AR.alloc(F32, S)
        fcn = AR.alloc(F32, S)
        fc3 = AR.alloc(BF16, 3, S)
        etmp = [AR.alloc(F32, 512) for _ in range(2)]
        dma("pool", wf_b, w_in[:, :, OFF_FG:OFF_FG + 8], writes=["wf_b"], key="c5")
        dma("sp", nbf[0:8, 0:1], b_forget, writes=["nbf0"], key="c6")
        op("dve", lambda e: e.tensor_scalar(out=nbf[0:8, 1:2], in0=nbf[0:8, 0:1], scalar1=-1.0, scalar2=None, op0=ALU.mult),
           reads=["nbf0"], writes=["nbf"])
        for pg in range(8):
            bk = pg % 2
            for blk in range(4):
                g = pg * 4 + blk
                for kc in range(KC):
                    op("pe", lambda e, bk=bk, blk=blk, g=g, kc=kc: e.matmul(
                        bank(bk)[0:8, blk * 128:(blk + 1) * 128], wf_b[:, kc, :], uT_blk(g, kc),
                        start=(kc == 0), stop=(kc == KC - 1)),
                       reads=["wf_b", ("uT", g)], writes=[("ps", bk)], inc=(blk == 3 and kc == KC - 1))
            op("act", lambda e, bk=bk: e.activation(out=etmp[bk][0:8, :], in_=bank(bk)[0:8, :], func=AF.Exp,
                                                    bias=nbf[0:8, 1:2], scale=-1.0),
               reads=[("ps", bk), "nbf"], writes=[("etmp", bk)])
            op("act", lambda e, bk=bk, pg=pg: e.activation(out=spf[0:8, pg * 512:(pg + 1) * 512], in_=etmp[bk][0:8, :],
                                                           func=AF.Ln, bias=1.0, scale=1.0),
               reads=[("etmp", bk)], writes=["spf"])
        op("dve", lambda e: e.tensor_tensor_scan(out=fcn[0:8, :], data0=spf[0:8, :], data1=spf[0:8, :], initial=0.0,
                                                 op0=ALU.add, op1=ALU.max),
           reads=["spf"], writes=["fcn"])
        if debug:
            dma("sp", dbg["fcn"], fcn[0:8, :], reads=["fcn"], key="dbg8")
        for g in range(NB):
            op("pe", lambda e, g=g: e.transpose(bank(2)[:, g * 8:(g + 1) * 8], fcn[0:8, g * 128:(g + 1) * 128], ident_f[0:8, 0:8]),
               reads=["fcn", "ident_f"], writes=[("ps", 2)], inc=(g == NB - 1))
        op("dve", lambda e: e.tensor_copy(out=fcncol, in_=bank(2)[:, 0:256]), reads=[("ps", 2)], writes=["fcncol"])
        op("dve", lambda e: e.tensor_copy(out=fc3[0:8, 0, :], in_=fcn[0:8, :]), reads=["fcn"], writes=[("fc3", 0)])
        op("dve", lambda e: e.tensor_tensor(out=spf[0:8, :], in0=fcn[0:8, :], in1=fc3[0:8, 0, :], op=ALU.subtract),
           reads=["fcn", ("fc3", 0), ("ps", 2)], writes=["spf"])
        op("dve", lambda e: e.tensor_copy(out=fc3[0:8, 1, :], in_=spf[0:8, :]), reads=["spf"], writes=[("fc3", 1)])
        op("dve", lambda e: e.tensor_tensor(out=fcn[0:8, :], in0=spf[0:8, :], in1=fc3[0:8, 1, :], op=ALU.subtract),
           reads=["spf", ("fc3", 1), "fcncol"], writes=["fcn"])
        op("dve", lambda e: e.tensor_copy(out=fc3[0:8, 2, :], in_=fcn[0:8, :]), reads=["fcn"], writes=[("fc3", 2)])
        dma("sp", fc_scr, fc3[0:8, :, :], reads=[("fc3", 0), ("fc3", 1), ("fc3", 2)], writes=["fc_scr"], key="c7")
        S_.barrier()
        AR.release(mF)

        wp = [AR.alloc(BF16, KC, 384) for _ in range(2)]
        KT = AR.alloc(BF16, S)
        KTB = AR.alloc(BF16, S)
        QT = AR.alloc(BF16, NOWN * 128)
        QTB = AR.alloc(BF16, NOWN * 128)
        VT = AR.alloc(BF16, NB, 2, 128)
        eb = [[AR.alloc(F32, 512) for _ in range(2)] for _ in range(2)]
        spb = [[AR.alloc(BF16, 512) for _ in range(2)] for _ in range(2)]
        Ab = [[AR.alloc(BF16, 512) for _ in range(2)] for _ in range(2)]
        Rb = [AR.alloc(BF16, 512) for _ in range(2)]
        rden = [AR.alloc(F32, 512) for _ in range(2)]

        def proj_pair(pr):
            fox = pr >= 4
            hp = pr % 4
            base = OFF_FOX if fox else OFF_SB
            w = wp[pr % 2]
            wk = ("wp", pr % 2)
            for i in range(3):
                c0 = base + i * 512 + hp * 128
                dma("pool", w[:, :, i * 128:(i + 1) * 128], w_in[:, :, c0:c0 + 128], writes=[(wk, i)], key=(wk, i))
            for sg in range(8):
                src = uT_oth if sg < 4 else uT_own
                cs = (sg % 4) * 512
                bk = sg % 2
                rk = [("uT", g) for g in range(NB) if sb_of(g) // 4 == sg]
                for kc in range(KC):
                    op("pe", lambda e, bk=bk, kc=kc, src=src, cs=cs: e.matmul(bank(bk), w[:, kc, 128:256], src[:, kc, cs:cs + 512],
                                                                          start=(kc == 0), stop=(kc == KC - 1)),
                       reads=[(wk, 1)] + rk, writes=[("ps", bk)], inc=(kc == KC - 1))
                if not fox:
                    op("dve", lambda e, bk=bk, sg=sg: e.tensor_copy(out=KT[:, sg * 512:(sg + 1) * 512], in_=bank(bk)),
                       reads=[("ps", bk)], writes=[("KT", sg)])
                else:
                    op("dve", lambda e, bk=bk, sg=sg: e.tensor_copy(out=KT[0:64, sg * 512:(sg + 1) * 512], in_=bank(bk)[0:64, :]),
                       reads=[("ps", bk)], writes=[("KT", sg)])
                    op("dve", lambda e, bk=bk, sg=sg: e.tensor_copy(out=KTB[64:128, sg * 512:(sg + 1) * 512], in_=bank(bk)[64:128, :]),
                       reads=[("ps", bk)], writes=[("KTB", sg)])
            for G in range(4):
                bk = G % 2
                rk = [("uT", 2 * j + 1) for j in range(4 * G, 4 * G + 4)]
                for kc in range(KC):
                    op("pe", lambda e, bk=bk, kc=kc, G=G: e.matmul(bank(bk), w[:, kc, 0:128], uT_own[:, kc, G * 512:(G + 1) * 512],
                                                                start=(kc == 0), stop=(kc == KC - 1)),
                       reads=[(wk, 0)] + rk, writes=[("ps", bk)], inc=(kc == KC - 1))
                if not fox:
                    op("dve", lambda e, bk=bk, G=G: e.tensor_scalar(out=QT[:, G * 512:(G + 1) * 512], in0=bank(bk), scalar1=0.125,
                                                                    scalar2=None, op0=ALU.mult),
                       reads=[("ps", bk)], writes=[("QT", G)])
                else:
                    op("dve", lambda e, bk=bk, G=G: e.tensor_scalar(out=QT[0:64, G * 512:(G + 1) * 512], in0=bank(bk)[0:64, :],
                                                                    scalar1=0.125, scalar2=None, op0=ALU.mult),
                       reads=[("ps", bk)], writes=[("QT", G)])
                    op("dve", lambda e, bk=bk, G=G: e.tensor_scalar(out=QTB[64:128, G * 512:(G + 1) * 512], in0=bank(bk)[64:128, :],
                                                                    scalar1=0.125, scalar2=None, op0=ALU.mult),
                       reads=[("ps", bk)], writes=[("QTB", G)])
            for sg in range(8):
                bk = sg % 2
                for blk in range(4):
                    sbk = sg * 4 + blk
                    g = 2 * sbk if sbk < 16 else 2 * (sbk - 16) + 1
                    for kc in range(KC):
                        op("pe", lambda e, bk=bk, blk=blk, g=g, kc=kc: e.matmul(
                            bank(bk)[:, blk * 128:(blk + 1) * 128], uT_blk(g, kc), w[:, kc, 256:384],
                            start=(kc == 0), stop=(kc == KC - 1)),
                           reads=[(wk, 2), ("uT", g)], writes=[("ps", bk)], inc=(blk == 3 and kc == KC - 1))
                if not fox:
                    op("dve", lambda e, bk=bk, sg=sg: e.tensor_copy(
                        out=VT[:, sg * 4:(sg + 1) * 4, 0, :], in_=bank(bk).rearrange("p (a b) -> p a b", a=4, b=128)),
                       reads=[("ps", bk)], writes=[("VT", sg)])
                else:
                    op("dve", lambda e, bk=bk, sg=sg: e.tensor_copy(
                        out=VT[:, sg * 4:(sg + 1) * 4, :, 0:64],
                        in_=bank(bk).rearrange("p (a h d) -> p a h d", a=4, h=2, d=64)),
                       reads=[("ps", bk)], writes=[("VT", sg)])
            if fox:
                for h in range(2):
                    hg = hp * 2 + h
                    dst = QT[64:67, :] if h == 0 else QTB[0:3, :]
                    srcap = fc_scr[hg].rearrange("r (j two c) -> r j two c", two=2, c=128)[:, :, 1, :]
                    dma("sp", dst.rearrange("r (j c) -> r j c", c=128), srcap, reads=["fc_scr"],
                        writes=[("QTaug", h)] + [(("QT" if h == 0 else "QTB"), G_) for G_ in range(4)], key=("qaug", h))

        def grp_cols(G, g):
            d = g - 8 * G
            return (d // 2) * 128 if d >= 0 else 0

        hcount = [0, 0]

        def attn_sb(pr):
            steps = []
            for G in range(4):
                gs = list(range(8 * G + 7, -1, -1))
                for it, g in enumerate(gs):
                    for h in range(2):
                        par = hcount[h] % 2
                        hcount[h] += 1
                        steps.append(dict(G=G, it=it, g=g, h=h, par=par, last=(g == 0), c0=grp_cols(G, g), b=sb_of(g)))
            N = len(steps)

            def bufs(st):
                h, par = st["h"], st["par"]
                return bank(2 + 2 * h + par), ("ps", 2 + 2 * h + par), bank(6 + h), ("ps", 6 + h)

            def QK(st):
                G, g, h, par, c0, b = st["G"], st["g"], st["h"], st["par"], st["c0"], st["b"]
                zb, zk, ob, ok_ = bufs(st)
                d = g - 8 * G
                rs = slice(h * 64, (h + 1) * 64)
                qsl = slice(G * 512 + c0, G * 512 + 512)
                has_tri = d >= 1 and d % 2 == 1
                has_dm = g == 0
                op("pe", lambda e, st_=not (has_tri or has_dm): e.matmul(
                    zb[:, c0:512], KT[rs, b * 128:(b + 1) * 128], QT[rs, qsl], start=True, stop=st_),
                   reads=[("KT", b // 4), ("QT", G)], writes=[zk], inc=not (has_tri or has_dm))
                if has_tri:
                    op("pe", lambda e: e.matmul(zb[:, c0:c0 + 128], ident_b, triS_b, start=False, stop=not has_dm),
                       reads=["cst_b"], writes=[zk], inc=not has_dm)
                if has_dm:
                    op("pe", lambda e: e.matmul(zb[:, c0:512], ident_b, dm_b[:, c0:512], start=False, stop=True),
                       reads=["cst_b"], writes=[zk])

            def EL(st):
                h, par, c0 = st["h"], st["par"], st["c0"]
                zb, zk, ob, ok_ = bufs(st)
                e_, sp_ = eb[h][par], spb[h][par]
                op("act", lambda e: e.activation(out=e_[:, c0:512], in_=zb[:, c0:512], func=AF.Exp),
                   reads=[zk], writes=[("e", h, par)])
                op("act", lambda e: e.activation(out=sp_[:, c0:512], in_=e_[:, c0:512], func=AF.Ln, bias=1.0, scale=1.0),
                   reads=[("e", h, par)], writes=[("sp", h, par)])

            def CUM(st):
                h, par, c0, it = st["h"], st["par"], st["c0"], st["it"]
                zb, zk, ob, ok_ = bufs(st)
                sp_ = spb[h][par]
                op("pe", lambda e: e.matmul(zb[:, c0:512], negU_b, sp_[:, c0:512], start=False, stop=True, skip_group_check=True),
                   reads=["cst_b", ("sp", h, par)], writes=[zk], inc=(it == 0))
                if it > 0:
                    op("pe", lambda e: e.matmul(zb[:, c0:512], negones_b, Rb[h][:, c0:512], start=False, stop=True,
                                                skip_group_check=True),
                       reads=["negones", ("R", h)], writes=[zk])

            def AR_(st):
                h, par, c0, it = st["h"], st["par"], st["c0"], st["it"]
                zb, zk, ob, ok_ = bufs(st)
                A_, sp_ = Ab[h][par], spb[h][par]
                op("act", lambda e: e.activation(out=A_[:, c0:512], in_=zb[:, c0:512], func=AF.Exp),
                   reads=[zk], writes=[("A", h, par)])
                if it == 0:
                    op("pool", lambda e: e.memset(Rb[h], 0.0), writes=[("R", h)])
                if not st["last"]:
                    op("pool", lambda e: e.tensor_tensor(out=Rb[h][:, c0:512], in0=Rb[h][:, c0:512], in1=sp_[:, c0:512], op=ALU.add),
                       reads=[("R", h), ("sp", h, par)], writes=[("R", h)])

            def AV(st):
                G, h, par, c0, it, b = st["G"], st["h"], st["par"], st["c0"], st["it"], st["b"]
                zb, zk, ob, ok_ = bufs(st)
                A_ = Ab[h][par]
                op("pe", lambda e: e.matmul(ob[0:64, c0:512], VT[:, b, 0, h * 64:(h + 1) * 64], A_[:, c0:512],
                                            start=(it == 0), stop=st["last"], skip_group_check=True),
                   reads=[("VT", b // 4), ("A", h, par)], writes=[ok_], inc=True)
                if st["last"]:
                    op("act", lambda e: e.activation(out=attnT[h * 64:(h + 1) * 64, pr, G * 512:(G + 1) * 512],
                                                     in_=ob[0:64, :], func=AF.Copy),
                       reads=[ok_], writes=[("attnT", pr, G)])

            for s in range(-2, N + 2):
                if 0 <= s - 1 < N:
                    CUM(steps[s - 1])
                if 0 <= s - 2 < N:
                    AV(steps[s - 2])
                if 0 <= s + 2 < N:
                    QK(steps[s + 2])
                if 0 <= s < N:
                    EL(steps[s])
                if 0 <= s - 1 < N:
                    AR_(steps[s - 1])

        def attn_fox(pr):
            hp = pr % 4
            steps = []
            for G in range(4):
                gs = list(range(8 * G + 7, -1, -1))
                for it, g in enumerate(gs):
                    for h in range(2):
                        par = hcount[h] % 2
                        hcount[h] += 1
                        steps.append(dict(G=G, it=it, g=g, h=h, par=par, last=(g == 0), c0=grp_cols(G, g), b=sb_of(g)))
            N = len(steps)

            def bufs(st):
                h, par = st["h"], st["par"]
                return bank(2 + 2 * h + par), ("ps", 2 + 2 * h + par), bank(6 + h), ("ps", 6 + h)

            def QK(st):
                G, g, h, par, c0, b = st["G"], st["g"], st["h"], st["par"], st["c0"], st["b"]
                zb, zk, ob, ok_ = bufs(st)
                d = g - 8 * G
                qsl = slice(G * 512 + c0, G * 512 + 512)
                has_tri = d >= 1 and d % 2 == 1
                has_dm = g == 0
                if h == 0:
                    kap, qap = KT[0:67, b * 128:(b + 1) * 128], QT[0:67, qsl]
                    rds = [("KT", b // 4), ("QT", G), ("QTaug", 0), "kaug"]
                else:
                    kap, qap = KTB[:, b * 128:(b + 1) * 128], QTB[:, qsl]
                    rds = [("KTB", b // 4), ("QTB", G), ("QTaug", 1), "kaug"]
                op("pe", lambda e, st_=not (has_tri or has_dm): e.matmul(zb[:, c0:512], kap, qap, start=True, stop=st_),
                   reads=rds, writes=[zk], inc=not (has_tri or has_dm))
                if has_tri:
                    op("pe", lambda e: e.matmul(zb[:, c0:c0 + 128], ident_b, triI_b, start=False, stop=not has_dm),
                       reads=["cst_b"], writes=[zk], inc=not has_dm)
                if has_dm:
                    op("pe", lambda e: e.matmul(zb[:, c0:512], ident_b, dm_b[:, c0:512], start=False, stop=True),
                       reads=["cst_b"], writes=[zk])

            def PX(st):
                g, h, par, c0 = st["g"], st["h"], st["par"], st["c0"]
                zb, zk, ob, ok_ = bufs(st)
                hg = hp * 2 + h
                A_ = Ab[h][par]
                op("act", lambda e: e.activation(out=A_[:, c0:512], in_=zb[:, c0:512], func=AF.Exp,
                                                 bias=fcncol[:, g * 8 + hg:g * 8 + hg + 1], scale=1.0),
                   reads=[zk, "fcncol"], writes=[("A", h, par)])

            def AV(st):
                G, h, par, c0, it, b = st["G"], st["h"], st["par"], st["c0"], st["it"], st["b"]
                zb, zk, ob, ok_ = bufs(st)
                A_ = Ab[h][par]
                op("pe", lambda e: e.matmul(ob[:, c0:512], VT[:, b, h, :], A_[:, c0:512], start=(it == 0), stop=st["last"],
                                            skip_group_check=True),
                   reads=[("VT", b // 4), ("A", h, par), "vones"], writes=[ok_], inc=True)
                if st["last"]:
                    op("dve", lambda e: e.reciprocal(out=rden[h][64:128, :], in_=ob[64:128, :]),
                       reads=[ok_], writes=[("rden", h)])
                    op("dve", lambda e: e.tensor_tensor(out=attnT[h * 64:(h + 1) * 64, pr, G * 512:(G + 1) * 512],
                                                        in0=ob[0:64, :], in1=rden[h][64:128, :], op=ALU.mult),
                       reads=[ok_, ("rden", h)], writes=[("attnT", pr, G)])

            for s in range(-2, N + 2):
                if 0 <= s - 2 < N:
                    AV(steps[s - 2])
                if 0 <= s + 2 < N:
                    QK(steps[s + 2])
                if 0 <= s < N:
                    PX(steps[s])

        for pr in range(8):
            if pr == 4:
                op("pool", lambda e: e.memset(KT[64:67, :], -1.0), writes=[("KT", i) for i in range(8)] + ["kaug"])
                op("pool", lambda e: e.memset(KTB[0:64, :], 0.0), writes=["kaug0"])
                op("pool", lambda e: e.memset(KTB[0:3, :], -1.0), reads=["kaug0"], writes=["kaug"])
                op("pool", lambda e: e.memset(QTB[0:64, :], 0.0), writes=[("QTaug", 1)])
                op("pool", lambda e: e.memset(VT[:, :, :, 64:128], 1.0), writes=[("VT", i) for i in range(8)] + ["vones"])
            proj_pair(pr)
            if pr < 4:
                attn_sb(pr)
            else:
                attn_fox(pr)
        if debug:
            dma("sp", dbg["attnT"], attnT, reads=[("attnT", pr, G) for pr in range(8) for G in range(4)], key="dbg9")
        S_.barrier()
        AR.release(m_att)

        mC1 = AR.mark()
        wbg = AR.alloc(BF16, KC, 2 * D)
        wsb_b = AR.alloc(BF16, 4, D)
        wfx_b = AR.alloc(BF16, 4, D)
        wo_b = AR.alloc(BF16, KC, D)
        lnB1 = AR.alloc(F32, 2, D)
        gT = [[AR.alloc(BF16, 512) for _ in range(2)] for _ in range(2)]
        tmpa = [AR.alloc(F32, 512) for _ in range(2)]
        tmpb = [AR.alloc(F32, 512) for _ in range(2)]
        mT = AR.alloc(BF16, KC, 512)
        xinB = [AR.alloc(F32, D) for _ in range(2)]
        r1 = [AR.alloc(F32, D) for _ in range(2)]
        x1b = [AR.alloc(F32, D) for _ in range(2)]
        sttB = [AR.alloc(F32, 16) for _ in range(2)]
        smlB = [AR.alloc(F32, 4) for _ in range(2)]

        for hf in range(2):
            dma("pool", wbg[:, :, hf * D:(hf + 1) * D], w_in[:, :, OFF_BG + hf * D:OFF_BG + (hf + 1) * D], writes=["wbg"], key="w0")
        dma("pool", wsb_b, w_sb, writes=["wsb"], key="w1")
        dma("pool", wfx_b, w_fx, writes=["wfx"], key="w2")
        for i in range(2):
            dma("sp", lnB1[:, i, :], lnp[i:i + 1, :].partition_broadcast(128), writes=[("lnB1", i)], key=("lnB1", i))
        for kc in range(KC):
            st_ = xinB[kc % 2]
            dma("sp", st_, w_o[:, kc, :], writes=[("xinB", kc % 2)], key=("xinB", kc % 2))
            op("pool", lambda e, st_=st_, kc=kc: e.tensor_tensor(out=wo_b[:, kc, :], in0=st_, in1=GB[:, 0, :], op=ALU.mult),
               reads=[("xinB", kc % 2), ("GB", 0, 0), ("GB", 0, 1)], writes=[("wo", kc)])

        def own_rows(j):
            g = 2 * j + 1
            return slice(g * 128, (g + 1) * 128)

        for G in range(4):
            ukeys = [("uT", 2 * j + 1) for j in range(4 * G, 4 * G + 4)]
            for n in range(8):
                par = n % 2
                for w_ in range(2):
                    bk = w_ * 2 + par
                    col = w_ * D + n * 128
                    for kc in range(KC):
                        op("pe", lambda e, bk=bk, kc=kc, col=col, G=G: e.matmul(bank(bk), wbg[:, kc, col:col + 128],
                                                                             uT_own[:, kc, G * 512:(G + 1) * 512],
                                                                             start=(kc == 0), stop=(kc == KC - 1)),
                           reads=["wbg"] + ukeys, writes=[("ps", bk)], inc=(kc == KC - 1))
                    op("act", lambda e, bk=bk, w_=w_, par=par, n=n: e.activation(out=gT[w_][par], in_=bank(bk), func=AF.Sigmoid,
                                                                                bias=bgcol[:, w_ * 8 + n:w_ * 8 + n + 1], scale=1.0),
                       reads=[("ps", bk), "bgcol"], writes=[("gT", w_, par)])
                for w_, wt in enumerate((wsb_b, wfx_b)):
                    bk = 4 + w_ * 2 + par
                    for kc in range(4):
                        op("pe", lambda e, bk=bk, kc=kc, wt=wt, w_=w_, n=n, G=G: e.matmul(
                            bank(bk), wt[:, kc, n * 128:(n + 1) * 128], attnT[:, w_ * 4 + kc, G * 512:(G + 1) * 512],
                            start=(kc == 0), stop=(kc == 3)),
                           reads=["wsb", "wfx"] + [("attnT", w_ * 4 + kc, G)], writes=[("ps", bk)], inc=(kc == 3))
                op("dve", lambda e, par=par: e.tensor_tensor(out=tmpa[par], in0=bank(4 + par), in1=gT[0][par], op=ALU.mult),
                   reads=[("ps", 4 + par), ("gT", 0, par)], writes=[("tmpa", par)])
                op("dve", lambda e, par=par: e.tensor_tensor(out=tmpb[par], in0=bank(6 + par), in1=gT[1][par], op=ALU.mult),
                   reads=[("ps", 6 + par), ("gT", 1, par)], writes=[("tmpb", par)])
                op("pool", lambda e, par=par, n=n: e.tensor_tensor(out=mT[:, n, :], in0=tmpa[par], in1=tmpb[par], op=ALU.add),
                   reads=[("tmpa", par), ("tmpb", par)], writes=[("mT", n)])
            for i in range(4):
                j = 4 * G + i
                k = j % 2
                dma("sp", xinB[k], xall[own_rows(j), :], writes=[("xinB", k)], key=("xinB", k))
                for hf in range(2):
                    bk = hf
                    for kc in range(KC):
                        op("pe", lambda e, bk=bk, kc=kc, i=i, hf=hf: e.matmul(bank(bk), mT[:, kc, i * 128:(i + 1) * 128],
                                                                           wo_b[:, kc, hf * 512:(hf + 1) * 512],
                                                                           start=(kc == 0), stop=(kc == KC - 1)),
                           reads=[("mT", kc), ("wo", kc)], writes=[("ps", bk)], inc=(kc == KC - 1))
                    op("dve", lambda e, bk=bk, k=k, hf=hf: e.scalar_tensor_tensor(
                        out=r1[k][:, hf * 512:(hf + 1) * 512], in0=xinB[k][:, hf * 512:(hf + 1) * 512], scalar=ALPHA,
                        in1=bank(bk), op0=ALU.mult, op1=ALU.add),
                       reads=[("xinB", k), ("ps", bk)], writes=[("r1", k, hf)])
                st, sm = sttB[k], smlB[k]

                def ln_stats2(src, rks, k=k, st=st, sm=sm):
                    op("dve", lambda e: e.bn_stats(out=st[:, 0:6], in_=src[:, 0:512]), reads=rks, writes=[("st", k, 0)])
                    op("dve", lambda e: e.bn_stats(out=st[:, 6:12], in_=src[:, 512:1024]), reads=rks, writes=[("st", k, 1)])
                    op("dve", lambda e: e.bn_aggr(out=st[:, 12:14], in_=st[:, 0:12]), reads=[("st", k, 0), ("st", k, 1)],
                       writes=[("mv", k)])
                    op("act", lambda e: e.activation(out=sm[:, 0:1], in_=st[:, 13:14], func=AF.Sqrt, bias=epsc[:, 0:1], scale=1.0),
                       reads=[("mv", k), "epsc"], writes=[("sd", k)])
                    op("dve", lambda e: e.reciprocal(out=sm[:, 1:2], in_=sm[:, 0:1]), reads=[("sd", k)], writes=[("rstd", k)])
                    op("dve", lambda e: e.tensor_scalar(out=sm[:, 2:3], in0=st[:, 12:13], scalar1=-1.0, scalar2=sm[:, 1:2],
                                                        op0=ALU.mult, op1=ALU.mult),
                       reads=[("mv", k), ("rstd", k)], writes=[("nmr", k)])

                ln_stats2(r1[k], [("r1", k, 0), ("r1", k, 1)])
                op("act", lambda e, k=k: e.activation(out=r1[k], in_=r1[k], func=AF.Identity, bias=smlB[k][:, 2:3], scale=smlB[k][:, 1:2]),
                   reads=[("r1", k, 0), ("r1", k, 1), ("rstd", k), ("nmr", k)], writes=[("r1", k, 0), ("r1", k, 1)])
                op("pool", lambda e, k=k: e.tensor_tensor(out=x1b[k], in0=r1[k], in1=lnB1[:, 0, :], op=ALU.mult),
                   reads=[("r1", k, 0), ("r1", k, 1), ("lnB1", 0)], writes=[("x1", k)])
                op("pool", lambda e, k=k: e.tensor_tensor(out=x1b[k], in0=x1b[k], in1=lnB1[:, 1, :], op=ALU.add),
                   reads=[("x1", k), ("lnB1", 1)], writes=[("x1", k)])
                dma("sp", x1_scr[j * 128:(j + 1) * 128, :], x1b[k], reads=[("x1", k)], writes=[("x1scr", j)], key=("x1o", k))
                if debug:
                    dma("sp", dbg["x1"][j * 128:(j + 1) * 128, :], x1b[k], reads=[("x1", k)], key=("dbgx1", k))
                ln_stats2(x1b[k], [("x1", k)])
                op("act", lambda e, k=k: e.activation(out=r1[k], in_=x1b[k], func=AF.Identity, bias=smlB[k][:, 2:3], scale=smlB[k][:, 1:2]),
                   reads=[("x1", k), ("rstd", k), ("nmr", k)], writes=[("r1", k, 0), ("r1", k, 1)])
                for kc in range(KC):
                    bk = 2 + kc // 4
                    op("pe", lambda e, bk=bk, kc=kc, k=k: e.transpose(bank(bk)[:, (kc % 4) * 128:(kc % 4 + 1) * 128],
                                                                  r1[k][:, kc * 128:(kc + 1) * 128], ident_f),
                       reads=[("r1", k, 0), ("r1", k, 1), "ident_f"], writes=[("ps", bk)], inc=(kc % 4 == 3))
                for kc in range(KC):
                    bk = 2 + kc // 4
                    op("dve", lambda e, bk=bk, kc=kc, j=j: e.tensor_scalar(
                        out=uT_own[:, kc, j * 128:(j + 1) * 128], in0=bank(bk)[:, (kc % 4) * 128:(kc % 4 + 1) * 128],
                        scalar1=s2col[:, kc:kc + 1], scalar2=adacol[:, 24 + kc:25 + kc], op0=ALU.mult, op1=ALU.add),
                       reads=[("ps", bk), "s2col", "adacol2"], writes=[("uT", 2 * j + 1)])
        S_.barrier()
        AR.release(m_attnT)

        wg_b = AR.alloc(BF16, KC, DFF)
        wu_b = AR.alloc(BF16, KC, DFF)
        wd_b = AR.alloc(BF16, FC, D)
        hT = AR.alloc(BF16, FC, 256)
        sgt = [AR.alloc(F32, 256) for _ in range(2)]
        xin = [AR.alloc(F32, D) for _ in range(2)]
        r2 = xin
        lnB = GB
        stt = [AR.alloc(F32, 16) for _ in range(2)]
        sml = [AR.alloc(F32, 4) for _ in range(2)]
        for kc in range(KC):
            for hf in range(2):
                dma("pool", wg_b[:, kc, hf * 1408:(hf + 1) * 1408], w_g[:, kc, hf * 1408:(hf + 1) * 1408], writes=[("wg", kc)], key="w3")
                dma("pool", wu_b[:, kc, hf * 1408:(hf + 1) * 1408], w_u[:, kc, hf * 1408:(hf + 1) * 1408], writes=[("wu", kc)], key="w4")
        S_.group_final("w3", [("wg", kc) for kc in range(KC)])
        S_.group_final("w4", [("wu", kc) for kc in range(KC)])
        for fc in range(FC):
            st_ = xin[fc % 2]
            dma("sp", st_, w_d[:, fc, :], writes=[("xin2", fc % 2)], key=("xin", fc % 2))
            op("pool", lambda e, st_=st_, fc=fc: e.tensor_tensor(out=wd_b[:, fc, :], in0=st_, in1=GB[:, 1, :], op=ALU.mult),
               reads=[("xin2", fc % 2), ("GB", 1, 0), ("GB", 1, 1)], writes=[("wd", fc)])
        for i in range(2):
            dma("sp", lnB[:, i, :], lnp[2 + i:3 + i, :].partition_broadcast(128), writes=[("GB", i, 0), ("GB", i, 1)],
                key=("lnB", i))
        wgk = [("wg", kc) for kc in range(KC)]
        wuk = [("wu", kc) for kc in range(KC)]
        for sub in range(8):
            cs = sub * 256
            ukeys = [("uT", 2 * j + 1) for j in (2 * sub, 2 * sub + 1)]
            for fc in range(FC):
                par = fc % 2
                bg, bu = bank(par), bank(2 + par)
                for kc in range(KC):
                    op("pe", lambda e, bg=bg, kc=kc, fc=fc, cs=cs: e.matmul(bg[:, 0:256], wg_b[:, kc, fc * 128:(fc + 1) * 128],
                                                                       uT_own[:, kc, cs:cs + 256], start=(kc == 0), stop=(kc == KC - 1)),
                       reads=wgk + ukeys, writes=[("ps", par)], inc=(kc == KC - 1))
                for kc in range(KC):
                    op("pe", lambda e, bu=bu, kc=kc, fc=fc, cs=cs: e.matmul(bu[:, 0:256], wu_b[:, kc, fc * 128:(fc + 1) * 128],
                                                                       uT_own[:, kc, cs:cs + 256], start=(kc == 0), stop=(kc == KC - 1)),
                       reads=wuk + ukeys, writes=[("ps", 2 + par)], inc=(kc == KC - 1))
                op("act", lambda e, bg=bg, par=par: e.activation(out=sgt[par], in_=bg[:, 0:256], func=AF.Silu),
                   reads=[("ps", par)], writes=[("sgt", par)])
                op("dve", lambda e, bu=bu, par=par, fc=fc: e.tensor_tensor(out=hT[:, fc, :], in0=bu[:, 0:256], in1=sgt[par], op=ALU.mult),
                   reads=[("ps", 2 + par), ("sgt", par)], writes=[("hT", fc)])
            for i in range(2):
                j = 2 * sub + i
                k = j % 2
                dma("sp", xin[k], x1_scr[j * 128:(j + 1) * 128, :], reads=[("x1scr", j)], writes=[("xin2", k)], key=("xin", k))
                for hf in range(2):
                    bk = 4 + 2 * k + hf
                    for fc in range(FC):
                        op("pe", lambda e, bk=bk, fc=fc, i=i, hf=hf: e.matmul(bank(bk), hT[:, fc, i * 128:(i + 1) * 128],
                                                                           wd_b[:, fc, hf * 512:(hf + 1) * 512],
                                                                           start=(fc == 0), stop=(fc == FC - 1)),
                           reads=[("hT", fc), ("wd", fc)], writes=[("ps", bk)], inc=(fc == FC - 1))
                    op("dve", lambda e, bk=bk, k=k, hf=hf: e.scalar_tensor_tensor(
                        out=r2[k][:, hf * 512:(hf + 1) * 512], in0=xin[k][:, hf * 512:(hf + 1) * 512], scalar=ALPHA,
                        in1=bank(bk), op0=ALU.mult, op1=ALU.add),
                       reads=[("xin2", k), ("ps", bk)], writes=[("xin2", k)])
                st, sm = stt[k], sml[k]
                op("dve", lambda e, st=st, k=k: e.bn_stats(out=st[:, 0:6], in_=r2[k][:, 0:512]), reads=[("xin2", k)], writes=[("st", k, 0)])
                op("dve", lambda e, st=st, k=k: e.bn_stats(out=st[:, 6:12], in_=r2[k][:, 512:1024]), reads=[("xin2", k)], writes=[("st", k, 1)])
                op("dve", lambda e, st=st: e.bn_aggr(out=st[:, 12:14], in_=st[:, 0:12]), reads=[("st", k, 0), ("st", k, 1)],
                   writes=[("mv", k)])
                op("act", lambda e, st=st, sm=sm: e.activation(out=sm[:, 0:1], in_=st[:, 13:14], func=AF.Sqrt, bias=epsc[:, 0:1], scale=1.0),
                   reads=[("mv", k), "epsc"], writes=[("sd", k)])
                op("dve", lambda e, sm=sm: e.reciprocal(out=sm[:, 1:2], in_=sm[:, 0:1]), reads=[("sd", k)], writes=[("rstd", k)])
                op("dve", lambda e, st=st, sm=sm: e.tensor_scalar(out=sm[:, 2:3], in0=st[:, 12:13], scalar1=-1.0, scalar2=sm[:, 1:2],
                                                                  op0=ALU.mult, op1=ALU.mult),
                   reads=[("mv", k), ("rstd", k)], writes=[("nmr", k)])
                op("act", lambda e, k=k, sm=sm: e.activation(out=r2[k], in_=r2[k], func=AF.Identity, bias=sm[:, 2:3], scale=sm[:, 1:2]),
                   reads=[("xin2", k), ("rstd", k), ("nmr", k)], writes=[("xin2", k)])
                op("pool", lambda e, k=k: e.tensor_tensor(out=r2[k], in0=r2[k], in1=lnB[:, 0, :], op=ALU.mult),
                   reads=[("xin2", k), ("GB", 0, 0), ("GB", 0, 1)], writes=[("xin2", k)])
                op("pool", lambda e, k=k: e.tensor_tensor(out=r2[k], in0=r2[k], in1=lnB[:, 1, :], op=ALU.add),
                   reads=[("xin2", k), ("GB", 1, 0), ("GB", 1, 1)], writes=[("xin2", k)])
                dma("sp", out[j * 128:(j + 1) * 128, :], r2[k], reads=[("xin2", k)], writes=[("out", j)], key=("outd", k))
        S_.wait_all("sp", [("outd", 0), ("outd", 1)] + ([k for k in S_.dma_cnt if str(k).startswith("dbg") or (isinstance(k, tuple) and k[0] == "dbgx1")] if debug else []))
        print("arena peak bytes", AR.peak, "pe", S_.cnt["pe"], "act", S_.cnt["act"], "dve", S_.cnt["dve"], "pool", S_.cnt["pool"],
              "instr", {k: len(v) for k, v in S_.prog.items()})

        with nc.Block() as block:
            @block.tensor
            def _(e):
                S_.emit("pe", e)

            @block.scalar
            def _(e):
                S_.emit("act", e)

            @block.vector
            def _(e):
                S_.emit("dve", e)

            @block.gpsimd
            def _(e):
                S_.emit("pool", e)

            @block.sync
            def _(e):
                S_.emit("sp", e)
    return nc


_CONSTS = None


def _consts():
    global _CONSTS
    if _CONSTS is None:
        s = np.arange(128)[:, None]
        t = np.arange(128)[None, :]
        ident = (s == t).astype(np.float32)
        triS = np.where(s < t, 0.0, NEG).astype(np.float32)
        triI = np.where(s <= t, 0.0, NEG).astype(np.float32)
        negU = np.where(s >= t, -1.0, 0.0).astype(np.float32)
        _CONSTS = np.concatenate([ident, triS, triI, negU], axis=1)
    return _CONSTS


def _rk(w, kc):
    n = w.shape[1]
    return np.ascontiguousarray(w.reshape(kc, 128, n).transpose(1, 0, 2))


def make_in_maps(x, c, w_ada, b_ada, w_in, b_gate, b_forget, w_sb_out, w_fox_out, w_o,
                 ln1_g, ln1_b, w_ffn_gate, w_ffn_up, w_ffn_down, ln2_g, ln2_b):
    f = np.float32
    x = np.asarray(x, f)
    shared = {
        "w_ada": _rk(np.asarray(w_ada, f)[0], KC),
        "b_ada": np.ascontiguousarray(np.asarray(b_ada, f)[0][None, :]),
        "w_in": _rk(np.asarray(w_in, f)[0], KC),
        "b_gate": np.ascontiguousarray(np.asarray(b_gate, f)[0].reshape(16, 128).T),
        "b_forget": np.ascontiguousarray(np.asarray(b_forget, f)[0].reshape(8, 1)),
        "w_sb": _rk(np.asarray(w_sb_out, f)[0], 4),
        "w_fx": _rk(np.asarray(w_fox_out, f)[0], 4),
        "w_o": _rk(np.asarray(w_o, f)[0], KC),
        "lnp": np.ascontiguousarray(np.stack([np.asarray(a, f)[0] for a in (ln1_g, ln1_b, ln2_g, ln2_b)])),
        "w_g": _rk(np.asarray(w_ffn_gate, f)[0], KC),
        "w_u": _rk(np.asarray(w_ffn_up, f)[0], KC),
        "w_d": _rk(np.asarray(w_ffn_down, f)[0], FC),
    }
    cb = _consts()
    maps = []
    for core in range(8):
        b, p = core // 2, core % 2
        if p == 1:
            xa = x[b]
        else:
            xa = np.concatenate([np.zeros((128, D), f), x[b][:S - 128]], axis=0)
        dmask = np.full((128, 512), NEG if p == 0 else 0.0, f)
        m = dict(shared)
        m["xall"] = np.ascontiguousarray(xa)
        m["c_col"] = np.ascontiguousarray(np.asarray(c, f)[b].reshape(KC, 128).T)
        m["consts"] = np.ascontiguousarray(np.concatenate([cb, dmask], axis=1))
        maps.append(m)
    return maps


def assemble(results):
    out = np.zeros((4, S, D), np.float32)
    for core in range(8):
        b, p = core // 2, core % 2
        o = np.asarray(results[core]["out"], np.float32).reshape(NOWN, 128, D)
        ov = out[b].reshape(NB, 128, D)
        for j in range(NOWN):
            ov[2 * j + p] = o[j]
    return out


def kernel(**inputs):
    nc = build_nc(debug=False)
    maps = make_in_maps(**inputs)
    res = run_bass_kernel_spmd(nc, maps, core_ids=list(range(8)))
    return assemble(res.results)
```

```python
import numpy as np
from contextlib import ExitStack
import concourse.bass as bass
import concourse.mybir as mybir
from concourse.bass_utils import run_bass_kernel_spmd

F32 = mybir.dt.float32
BF16 = mybir.dt.bfloat16
AF = mybir.ActivationFunctionType
ALU = mybir.AluOpType

D = 1024
KC = 8
S = 4096
NB = 32
NOWN = 16
DFF = 2816
FC = 22
IN_COLS = 5128
OFF_SB = 0
OFF_FOX = 1536
OFF_FG = 3072
OFF_BG = 3080
LN_EPS = 1e-5
ALPHA = 2.0 ** 0.25
NEG = -30000.0

COMPUTE = ("pe", "act", "dve", "pool")


class Sched:
    def __init__(self, nc, stack, n_dma_sems=64):
        self.nc = nc
        self.prog = {e: [] for e in ("pe", "act", "dve", "pool", "sp")}
        self.cnt = {e: 0 for e in COMPUTE}
        self.sem = {e: stack.enter_context(nc.semaphore("s_" + e)) for e in COMPUTE}
        self.dma_pool = [stack.enter_context(nc.semaphore("d%d" % i)) for i in range(n_dma_sems)]
        self.dma_key = {}
        self.dma_cnt = {}
        self.known = {e: {} for e in self.prog}
        self.lastw = {}
        self.reads = {}

    def _semh(self, k):
        return self.sem[k] if k in self.sem else self.dma_pool[self.dma_key[k]]

    def _deps(self, eng, reads, writes):
        deps = set()
        for b in reads:
            ev = self.lastw.get(b)
            if ev:
                deps.add(ev)
        for b in writes:
            ev = self.lastw.get(b)
            if ev and not (ev[0] == eng and eng in COMPUTE):
                deps.add(ev)
            for ev in self.reads.get(b, ()):
                if not (ev[0] == eng and eng in COMPUTE):
                    deps.add(ev)
        waits = {}
        kn = self.known[eng]
        for (k, v) in deps:
            if k == eng and eng == "pe":
                continue
            if kn.get(k, 0) >= v:
                continue
            if waits.get(k, 0) < v:
                waits[k] = v
        for k, v in waits.items():
            kn[k] = v
        return [(self._semh(k), v) for k, v in waits.items()]

    def _record(self, ev, reads, writes):
        for b in reads:
            self.reads.setdefault(b, []).append(ev)
        for b in writes:
            self.lastw[b] = ev
            self.reads[b] = []

    def op(self, eng, fn, reads=(), writes=(), inc=True):
        waits = self._deps(eng, reads, writes)
        if inc:
            self.cnt[eng] += 1
            ev = (eng, self.cnt[eng])
        else:
            ev = (eng, self.cnt[eng] + 1)
        self._record(ev, reads, writes)
        self.prog[eng].append((waits, fn, (self.sem[eng], 1) if inc else None))

    def dma(self, queue, out, in_, reads=(), writes=(), key=None, **kw):
        assert key is not None
        if key not in self.dma_key:
            idx = len(self.dma_key)
            assert idx < len(self.dma_pool), "out of dma semaphores"
            self.dma_key[key] = idx
            self.dma_cnt[key] = 0
        waits = self._deps(queue, reads, writes)
        self.dma_cnt[key] += 16
        ev = (key, self.dma_cnt[key])
        self._record(ev, reads, writes)
        fn = lambda e, out=out, in_=in_, kw=kw: e.dma_start(out=out, in_=in_, **kw)
        self.prog[queue].append((waits, fn, (self._semh(key), 16)))

    def group_final(self, key, bufs):
        for b in bufs:
            self.lastw[b] = (key, self.dma_cnt[key])

    def wait_all(self, eng, keys):
        waits = []
        for k in keys:
            v = self.dma_cnt[k] if k in self.dma_cnt else self.cnt[k]
            if v > 0:
                waits.append((self._semh(k), v))
        self.prog[eng].append((waits, None, None))

    def barrier(self):
        keys = list(COMPUTE) + list(self.dma_cnt.keys())
        for eng in self.prog:
            waits = []
            for k in keys:
                v = self.dma_cnt[k] if k in self.dma_cnt else self.cnt[k]
                if k == eng or v == 0:
                    continue
                if self.known[eng].get(k, 0) >= v:
                    continue
                self.known[eng][k] = v
                waits.append((self._semh(k), v))
            if waits:
                self.prog[eng].append((waits, None, None))

    def emit(self, eng, e):
        for waits, fn, inc in self.prog[eng]:
            for (h, v) in waits:
                e.wait_ge(h, v)
            if fn is None:
                continue
            ins = fn(e)
            if inc is not None:
                ins.then_inc(inc[0], inc[1])


class Arena:
    def __init__(self, nc, stack, nbytes):
        self.h32 = stack.enter_context(nc.sbuf_tensor("arena", [128, nbytes // 4], F32))
        self.h16 = self.h32.bitcast(BF16)
        self.top = 0
        self.limit = nbytes
        self.peak = 0

    def alloc(self, dtype, *free):
        n = 1
        for f in free:
            n *= f
        sz = 4 if dtype == F32 else 2
        nb = (n * sz + 63) // 64 * 64
        off = self.top
        self.top += nb
        self.peak = max(self.peak, self.top)
        assert self.top <= self.limit, "arena overflow %d > %d" % (self.top, self.limit)
        base = self.h32 if dtype == F32 else self.h16
        e0 = off // sz
        ap = base[:, e0:e0 + n]
        if len(free) == 2:
            ap = ap.rearrange("p (a b) -> p a b", a=free[0], b=free[1])
        elif len(free) == 3:
            ap = ap.rearrange("p (a b c) -> p a b c", a=free[0], b=free[1], c=free[2])
        return ap

    def mark(self):
        return self.top

    def release(self, m):
        self.top = m


def sb_of(g):
    return g // 2 if g % 2 == 0 else 16 + g // 2


def build_nc(debug=False):
    nc = bass.Bass("TRN2", target_bir_lowering=False)
    dt = nc.dram_tensor
    xall = dt("xall", [S, D], F32, kind="ExternalInput").ap()
    c_col = dt("c_col", [128, KC], F32, kind="ExternalInput").ap()
    w_ada = dt("w_ada", [128, KC, 6 * D], F32, kind="ExternalInput").ap()
    b_ada = dt("b_ada", [1, 6 * D], F32, kind="ExternalInput").ap()
    w_in = dt("w_in", [128, KC, IN_COLS], F32, kind="ExternalInput").ap()
    b_gate = dt("b_gate", [128, 16], F32, kind="ExternalInput").ap()
    b_forget = dt("b_forget", [8, 1], F32, kind="ExternalInput").ap()
    w_sb = dt("w_sb", [128, 4, D], F32, kind="ExternalInput").ap()
    w_fx = dt("w_fx", [128, 4, D], F32, kind="ExternalInput").ap()
    w_o = dt("w_o", [128, KC, D], F32, kind="ExternalInput").ap()
    lnp = dt("lnp", [4, D], F32, kind="ExternalInput").ap()
    w_g = dt("w_g", [128, KC, DFF], F32, kind="ExternalInput").ap()
    w_u = dt("w_u", [128, KC, DFF], F32, kind="ExternalInput").ap()
    w_d = dt("w_d", [128, FC, D], F32, kind="ExternalInput").ap()
    consts = dt("consts", [128, 4 * 128 + 512], F32, kind="ExternalInput").ap()
    out = dt("out", [NOWN * 128, D], F32, kind="ExternalOutput").ap()
    fc_scr = dt("fc_scr", [8, 3, S], BF16, kind="Internal").ap()
    x1_scr = dt("x1_scr", [NOWN * 128, D], F32, kind="Internal").ap()
    dbg = {}
    if debug:
        dbg["uT"] = dt("dbg_uT", [128, KC, S], BF16, kind="ExternalOutput").ap()
        dbg["attnT"] = dt("dbg_attnT", [128, KC, NOWN * 128], BF16, kind="ExternalOutput").ap()
        dbg["ada"] = dt("dbg_ada", [128, 48], F32, kind="ExternalOutput").ap()
        dbg["fcn"] = dt("dbg_fcn", [8, S], F32, kind="ExternalOutput").ap()
        dbg["x1"] = dt("dbg_x1", [NOWN * 128, D], F32, kind="ExternalOutput").ap()
        dbg["sml"] = dt("dbg_sml", [128, 4], F32, kind="ExternalOutput").ap()
        dbg["stt"] = dt("dbg_stt", [128, 16], F32, kind="ExternalOutput").ap()
        dbg["xn"] = dt("dbg_xn", [128, D], F32, kind="ExternalOutput").ap()
        dbg["s1col"] = dt("dbg_s1col", [128, 8], F32, kind="ExternalOutput").ap()

    with ExitStack() as stack:
        S_ = Sched(nc, stack)
        AR = Arena(nc, stack, 206 * 1024)
        ps = stack.enter_context(nc.psum_tensor("ps", [128, 8 * 512], F32))

        def bank(i):
            return ps[:, i * 512:(i + 1) * 512]

        op, dma = S_.op, S_.dma

        ident_f = AR.alloc(F32, 128)
        cst_b = AR.alloc(BF16, 4 * 128 + 512)
        ident_b = cst_b[:, 0:128]
        triS_b = cst_b[:, 128:256]
        triI_b = cst_b[:, 256:384]
        negU_b = cst_b[:, 384:512]
        dm_b = cst_b[:, 512:1024]
        negones_b = AR.alloc(BF16, 128)
        adacol = AR.alloc(F32, 48)
        s1col = AR.alloc(F32, 8)
        s2col = AR.alloc(F32, 8)
        bgcol = AR.alloc(F32, 16)
        fcncol = AR.alloc(F32, 256)
        one_f = AR.alloc(F32, 128)
        GB = AR.alloc(F32, 2, D)
        epsc = AR.alloc(F32, 1)
        uT_own = AR.alloc(BF16, KC, NOWN * 128)
        m_attnT = AR.mark()
        attnT = AR.alloc(BF16, KC, NOWN * 128)
        op("dve", lambda e: e.memset(epsc, LN_EPS), writes=["epsc"])

        dma("sp", ident_f, consts[:, 0:128], writes=["ident_f"], key="c0")
        dma("pool", cst_b, consts, writes=["cst_b"], key="c1")
        dma("sp", bgcol, b_gate, writes=["bgcol"], key="c2")
        op("dve", lambda e: e.memset(negones_b, -1.0), writes=["negones"])
        op("dve", lambda e: e.memset(one_f, 1.0), writes=["one_f"])

        m_att = AR.mark()
        uT_oth = AR.alloc(BF16, KC, NOWN * 128)

        def uT_blk(g, kc):
            t = uT_own if g % 2 == 1 else uT_oth
            j = g // 2
            return t[:, kc, j * 128:(j + 1) * 128]

        mA = AR.mark()
        ccol = AR.alloc(F32, KC)
        cact = AR.alloc(F32, KC)
        badg = [AR.alloc(F32, 512) for _ in range(2)]
        adag = [AR.alloc(F32, 512) for _ in range(2)]
        wst = [AR.alloc(F32, KC, 512) for _ in range(2)]
        xb = [AR.alloc(F32, D) for _ in range(4)]
        xn = [AR.alloc(F32, D) for _ in range(3)]
        sttA = [AR.alloc(F32, 16) for _ in range(4)]
        smlA = [AR.alloc(F32, 4) for _ in range(4)]
        dma("sp", ccol, c_col, writes=["ccol"], key="c3")
        op("act", lambda e: e.activation(out=cact, in_=ccol, func=AF.Silu), reads=["ccol"], writes=["cact"])

        def ada_group(ng):
            sl = ng % 2
            dma("pool", wst[sl], w_ada[:, :, ng * 512:(ng + 1) * 512], writes=[("wst", sl)], key=("wst", sl))
            dma("pool", badg[sl][0:1, :], b_ada[:, ng * 512:(ng + 1) * 512], writes=[("badg", sl)], key=("badg", sl))
            pb = bank(sl)
            for kc in range(KC):
                op("pe", lambda e, pb=pb, kc=kc, sl=sl: e.matmul(pb[0:1, :], cact[:, kc:kc + 1], wst[sl][:, kc, :],
                                                              start=(kc == 0), stop=(kc == KC - 1)),
                   reads=["cact", ("wst", sl)], writes=[("ps", sl)], inc=(kc == KC - 1))
            op("dve", lambda e, pb=pb, sl=sl: e.tensor_tensor(out=adag[sl][0:1, :], in0=pb[0:1, :], in1=badg[sl][0:1, :], op=ALU.add),
               reads=[("ps", sl), ("badg", sl)], writes=[("adag", sl)])
            for c4 in range(4):
                c = ng * 4 + c4
                op("pe", lambda e, c=c, c4=c4, sl=sl: e.matmul(bank(2)[:, c:c + 1], adag[sl][0:1, c4 * 128:(c4 + 1) * 128], one_f[0:1, 0:1],
                                                            start=True, stop=True),
                   reads=[("adag", sl), "one_f"], writes=[("ps", 2)], inc=(c4 == 3))
            if ng in (4, 5, 10, 11):
                i, hf = (0 if ng < 6 else 1), ng % 2
                op("pe", lambda e, sl=sl: e.matmul(bank(3), one_f[0:1, :], adag[sl][0:1, :], start=True, stop=True),
                   reads=[("adag", sl), "one_f"], writes=[("ps", 3)])
                op("dve", lambda e, i=i, hf=hf: e.tensor_copy(out=GB[:, i, hf * 512:(hf + 1) * 512], in_=bank(3)),
                   reads=[("ps", 3)], writes=[("GB", i, hf)])

        for ng in range(4):
            ada_group(ng)
        op("dve", lambda e: e.tensor_copy(out=adacol[:, 0:16], in_=bank(2)[:, 0:16]), reads=[("ps", 2)], writes=["adacol"])
        op("dve", lambda e: e.tensor_scalar(out=s1col, in0=adacol[:, 8:16], scalar1=1.0, scalar2=None, op0=ALU.add),
           reads=["adacol"], writes=["s1col"])

        def A0(g):
            k4 = g % 4
            dma("sp", xb[k4], xall[g * 128:(g + 1) * 128, :], writes=[("xb", k4)], key=("xb", k4))

        def A1(g):
            k4 = g % 4
            xt, st = xb[k4], sttA[k4]
            op("dve", lambda e: e.bn_stats(out=st[:, 0:6], in_=xt[:, 0:512]), reads=[("xb", k4)], writes=[("st", k4, 0)])
            op("dve", lambda e: e.bn_stats(out=st[:, 6:12], in_=xt[:, 512:1024]), reads=[("xb", k4)], writes=[("st", k4, 1)])
            op("dve", lambda e: e.bn_aggr(out=st[:, 12:14], in_=st[:, 0:12]), reads=[("st", k4, 0), ("st", k4, 1)],
               writes=[("mv", k4)])
            sm = smlA[k4]
            op("act", lambda e: e.activation(out=sm[:, 0:1], in_=st[:, 13:14], func=AF.Sqrt, bias=epsc[:, 0:1], scale=1.0),
               reads=[("mv", k4), "epsc"], writes=[("sd", k4)])

        def A3(g):
            k4 = g % 4
            st, sm, xt = sttA[k4], smlA[k4], xb[k4]
            xo = xn[g % 3]
            op("dve", lambda e: e.reciprocal(out=sm[:, 1:2], in_=sm[:, 0:1]), reads=[("sd", k4)], writes=[("rstd", k4)])
            op("dve", lambda e: e.tensor_scalar(out=sm[:, 2:3], in0=st[:, 12:13], scalar1=-1.0, scalar2=sm[:, 1:2],
                                                op0=ALU.mult, op1=ALU.mult),
               reads=[("mv", k4), ("rstd", k4)], writes=[("nmr", k4)])
            op("act", lambda e: e.activation(out=xo, in_=xt, func=AF.Identity, bias=sm[:, 2:3], scale=sm[:, 1:2]),
               reads=[("xb", k4), ("rstd", k4), ("nmr", k4)], writes=[("xn", g % 3)])

        def A5(g):
            k = g % 2
            xo = xn[g % 3]
            for kc in range(KC):
                bk = 4 + 2 * k + kc // 4
                op("pe", lambda e, bk=bk, kc=kc: e.transpose(bank(bk)[:, (kc % 4) * 128:(kc % 4 + 1) * 128],
                                                         xo[:, kc * 128:(kc + 1) * 128], ident_f),
                   reads=[("xn", g % 3), "ident_f"], writes=[("ps", bk)], inc=(kc % 4 == 3))

        def A6(g):
            k = g % 2
            for kc in range(KC):
                bk = 4 + 2 * k + kc // 4
                src = bank(bk)[:, (kc % 4) * 128:(kc % 4 + 1) * 128]
                if kc // 4 == 0:
                    op("dve", lambda e, src=src, kc=kc: e.tensor_scalar(
                        out=uT_blk(g, kc), in0=src, scalar1=s1col[:, kc:kc + 1], scalar2=adacol[:, kc:kc + 1],
                        op0=ALU.mult, op1=ALU.add),
                       reads=[("ps", bk), "s1col", "adacol"], writes=[("uT", g)])
                else:
                    op("act", lambda e, src=src, kc=kc: e.activation(out=uT_blk(g, kc), in_=src, func=AF.Identity,
                                                                 bias=adacol[:, kc:kc + 1], scale=s1col[:, kc:kc + 1]),
                       reads=[("ps", bk), "s1col", "adacol"], writes=[("uT", g)])

        for g_ in range(2):
            A0(g_)
        for s_ in range(NB + 3):
            if s_ < NB:
                A1(s_)
            if s_ + 2 < NB:
                A0(s_ + 2)
            if 0 <= s_ - 1 < NB:
                A3(s_ - 1)
            if 0 <= s_ - 2 < NB:
                A5(s_ - 2)
            if 0 <= s_ - 3 < NB:
                A6(s_ - 3)
            if s_ % 4 == 1 and s_ // 4 < 8:
                ada_group(4 + s_ // 4)
        op("dve", lambda e: e.tensor_copy(out=adacol[:, 16:48], in_=bank(2)[:, 16:48]), reads=[("ps", 2)], writes=["adacol2"])
        op("dve", lambda e: e.tensor_scalar(out=s2col, in0=adacol[:, 32:40], scalar1=1.0, scalar2=None, op0=ALU.add),
           reads=["adacol2"], writes=["s2col"])
        if debug:
            dma("sp", dbg["ada"], adacol, reads=["adacol", "adacol2"], key="dbg1")
            dma("sp", dbg["uT"][:, :, 0:2048], uT_oth, reads=[("uT", g) for g in range(0, NB, 2)], key="dbg6")
            dma("sp", dbg["uT"][:, :, 2048:4096], uT_own, reads=[("uT", g) for g in range(1, NB, 2)], key="dbg7")
        S_.barrier()
        AR.release(mA)

        mF = AR.mark()
        wf_b = AR.alloc(BF16, KC, 8)
        nbf = AR.alloc(F32, 2)
        spf = AR.alloc(F32, S)
        fcn = AR.alloc(F32, S)
        fc3 = AR.alloc(BF16, 3, S)
        etmp = [AR.alloc(F32, 512) for _ in range(2)]
        dma("pool", wf_b, w_in[:, :, OFF_FG:OFF_FG + 8], writes=["wf_b"], key="c5")
        dma("sp", nbf[0:8, 0:1], b_forget, writes=["nbf0"], key="c6")
        op("dve", lambda e: e.tensor_scalar(out=nbf[0:8, 1:2], in0=nbf[0:8, 0:1], scalar1=-1.0, scalar2=None, op0=ALU.mult),
           reads=["nbf0"], writes=["nbf"])
        for pg in range(8):
            bk = pg % 2
            for blk in range(4):
                g = pg * 4 + blk
                for kc in range(KC):
                    op("pe", lambda e, bk=bk, blk=blk, g=g, kc=kc: e.matmul(
                        bank(bk)[0:8, blk * 128:(blk + 1) * 128], wf_b[:, kc, :], uT_blk(g, kc),
                        start=(kc == 0), stop=(kc == KC - 1)),
                       reads=["wf_b", ("uT", g)], writes=[("ps", bk)], inc=(blk == 3 and kc == KC - 1))
            op("act", lambda e, bk=bk: e.activation(out=etmp[bk][0:8, :], in_=bank(bk)[0:8, :], func=AF.Exp,
                                                    bias=nbf[0:8, 1:2], scale=-1.0),
               reads=[("ps", bk), "nbf"], writes=[("etmp", bk)])
            op("act", lambda e, bk=bk, pg=pg: e.activation(out=spf[0:8, pg * 512:(pg + 1) * 512], in_=etmp[bk][0:8, :],
                                                           func=AF.Ln, bias=1.0, scale=1.0),
               reads=[("etmp", bk)], writes=["spf"])
        op("dve", lambda e: e.tensor_tensor_scan(out=fcn[0:8, :], data0=spf[0:8, :], data1=spf[0:8, :], initial=0.0,
                                                 op0=ALU.add, op1=ALU.max),
           reads=["spf"], writes=["fcn"])
        if debug:
            dma("sp", dbg["fcn"], fcn[0:8, :], reads=["fcn"], key="dbg8")
        for g in range(NB):
            op("pe", lambda e, g=g: e.transpose(bank(2)[:, g * 8:(g + 1) * 8], fcn[0:8, g * 128:(g + 1) * 128], ident_f[0:8, 0:8]),
               reads=["fcn", "ident_f"], writes=[("ps", 2)], inc=(g == NB - 1))
        op("dve", lambda e: e.tensor_copy(out=fcncol, in_=bank(2)[:, 0:256]), reads=[("ps", 2)], writes=["fcncol"])
        op("dve", lambda e: e.tensor_copy(out=fc3[0:8, 0, :], in_=fcn[0:8, :]), reads=["fcn"], writes=[("fc3", 0)])
        op("dve", lambda e: e.tensor_tensor(out=spf[0:8, :], in0=fcn[0:8, :], in1=fc3[0:8, 0, :], op=ALU.subtract),
           reads=["fcn", ("fc3", 0), ("ps", 2)], writes=["spf"])
        op("dve", lambda e: e.tensor_copy(out=fc3[0:8, 1, :], in_=spf[0:8, :]), reads=["spf"], writes=[("fc3", 1)])
        op("dve", lambda e: e.tensor_tensor(out=fcn[0:8, :], in0=spf[0:8, :], in1=fc3[0:8, 1, :], op=ALU.subtract),
           reads=["spf", ("fc3", 1), "fcncol"], writes=["fcn"])
        op("dve", lambda e: e.tensor_copy(out=fc3[0:8, 2, :], in_=fcn[0:8, :]), reads=["fcn"], writes=[("fc3", 2)])
        dma("sp", fc_scr, fc3[0:8, :, :], reads=[("fc3", 0), ("fc3", 1), ("fc3", 2)], writes=["fc_scr"], key="c7")
        S_.barrier()
        AR.release(mF)

        wp = [AR.alloc(BF16, KC, 384) for _ in range(2)]
        KT = AR.alloc(BF16, S)
        KTB = AR.alloc(BF16, S)
        QT = AR.alloc(BF16, NOWN * 128)
        QTB = AR.alloc(BF16, NOWN * 128)
        VT = AR.alloc(BF16, NB, 2, 128)
        eb = [[AR.alloc(F32, 512) for _ in range(2)] for _ in range(2)]
        spb = [[AR.alloc(BF16, 512) for _ in range(2)] for _ in range(2)]
        Ab = [[AR.alloc(BF16, 512) for _ in range(2)] for _ in range(2)]
        Rb = [AR.alloc(BF16, 512) for _ in range(2)]
        rden = [AR.alloc(F32, 512) for _ in range(2)]

        def proj_pair(pr):
            fox = pr >= 4
            hp = pr % 4
            base = OFF_FOX if fox else OFF_SB
            w = wp[pr % 2]
            wk = ("wp", pr % 2)
            for i in range(3):
                c0 = base + i * 512 + hp * 128
                dma("pool", w[:, :, i * 128:(i + 1) * 128], w_in[:, :, c0:c0 + 128], writes=[(wk, i)], key=(wk, i))
            for sg in range(8):
                src = uT_oth if sg < 4 else uT_own
                cs = (sg % 4) * 512
                bk = sg % 2
                rk = [("uT", g) for g in range(NB) if sb_of(g) // 4 == sg]
                for kc in range(KC):
                    op("pe", lambda e, bk=bk, kc=kc, src=src, cs=cs: e.matmul(bank(bk), w[:, kc, 128:256], src[:, kc, cs:cs + 512],
                                                                          start=(kc == 0), stop=(kc == KC - 1)),
                       reads=[(wk, 1)] + rk, writes=[("ps", bk)], inc=(kc == KC - 1))
                if not fox:
                    op("dve", lambda e, bk=bk, sg=sg: e.tensor_copy(out=KT[:, sg * 512:(sg + 1) * 512], in_=bank(bk)),
                       reads=[("ps", bk)], writes=[("KT", sg)])
                else:
                    op("dve", lambda e, bk=bk, sg=sg: e.tensor_copy(out=KT[0:64, sg * 512:(sg + 1) * 512], in_=bank(bk)[0:64, :]),
                       reads=[("ps", bk)], writes=[("KT", sg)])
                    op("dve", lambda e, bk=bk, sg=sg: e.tensor_copy(out=KTB[64:128, sg * 512:(sg + 1) * 512], in_=bank(bk)[64:128, :]),
                       reads=[("ps", bk)], writes=[("KTB", sg)])
            for G in range(4):
                bk = G % 2
                rk = [("uT", 2 * j + 1) for j in range(4 * G, 4 * G + 4)]
                for kc in range(KC):
                    op("pe", lambda e, bk=bk, kc=kc, G=G: e.matmul(bank(bk), w[:, kc, 0:128], uT_own[:, kc, G * 512:(G + 1) * 512],
                                                                start=(kc == 0), stop=(kc == KC - 1)),
                       reads=[(wk, 0)] + rk, writes=[("ps", bk)], inc=(kc == KC - 1))
                if not fox:
                    op("dve", lambda e, bk=bk, G=G: e.tensor_scalar(out=QT[:, G * 512:(G + 1) * 512], in0=bank(bk), scalar1=0.125,
                                                                    scalar2=None, op0=ALU.mult),
                       reads=[("ps", bk)], writes=[("QT", G)])
                else:
                    op("dve", lambda e, bk=bk, G=G: e.tensor_scalar(out=QT[0:64, G * 512:(G + 1) * 512], in0=bank(bk)[0:64, :],
                                                                    scalar1=0.125, scalar2=None, op0=ALU.mult),
                       reads=[("ps", bk)], writes=[("QT", G)])
                    op("dve", lambda e, bk=bk, G=G: e.tensor_scalar(out=QTB[64:128, G * 512:(G + 1) * 512], in0=bank(bk)[64:128, :],
                                                                    scalar1=0.125, scalar2=None, op0=ALU.mult),
                       reads=[("ps", bk)], writes=[("QTB", G)])
            for sg in range(8):
                bk = sg % 2
                for blk in range(4):
                    sbk = sg * 4 + blk
                    g = 2 * sbk if sbk < 16 else 2 * (sbk - 16) + 1
                    for kc in range(KC):
                        op("pe", lambda e, bk=bk, blk=blk, g=g, kc=kc: e.matmul(
                            bank(bk)[:, blk * 128:(blk + 1) * 128], uT_blk(g, kc), w[:, kc, 256:384],
                            start=(kc == 0), stop=(kc == KC - 1)),
                           reads=[(wk, 2), ("uT", g)], writes=[("ps", bk)], inc=(blk == 3 and kc == KC - 1))
                if not fox:
                    op("dve", lambda e, bk=bk, sg=sg: e.tensor_copy(
                        out=VT[:, sg * 4:(sg + 1) * 4, 0, :], in_=bank(bk).rearrange("p (a b) -> p a b", a=4, b=128)),
                       reads=[("ps", bk)], writes=[("VT", sg)])
                else:
                    op("dve", lambda e, bk=bk, sg=sg: e.tensor_copy(
                        out=VT[:, sg * 4:(sg + 1) * 4, :, 0:64],
                        in_=bank(bk).rearrange("p (a h d) -> p a h d", a=4, h=2, d=64)),
                       reads=[("ps", bk)], writes=[("VT", sg)])
            if fox:
                for h in range(2):
                    hg = hp * 2 + h
                    dst = QT[64:67, :] if h == 0 else QTB[0:3, :]
                    srcap = fc_scr[hg].rearrange("r (j two c) -> r j two c", two=2, c=128)[:, :, 1, :]
                    dma("sp", dst.rearrange("r (j c) -> r j c", c=128), srcap, reads=["fc_scr"],
                        writes=[("QTaug", h)] + [(("QT" if h == 0 else "QTB"), G_) for G_ in range(4)], key=("qaug", h))

        def grp_cols(G, g):
            d = g - 8 * G
            return (d // 2) * 128 if d >= 0 else 0

        hcount = [0, 0]

        def attn_sb(pr):
            steps = []
            for G in range(4):
                gs = list(range(8 * G + 7, -1, -1))
                for it, g in enumerate(gs):
                    for h in range(2):
                        par = hcount[h] % 2
                        hcount[h] += 1
                        steps.append(dict(G=G, it=it, g=g, h=h, par=par, last=(g == 0), c0=grp_cols(G, g), b=sb_of(g)))
            N = len(steps)

            def bufs(st):
                h, par = st["h"], st["par"]
                return bank(2 + 2 * h + par), ("ps", 2 + 2 * h + par), bank(6 + h), ("ps", 6 + h)

            def QK(st):
                G, g, h, par, c0, b = st["G"], st["g"], st["h"], st["par"], st["c0"], st["b"]
                zb, zk, ob, ok_ = bufs(st)
                d = g - 8 * G
                rs = slice(h * 64, (h + 1) * 64)
                qsl = slice(G * 512 + c0, G * 512 + 512)
                has_tri = d >= 1 and d % 2 == 1
                has_dm = g == 0
                op("pe", lambda e, st_=not (has_tri or has_dm): e.matmul(
                    zb[:, c0:512], KT[rs, b * 128:(b + 1) * 128], QT[rs, qsl], start=True, stop=st_),
                   reads=[("KT", b // 4), ("QT", G)], writes=[zk], inc=not (has_tri or has_dm))
                if has_tri:
                    op("pe", lambda e: e.matmul(zb[:, c0:c0 + 128], ident_b, triS_b, start=False, stop=not has_dm),
                       reads=["cst_b"], writes=[zk], inc=not has_dm)
                if has_dm:
                    op("pe", lambda e: e.matmul(zb[:, c0:512], ident_b, dm_b[:, c0:512], start=False, stop=True),
                       reads=["cst_b"], writes=[zk])

            def EL(st):
                h, par, c0 = st["h"], st["par"], st["c0"]
                zb, zk, ob, ok_ = bufs(st)
                e_, sp_ = eb[h][par], spb[h][par]
                op("act", lambda e: e.activation(out=e_[:, c0:512], in_=zb[:, c0:512], func=AF.Exp),
                   reads=[zk], writes=[("e", h, par)])
                op("act", lambda e: e.activation(out=sp_[:, c0:512], in_=e_[:, c0:512], func=AF.Ln, bias=1.0, scale=1.0),
                   reads=[("e", h, par)], writes=[("sp", h, par)])

            def CUM(st):
                h, par, c0, it = st["h"], st["par"], st["c0"], st["it"]
                zb, zk, ob, ok_ = bufs(st)
                sp_ = spb[h][par]
                op("pe", lambda e: e.matmul(zb[:, c0:512], negU_b, sp_[:, c0:512], start=False, stop=True, skip_group_check=True),
                   reads=["cst_b", ("sp", h, par)], writes=[zk], inc=(it == 0))
                if it > 0:
                    op("pe", lambda e: e.matmul(zb[:, c0:512], negones_b, Rb[h][:, c0:512], start=False, stop=True,
                                                skip_group_check=True),
                       reads=["negones", ("R", h)], writes=[zk])

            def AR_(st):
                h, par, c0, it = st["h"], st["par"], st["c0"], st["it"]
                zb, zk, ob, ok_ = bufs(st)
                A_, sp_ = Ab[h][par], spb[h][par]
                op("act", lambda e: e.activation(out=A_[:, c0:512], in_=zb[:, c0:512], func=AF.Exp),
                   reads=[zk], writes=[("A", h, par)])
                if it == 0:
                    op("pool", lambda e: e.memset(Rb[h], 0.0), writes=[("R", h)])
                if not st["last"]:
                    op("pool", lambda e: e.tensor_tensor(out=Rb[h][:, c0:512], in0=Rb[h][:, c0:512], in1=sp_[:, c0:512], op=ALU.add),
                       reads=[("R", h), ("sp", h, par)], writes=[("R", h)])

            def AV(st):
                G, h, par, c0, it, b = st["G"], st["h"], st["par"], st["c0"], st["it"], st["b"]
                zb, zk, ob, ok_ = bufs(st)
                A_ = Ab[h][par]
                op("pe", lambda e: e.matmul(ob[0:64, c0:512], VT[:, b, 0, h * 64:(h + 1) * 64], A_[:, c0:512],
                                            start=(it == 0), stop=st["last"], skip_group_check=True),
                   reads=[("VT", b // 4), ("A", h, par)], writes=[ok_], inc=True)
                if st["last"]:
                    op("act", lambda e: e.activation(out=attnT[h * 64:(h + 1) * 64, pr, G * 512:(G + 1) * 512],
                                                     in_=ob[0:64, :], func=AF.Copy),
                       reads=[ok_], writes=[("attnT", pr, G)])

            for s in range(-2, N + 2):
                if 0 <= s - 1 < N:
                    CUM(steps[s - 1])
                if 0 <= s - 2 < N:
                    AV(steps[s - 2])
                if 0 <= s + 2 < N:
                    QK(steps[s + 2])
                if 0 <= s < N:
                    EL(steps[s])
                if 0 <= s - 1 < N:
                    AR_(steps[s - 1])

        def attn_fox(pr):
            hp = pr % 4
            steps = []
            for G in range(4):
                gs = list(range(8 * G + 7, -1, -1))
                for it, g in enumerate(gs):
                    for h in range(2):
                        par = hcount[h] % 2
                        hcount[h] += 1
                        steps.append(dict(G=G, it=it, g=g, h=h, par=par, last=(g == 0), c0=grp_cols(G, g), b=sb_of(g)))
            N = len(steps)

            def bufs(st):
                h, par = st["h"], st["par"]
                return bank(2 + 2 * h + par), ("ps", 2 + 2 * h + par), bank(6 + h), ("ps", 6 + h)

            def QK(st):
                G, g, h, par, c0, b = st["G"], st["g"], st["h"], st["par"], st["c0"], st["b"]
                zb, zk, ob, ok_ = bufs(st)
                d = g - 8 * G
                qsl = slice(G * 512 + c0, G * 512 + 512)
                has_tri = d >= 1 and d % 2 == 1
                has_dm = g == 0
                if h == 0:
                    kap, qap = KT[0:67, b * 128:(b + 1) * 128], QT[0:67, qsl]
                    rds = [("KT", b // 4), ("QT", G), ("QTaug", 0), "kaug"]
                else:
                    kap, qap = KTB[:, b * 128:(b + 1) * 128], QTB[:, qsl]
                    rds = [("KTB", b // 4), ("QTB", G), ("QTaug", 1), "kaug"]
                op("pe", lambda e, st_=not (has_tri or has_dm): e.matmul(zb[:, c0:512], kap, qap, start=True, stop=st_),
                   reads=rds, writes=[zk], inc=not (has_tri or has_dm))
                if has_tri:
                    op("pe", lambda e: e.matmul(zb[:, c0:c0 + 128], ident_b, triI_b, start=False, stop=not has_dm),
                       reads=["cst_b"], writes=[zk], inc=not has_dm)
                if has_dm:
                    op("pe", lambda e: e.matmul(zb[:, c0:512], ident_b, dm_b[:, c0:512], start=False, stop=True),
                       reads=["cst_b"], writes=[zk])

            def PX(st):
                g, h, par, c0 = st["g"], st["h"], st["par"], st["c0"]
                zb, zk, ob, ok_ = bufs(st)
                hg = hp * 2 + h
                A_ = Ab[h][par]
                op("act", lambda e: e.activation(out=A_[:, c0:512], in_=zb[:, c0:512], func=AF.Exp,
                                                 bias=fcncol[:, g * 8 + hg:g * 8 + hg + 1], scale=1.0),
                   reads=[zk, "fcncol"], writes=[("A", h, par)])

            def AV(st):
                G, h, par, c0, it, b = st["G"], st["h"], st["par"], st["c0"], st["it"], st["b"]
                zb, zk, ob, ok_ = bufs(st)
                A_ = Ab[h][par]
                op("pe", lambda e: e.matmul(ob[:, c0:512], VT[:, b, h, :], A_[:, c0:512], start=(it == 0), stop=st["last"],
                                            skip_group_check=True),
                   reads=[("VT", b // 4), ("A", h, par), "vones"], writes=[ok_], inc=True)
                if st["last"]:
                    op("dve", lambda e: e.reciprocal(out=rden[h][64:128, :], in_=ob[64:128, :]),
                       reads=[ok_], writes=[("rden", h)])
                    op("dve", lambda e: e.tensor_tensor(out=attnT[h * 64:(h + 1) * 64, pr, G * 512:(G + 1) * 512],
                                                        in0=ob[0:64, :], in1=rden[h][64:128, :], op=ALU.mult),
                       reads=[ok_, ("rden", h)], writes=[("attnT", pr, G)])

            for s in range(-2, N + 2):
                if 0 <= s - 2 < N:
                    AV(steps[s - 2])
                if 0 <= s + 2 < N:
                    QK(steps[s + 2])
                if 0 <= s < N:
                    PX(steps[s])

        for pr in range(8):
            if pr == 4:
                op("pool", lambda e: e.memset(KT[64:67, :], -1.0), writes=[("KT", i) for i in range(8)] + ["kaug"])
                op("pool", lambda e: e.memset(KTB[0:64, :], 0.0), writes=["kaug0"])
                op("pool", lambda e: e.memset(KTB[0:3, :], -1.0), reads=["kaug0"], writes=["kaug"])
                op("pool", lambda e: e.memset(QTB[0:64, :], 0.0), writes=[("QTaug", 1)])
                op("pool", lambda e: e.memset(VT[:, :, :, 64:128], 1.0), writes=[("VT", i) for i in range(8)] + ["vones"])
            proj_pair(pr)
            if pr < 4:
                attn_sb(pr)
            else:
                attn_fox(pr)
        if debug:
            dma("sp", dbg["attnT"], attnT, reads=[("attnT", pr, G) for pr in range(8) for G in range(4)], key="dbg9")
        S_.barrier()
        AR.release(m_att)

        mC1 = AR.mark()
        wbg = AR.alloc(BF16, KC, 2 * D)
        wsb_b = AR.alloc(BF16, 4, D)
        wfx_b = AR.alloc(BF16, 4, D)
        wo_b = AR.alloc(BF16, KC, D)
        lnB1 = AR.alloc(F32, 2, D)
        gT = [AR.alloc(BF16, 512) for _ in range(2)]
        tmpa = [AR.alloc(F32, 512) for _ in range(2)]
        tmpb = [AR.alloc(F32, 512) for _ in range(2)]
        mT = [AR.alloc(BF16, KC, 512) for _ in range(2)]
        xinB = [AR.alloc(F32, D) for _ in range(2)]
        r1 = [AR.alloc(F32, D) for _ in range(2)]
        x1b = [AR.alloc(F32, D) for _ in range(2)]
        sttB = [AR.alloc(F32, 16) for _ in range(4)]
        smlB = [AR.alloc(F32, 4) for _ in range(4)]

        for hf in range(2):
            dma("pool", wbg[:, :, hf * D:(hf + 1) * D], w_in[:, :, OFF_BG + hf * D:OFF_BG + (hf + 1) * D], writes=[("wbg", hf)],
                key=("w0", hf))
        dma("pool", wsb_b, w_sb, writes=["wsb"], key="w1")
        dma("pool", wfx_b, w_fx, writes=["wfx"], key="w2")
        for i in range(2):
            dma("sp", lnB1[:, i, :], lnp[i:i + 1, :].partition_broadcast(128), writes=[("lnB1", i)], key=("lnB1", i))
        for kc in range(KC):
            st_ = xinB[kc % 2]
            dma("sp", st_, w_o[:, kc, :], writes=[("xinB", kc % 2)], key=("xinB", kc % 2))
            op("dve", lambda e, st_=st_, kc=kc: e.tensor_tensor(out=wo_b[:, kc, :], in0=st_, in1=GB[:, 0, :], op=ALU.mult),
               reads=[("xinB", kc % 2), ("GB", 0, 0), ("GB", 0, 1)], writes=[("wo", kc)])

        def own_rows(j):
            g = 2 * j + 1
            return slice(g * 128, (g + 1) * 128)

        def nloop_thread(G):
            ukeys = [("uT", 2 * j + 1) for j in range(4 * G, 4 * G + 4)]
            mt = mT[G % 2]
            for n in range(8):
                par = n % 2
                for w_ in range(2):
                    col = w_ * D + n * 128
                    for kc in range(KC):
                        op("pe", lambda e, w_=w_, kc=kc, col=col: e.matmul(bank(w_), wbg[:, kc, col:col + 128],
                                                                        uT_own[:, kc, G * 512:(G + 1) * 512],
                                                                        start=(kc == 0), stop=(kc == KC - 1)),
                           reads=[("wbg", w_)] + ukeys, writes=[("ps", w_)], inc=(kc == KC - 1))
                    op("act", lambda e, w_=w_, n=n: e.activation(out=gT[w_], in_=bank(w_), func=AF.Sigmoid,
                                                                bias=bgcol[:, w_ * 8 + n:w_ * 8 + n + 1], scale=1.0),
                       reads=[("ps", w_), "bgcol"], writes=[("gT", w_)])
                for w_, wt in enumerate((wsb_b, wfx_b)):
                    for kc in range(4):
                        op("pe", lambda e, kc=kc, wt=wt, w_=w_, n=n: e.matmul(
                            bank(2 + w_), wt[:, kc, n * 128:(n + 1) * 128], attnT[:, w_ * 4 + kc, G * 512:(G + 1) * 512],
                            start=(kc == 0), stop=(kc == 3)),
                           reads=["wsb", "wfx"] + [("attnT", w_ * 4 + kc, G)], writes=[("ps", 2 + w_)], inc=(kc == 3))
                op("dve", lambda e, par=par: e.tensor_tensor(out=tmpa[par], in0=bank(2), in1=gT[0], op=ALU.mult),
                   reads=[("ps", 2), ("gT", 0)], writes=[("tmpa", par)])
                op("dve", lambda e, par=par: e.tensor_tensor(out=tmpb[par], in0=bank(3), in1=gT[1], op=ALU.mult),
                   reads=[("ps", 3), ("gT", 1)], writes=[("tmpb", par)])
                op("pool", lambda e, par=par, n=n: e.tensor_tensor(out=mt[:, n, :], in0=tmpa[par], in1=tmpb[par], op=ALU.add),
                   reads=[("tmpa", par), ("tmpb", par)], writes=[("mT", G % 2, n)])
                yield

        def stats_ops(src, rks, k4):
            st, sm = sttB[k4], smlB[k4]
            op("dve", lambda e: e.bn_stats(out=st[:, 0:6], in_=src[:, 0:512]), reads=rks, writes=[("st", k4, 0)])
            op("dve", lambda e: e.bn_stats(out=st[:, 6:12], in_=src[:, 512:1024]), reads=rks, writes=[("st", k4, 1)])
            op("dve", lambda e: e.bn_aggr(out=st[:, 12:14], in_=st[:, 0:12]), reads=[("st", k4, 0), ("st", k4, 1)],
               writes=[("mv", k4)])
            op("act", lambda e: e.activation(out=sm[:, 0:1], in_=st[:, 13:14], func=AF.Sqrt, bias=epsc[:, 0:1], scale=1.0),
               reads=[("mv", k4), "epsc"], writes=[("sd", k4)])

        def rstd_ops(k4):
            st, sm = sttB[k4], smlB[k4]
            op("dve", lambda e: e.reciprocal(out=sm[:, 1:2], in_=sm[:, 0:1]), reads=[("sd", k4)], writes=[("rstd", k4)])
            op("dve", lambda e: e.tensor_scalar(out=sm[:, 2:3], in0=st[:, 12:13], scalar1=-1.0, scalar2=sm[:, 1:2],
                                                op0=ALU.mult, op1=ALU.mult),
               reads=[("mv", k4), ("rstd", k4)], writes=[("nmr", k4)])

        def tile_thread(j):
            G, i, k = j // 4, j % 4, j % 2
            ka, kb = (2 * j) % 4, (2 * j + 1) % 4
            mt = mT[G % 2]
            dma("sp", xinB[k], xall[own_rows(j), :], writes=[("xinB", k)], key=("xinB", k))
            for hf in range(2):
                for kc in range(KC):
                    op("pe", lambda e, kc=kc, hf=hf: e.matmul(bank(4 + 2 * k + hf), mt[:, kc, i * 128:(i + 1) * 128],
                                                           wo_b[:, kc, hf * 512:(hf + 1) * 512],
                                                           start=(kc == 0), stop=(kc == KC - 1)),
                       reads=[("mT", G % 2, kc), ("wo", kc)], writes=[("ps", 4 + 2 * k + hf)], inc=(kc == KC - 1))
            yield
            for hf in range(2):
                op("dve", lambda e, hf=hf: e.scalar_tensor_tensor(
                    out=r1[k][:, hf * 512:(hf + 1) * 512], in0=xinB[k][:, hf * 512:(hf + 1) * 512], scalar=ALPHA,
                    in1=bank(4 + 2 * k + hf), op0=ALU.mult, op1=ALU.add),
                   reads=[("xinB", k), ("ps", 4 + 2 * k + hf)], writes=[("r1", k)])
            stats_ops(r1[k], [("r1", k)], ka)
            yield
            rstd_ops(ka)
            sm = smlB[ka]
            op("act", lambda e: e.activation(out=r1[k], in_=r1[k], func=AF.Identity, bias=sm[:, 2:3], scale=sm[:, 1:2]),
               reads=[("r1", k), ("rstd", ka), ("nmr", ka)], writes=[("r1", k)])
            yield
            op("pool", lambda e: e.tensor_tensor(out=x1b[k], in0=r1[k], in1=lnB1[:, 0, :], op=ALU.mult),
               reads=[("r1", k), ("lnB1", 0)], writes=[("x1", k)])
            op("pool", lambda e: e.tensor_tensor(out=x1b[k], in0=x1b[k], in1=lnB1[:, 1, :], op=ALU.add),
               reads=[("x1", k), ("lnB1", 1)], writes=[("x1", k)])
            yield
            dma("sp", x1_scr[j * 128:(j + 1) * 128, :], x1b[k], reads=[("x1", k)], writes=[("x1scr", j)], key=("x1o", k))
            if debug:
                dma("sp", dbg["x1"][j * 128:(j + 1) * 128, :], x1b[k], reads=[("x1", k)], key=("dbgx1", k))
            stats_ops(x1b[k], [("x1", k)], kb)
            yield
            rstd_ops(kb)
            sm2 = smlB[kb]
            op("act", lambda e: e.activation(out=r1[k], in_=x1b[k], func=AF.Identity, bias=sm2[:, 2:3], scale=sm2[:, 1:2]),
               reads=[("x1", k), ("rstd", kb), ("nmr", kb)], writes=[("r1", k)])
            yield
            for kc in range(KC):
                bk = 4 + 2 * k + kc // 4
                op("pe", lambda e, bk=bk, kc=kc: e.transpose(bank(bk)[:, (kc % 4) * 128:(kc % 4 + 1) * 128],
                                                         r1[k][:, kc * 128:(kc + 1) * 128], ident_f),
                   reads=[("r1", k), "ident_f"], writes=[("ps", bk)], inc=(kc % 4 == 3))
            yield
            for kc in range(KC):
                bk = 4 + 2 * k + kc // 4
                src = bank(bk)[:, (kc % 4) * 128:(kc % 4 + 1) * 128]
                if kc // 4 == 0:
                    op("dve", lambda e, src=src, kc=kc: e.tensor_scalar(
                        out=uT_own[:, kc, j * 128:(j + 1) * 128], in0=src,
                        scalar1=s2col[:, kc:kc + 1], scalar2=adacol[:, 24 + kc:25 + kc], op0=ALU.mult, op1=ALU.add),
                       reads=[("ps", bk), "s2col", "adacol2"], writes=[("uT", 2 * j + 1)])
                else:
                    op("act", lambda e, src=src, kc=kc: e.activation(out=uT_own[:, kc, j * 128:(j + 1) * 128], in_=src, func=AF.Identity,
                                                                 bias=adacol[:, 24 + kc:25 + kc], scale=s2col[:, kc:kc + 1]),
                       reads=[("ps", bk), "s2col", "adacol2"], writes=[("uT", 2 * j + 1)])
            yield

        for _ in nloop_thread(0):
            pass
        nl_done = 1
        nl = None
        next_tile = 0
        tiles = []
        while next_tile < NOWN or tiles or nl is not None:
            while len(tiles) < 2 and next_tile < NOWN and next_tile // 4 < nl_done:
                tiles.append(tile_thread(next_tile))
                next_tile += 1
            if nl is None and nl_done < 4 and next_tile >= 4 * (nl_done - 1):
                nl = nloop_thread(nl_done)
            progressed = False
            for t in list(tiles):
                try:
                    next(t)
                    progressed = True
                except StopIteration:
                    tiles.remove(t)
            if nl is not None:
                try:
                    next(nl)
                    progressed = True
                except StopIteration:
                    nl = None
                    nl_done += 1
            if not progressed and not tiles and nl is None and next_tile >= NOWN:
                break
        S_.barrier()
        AR.release(m_attnT)

        wg_b = AR.alloc(BF16, KC, DFF)
        wu_b = AR.alloc(BF16, KC, DFF)
        wd_b = AR.alloc(BF16, FC, D)
        hT = AR.alloc(BF16, FC, 256)
        sgt = [AR.alloc(F32, 256) for _ in range(2)]
        xin = [AR.alloc(F32, D) for _ in range(2)]
        r2 = xin
        lnB = GB
        stt = [AR.alloc(F32, 16) for _ in range(2)]
        sml = [AR.alloc(F32, 4) for _ in range(2)]
        CG = [(0, 768), (768, 1536), (1536, 2304), (2304, 2816)]
        for ci, (a, b_) in enumerate(CG):
            dma("pool", wg_b[:, :, a:b_], w_g[:, :, a:b_], writes=[("wg", ci)], key=("w3", ci))
            dma("pool", wu_b[:, :, a:b_], w_u[:, :, a:b_], writes=[("wu", ci)], key=("w4", ci))
        for fc in range(FC):
            st_ = xin[fc % 2]
            dma("sp", st_, w_d[:, fc, :], writes=[("xin2", fc % 2)], key=("xin", fc % 2))
            op("dve", lambda e, st_=st_, fc=fc: e.tensor_tensor(out=wd_b[:, fc, :], in0=st_, in1=GB[:, 1, :], op=ALU.mult),
               reads=[("xin2", fc % 2), ("GB", 1, 0), ("GB", 1, 1)], writes=[("wd", fc)])
        for i in range(2):
            dma("sp", lnB[:, i, :], lnp[2 + i:3 + i, :].partition_broadcast(128), writes=[("GB", i, 0), ("GB", i, 1)],
                key=("lnB", i))
        for sub in range(8):
            cs = sub * 256
            ukeys = [("uT", 2 * j + 1) for j in (2 * sub, 2 * sub + 1)]
            for fc in range(FC):
                par = fc % 2
                bg, bu = bank(par), bank(2 + par)
                for kc in range(KC):
                    op("pe", lambda e, bg=bg, kc=kc, fc=fc, cs=cs: e.matmul(bg[:, 0:256], wg_b[:, kc, fc * 128:(fc + 1) * 128],
                                                                       uT_own[:, kc, cs:cs + 256], start=(kc == 0), stop=(kc == KC - 1)),
                       reads=[("wg", min(fc // 6, 3))] + ukeys, writes=[("ps", par)], inc=(kc == KC - 1))
                for kc in range(KC):
                    op("pe", lambda e, bu=bu, kc=kc, fc=fc, cs=cs: e.matmul(bu[:, 0:256], wu_b[:, kc, fc * 128:(fc + 1) * 128],
                                                                       uT_own[:, kc, cs:cs + 256], start=(kc == 0), stop=(kc == KC - 1)),
                       reads=[("wu", min(fc // 6, 3))] + ukeys, writes=[("ps", 2 + par)], inc=(kc == KC - 1))
                op("act", lambda e, bg=bg, par=par: e.activation(out=sgt[par], in_=bg[:, 0:256], func=AF.Silu),
                   reads=[("ps", par)], writes=[("sgt", par)])
                op("dve", lambda e, bu=bu, par=par, fc=fc: e.tensor_tensor(out=hT[:, fc, :], in0=bu[:, 0:256], in1=sgt[par], op=ALU.mult),
                   reads=[("ps", 2 + par), ("sgt", par)], writes=[("hT", fc)])
            for i in range(2):
                j = 2 * sub + i
                k = j % 2
                dma("sp", xin[k], x1_scr[j * 128:(j + 1) * 128, :], reads=[("x1scr", j)], writes=[("xin2", k)], key=("xin", k))
                for hf in range(2):
                    bk = 4 + 2 * k + hf
                    for fc in range(FC):
                        op("pe", lambda e, bk=bk, fc=fc, i=i, hf=hf: e.matmul(bank(bk), hT[:, fc, i * 128:(i + 1) * 128],
                                                                           wd_b[:, fc, hf * 512:(hf + 1) * 512],
                                                                           start=(fc == 0), stop=(fc == FC - 1)),
                           reads=[("hT", fc), ("wd", fc)], writes=[("ps", bk)], inc=(fc == FC - 1))
                    op("dve", lambda e, bk=bk, k=k, hf=hf: e.scalar_tensor_tensor(
                        out=r2[k][:, hf * 512:(hf + 1) * 512], in0=xin[k][:, hf * 512:(hf + 1) * 512], scalar=ALPHA,
                        in1=bank(bk), op0=ALU.mult, op1=ALU.add),
                       reads=[("xin2", k), ("ps", bk)], writes=[("xin2", k)])
                st, sm = stt[k], sml[k]
                op("dve", lambda e, st=st, k=k: e.bn_stats(out=st[:, 0:6], in_=r2[k][:, 0:512]), reads=[("xin2", k)], writes=[("st", k, 0)])
                op("dve", lambda e, st=st, k=k: e.bn_stats(out=st[:, 6:12], in_=r2[k][:, 512:1024]), reads=[("xin2", k)], writes=[("st", k, 1)])
                op("dve", lambda e, st=st: e.bn_aggr(out=st[:, 12:14], in_=st[:, 0:12]), reads=[("st", k, 0), ("st", k, 1)],
                   writes=[("mv", k)])
                op("act", lambda e, st=st, sm=sm: e.activation(out=sm[:, 0:1], in_=st[:, 13:14], func=AF.Sqrt, bias=epsc[:, 0:1], scale=1.0),
                   reads=[("mv", k), "epsc"], writes=[("sd", k)])
                op("dve", lambda e, sm=sm: e.reciprocal(out=sm[:, 1:2], in_=sm[:, 0:1]), reads=[("sd", k)], writes=[("rstd", k)])
                op("dve", lambda e, st=st, sm=sm: e.tensor_scalar(out=sm[:, 2:3], in0=st[:, 12:13], scalar1=-1.0, scalar2=sm[:, 1:2],
                                                                  op0=ALU.mult, op1=ALU.mult),
                   reads=[("mv", k), ("rstd", k)], writes=[("nmr", k)])
                op("act", lambda e, k=k, sm=sm: e.activation(out=r2[k], in_=r2[k], func=AF.Identity, bias=sm[:, 2:3], scale=sm[:, 1:2]),
                   reads=[("xin2", k), ("rstd", k), ("nmr", k)], writes=[("xin2", k)])
                op("pool", lambda e, k=k: e.tensor_tensor(out=r2[k], in0=r2[k], in1=lnB[:, 0, :], op=ALU.mult),
                   reads=[("xin2", k), ("GB", 0, 0), ("GB", 0, 1)], writes=[("xin2", k)])
                op("pool", lambda e, k=k: e.tensor_tensor(out=r2[k], in0=r2[k], in1=lnB[:, 1, :], op=ALU.add),
                   reads=[("xin2", k), ("GB", 1, 0), ("GB", 1, 1)], writes=[("xin2", k)])
                dma("sp", out[j * 128:(j + 1) * 128, :], r2[k], reads=[("xin2", k)], writes=[("out", j)], key=("outd", k))
        S_.wait_all("sp", [("outd", 0), ("outd", 1)] + ([k for k in S_.dma_cnt if str(k).startswith("dbg") or (isinstance(k, tuple) and k[0] == "dbgx1")] if debug else []))
        print("arena peak bytes", AR.peak, "pe", S_.cnt["pe"], "act", S_.cnt["act"], "dve", S_.cnt["dve"], "pool", S_.cnt["pool"],
              "instr", {k: len(v) for k, v in S_.prog.items()})

        with nc.Block() as block:
            @block.tensor
            def _(e):
                S_.emit("pe", e)

            @block.scalar
            def _(e):
                S_.emit("act", e)

            @block.vector
            def _(e):
                S_.emit("dve", e)

            @block.gpsimd
            def _(e):
                S_.emit("pool", e)

            @block.sync
            def _(e):
                S_.emit("sp", e)
    return nc


_CONSTS = None


def _consts():
    global _CONSTS
    if _CONSTS is None:
        s = np.arange(128)[:, None]
        t = np.arange(128)[None, :]
        ident = (s == t).astype(np.float32)
        triS = np.where(s < t, 0.0, NEG).astype(np.float32)
        triI = np.where(s <= t, 0.0, NEG).astype(np.float32)
        negU = np.where(s >= t, -1.0, 0.0).astype(np.float32)
        _CONSTS = np.concatenate([ident, triS, triI, negU], axis=1)
    return _CONSTS


def _rk(w, kc):
    n = w.shape[1]
    return np.ascontiguousarray(w.reshape(kc, 128, n).transpose(1, 0, 2))


def make_in_maps(x, c, w_ada, b_ada, w_in, b_gate, b_forget, w_sb_out, w_fox_out, w_o,
                 ln1_g, ln1_b, w_ffn_gate, w_ffn_up, w_ffn_down, ln2_g, ln2_b):
    f = np.float32
    x = np.asarray(x, f)
    shared = {
        "w_ada": _rk(np.asarray(w_ada, f)[0], KC),
        "b_ada": np.ascontiguousarray(np.asarray(b_ada, f)[0][None, :]),
        "w_in": _rk(np.asarray(w_in, f)[0], KC),
        "b_gate": np.ascontiguousarray(np.asarray(b_gate, f)[0].reshape(16, 128).T),
        "b_forget": np.ascontiguousarray(np.asarray(b_forget, f)[0].reshape(8, 1)),
        "w_sb": _rk(np.asarray(w_sb_out, f)[0], 4),
        "w_fx": _rk(np.asarray(w_fox_out, f)[0], 4),
        "w_o": _rk(np.asarray(w_o, f)[0], KC),
        "lnp": np.ascontiguousarray(np.stack([np.asarray(a, f)[0] for a in (ln1_g, ln1_b, ln2_g, ln2_b)])),
        "w_g": _rk(np.asarray(w_ffn_gate, f)[0], KC),
        "w_u": _rk(np.asarray(w_ffn_up, f)[0], KC),
        "w_d": _rk(np.asarray(w_ffn_down, f)[0], FC),
    }
    cb = _consts()
    maps = []
    for core in range(8):
        b, p = core // 2, core % 2
        if p == 1:
            xa = x[b]
        else:
            xa = np.concatenate([np.zeros((128, D), f), x[b][:S - 128]], axis=0)
        dmask = np.full((128, 512), NEG if p == 0 else 0.0, f)
        m = dict(shared)
        m["xall"] = np.ascontiguousarray(xa)
        m["c_col"] = np.ascontiguousarray(np.asarray(c, f)[b].reshape(KC, 128).T)
        m["consts"] = np.ascontiguousarray(np.concatenate([cb, dmask], axis=1))
        maps.append(m)
    return maps


def assemble(results):
    out = np.zeros((4, S, D), np.float32)
    for core in range(8):
        b, p = core // 2, core % 2
        o = np.asarray(results[core]["out"], np.float32).reshape(NOWN, 128, D)
        ov = out[b].reshape(NB, 128, D)
        for j in range(NOWN):
            ov[2 * j + p] = o[j]
    return out


def kernel(**inputs):
    nc = build_nc(debug=False)
    maps = make_in_maps(**inputs)
    res = run_bass_kernel_spmd(nc, maps, core_ids=list(range(8)))
    return assemble(res.results)
```

```python
import numpy as np
from contextlib import ExitStack
import concourse.bass as bass
import concourse.mybir as mybir
from concourse.bass_utils import run_bass_kernel_spmd

F32 = mybir.dt.float32
BF16 = mybir.dt.bfloat16
AF = mybir.ActivationFunctionType
ALU = mybir.AluOpType

D = 1024
KC = 8
S = 4096
NB = 32
NOWN = 16
DFF = 2816
FC = 22
IN_COLS = 5128
OFF_SB = 0
OFF_FOX = 1536
OFF_FG = 3072
OFF_BG = 3080
LN_EPS = 1e-5
ALPHA = 2.0 ** 0.25
NEG = -30000.0

COMPUTE = ("pe", "act", "dve", "pool")


class Sched:
    def __init__(self, nc, stack, n_dma_sems=64):
        self.nc = nc
        self.prog = {e: [] for e in ("pe", "act", "dve", "pool", "sp")}
        self.cnt = {e: 0 for e in COMPUTE}
        self.sem = {e: stack.enter_context(nc.semaphore("s_" + e)) for e in COMPUTE}
        self.dma_pool = [stack.enter_context(nc.semaphore("d%d" % i)) for i in range(n_dma_sems)]
        self.dma_key = {}
        self.dma_cnt = {}
        self.known = {e: {} for e in self.prog}
        self.lastw = {}
        self.reads = {}

    def _semh(self, k):
        return self.sem[k] if k in self.sem else self.dma_pool[self.dma_key[k]]

    def _deps(self, eng, reads, writes):
        deps = set()
        for b in reads:
            ev = self.lastw.get(b)
            if ev:
                deps.add(ev)
        for b in writes:
            ev = self.lastw.get(b)
            if ev and not (ev[0] == eng and eng in COMPUTE):
                deps.add(ev)
            for ev in self.reads.get(b, ()):
                if not (ev[0] == eng and eng in COMPUTE):
                    deps.add(ev)
        waits = {}
        kn = self.known[eng]
        for (k, v) in deps:
            if k == eng and eng == "pe":
                continue
            if kn.get(k, 0) >= v:
                continue
            if waits.get(k, 0) < v:
                waits[k] = v
        for k, v in waits.items():
            kn[k] = v
        return [(self._semh(k), v) for k, v in waits.items()]

    def _record(self, ev, reads, writes):
        for b in reads:
            self.reads.setdefault(b, []).append(ev)
        for b in writes:
            self.lastw[b] = ev
            self.reads[b] = []

    def op(self, eng, fn, reads=(), writes=(), inc=True):
        waits = self._deps(eng, reads, writes)
        if inc:
            self.cnt[eng] += 1
            ev = (eng, self.cnt[eng])
        else:
            ev = (eng, self.cnt[eng] + 1)
        self._record(ev, reads, writes)
        self.prog[eng].append((waits, fn, (self.sem[eng], 1) if inc else None))

    def dma(self, queue, out, in_, reads=(), writes=(), key=None, **kw):
        assert key is not None
        if key not in self.dma_key:
            idx = len(self.dma_key)
            assert idx < len(self.dma_pool), "out of dma semaphores"
            self.dma_key[key] = idx
            self.dma_cnt[key] = 0
        waits = self._deps(queue, reads, writes)
        self.dma_cnt[key] += 16
        ev = (key, self.dma_cnt[key])
        self._record(ev, reads, writes)
        fn = lambda e, out=out, in_=in_, kw=kw: e.dma_start(out=out, in_=in_, **kw)
        self.prog[queue].append((waits, fn, (self._semh(key), 16)))

    def group_final(self, key, bufs):
        for b in bufs:
            self.lastw[b] = (key, self.dma_cnt[key])

    def wait_all(self, eng, keys):
        waits = []
        for k in keys:
            v = self.dma_cnt[k] if k in self.dma_cnt else self.cnt[k]
            if v > 0:
                waits.append((self._semh(k), v))
        self.prog[eng].append((waits, None, None))

    def barrier(self):
        keys = list(COMPUTE) + list(self.dma_cnt.keys())
        for eng in self.prog:
            waits = []
            for k in keys:
                v = self.dma_cnt[k] if k in self.dma_cnt else self.cnt[k]
                if k == eng or v == 0:
                    continue
                if self.known[eng].get(k, 0) >= v:
                    continue
                self.known[eng][k] = v
                waits.append((self._semh(k), v))
            if waits:
                self.prog[eng].append((waits, None, None))

    def emit(self, eng, e):
        for waits, fn, inc in self.prog[eng]:
            for (h, v) in waits:
                e.wait_ge(h, v)
            if fn is None:
                continue
            ins = fn(e)
            if inc is not None:
                ins.then_inc(inc[0], inc[1])


class Arena:
    def __init__(self, nc, stack, nbytes):
        self.h32 = stack.enter_context(nc.sbuf_tensor("arena", [128, nbytes // 4], F32))
        self.h16 = self.h32.bitcast(BF16)
        self.top = 0
        self.limit = nbytes
        self.peak = 0

    def alloc(self, dtype, *free):
        n = 1
        for f in free:
            n *= f
        sz = 4 if dtype == F32 else 2
        nb = (n * sz + 63) // 64 * 64
        off = self.top
        self.top += nb
        self.peak = max(self.peak, self.top)
        assert self.top <= self.limit, "arena overflow %d > %d" % (self.top, self.limit)
        base = self.h32 if dtype == F32 else self.h16
        e0 = off // sz
        ap = base[:, e0:e0 + n]
        if len(free) == 2:
            ap = ap.rearrange("p (a b) -> p a b", a=free[0], b=free[1])
        elif len(free) == 3:
            ap = ap.rearrange("p (a b c) -> p a b c", a=free[0], b=free[1], c=free[2])
        return ap

    def mark(self):
        return self.top

    def release(self, m):
        self.top = m


def sb_of(g):
    return g // 2 if g % 2 == 0 else 16 + g // 2


def build_nc(debug=False):
    nc = bass.Bass("TRN2", target_bir_lowering=False)
    dt = nc.dram_tensor
    xall = dt("xall", [S, D], F32, kind="ExternalInput").ap()
    c_col = dt("c_col", [128, KC], F32, kind="ExternalInput").ap()
    w_ada = dt("w_ada", [128, KC, 6 * D], F32, kind="ExternalInput").ap()
    b_ada = dt("b_ada", [1, 6 * D], F32, kind="ExternalInput").ap()
    w_in = dt("w_in", [128, KC, IN_COLS], F32, kind="ExternalInput").ap()
    b_gate = dt("b_gate", [128, 16], F32, kind="ExternalInput").ap()
    b_forget = dt("b_forget", [8, 1], F32, kind="ExternalInput").ap()
    w_sb = dt("w_sb", [128, 4, D], F32, kind="ExternalInput").ap()
    w_fx = dt("w_fx", [128, 4, D], F32, kind="ExternalInput").ap()
    w_o = dt("w_o", [128, KC, D], F32, kind="ExternalInput").ap()
    lnp = dt("lnp", [4, D], F32, kind="ExternalInput").ap()
    w_g = dt("w_g", [128, KC, DFF], F32, kind="ExternalInput").ap()
    w_u = dt("w_u", [128, KC, DFF], F32, kind="ExternalInput").ap()
    w_d = dt("w_d", [128, FC, D], F32, kind="ExternalInput").ap()
    consts = dt("consts", [128, 4 * 128 + 512], F32, kind="ExternalInput").ap()
    out = dt("out", [NOWN * 128, D], F32, kind="ExternalOutput").ap()
    fc_scr = dt("fc_scr", [8, 3, S], BF16, kind="Internal").ap()
    x1_scr = dt("x1_scr", [NOWN * 128, D], F32, kind="Internal").ap()
    dbg = {}
    if debug:
        dbg["uT"] = dt("dbg_uT", [128, KC, S], BF16, kind="ExternalOutput").ap()
        dbg["attnT"] = dt("dbg_attnT", [128, KC, NOWN * 128], BF16, kind="ExternalOutput").ap()
        dbg["ada"] = dt("dbg_ada", [128, 48], F32, kind="ExternalOutput").ap()
        dbg["fcn"] = dt("dbg_fcn", [8, S], F32, kind="ExternalOutput").ap()
        dbg["x1"] = dt("dbg_x1", [NOWN * 128, D], F32, kind="ExternalOutput").ap()
        dbg["sml"] = dt("dbg_sml", [128, 4], F32, kind="ExternalOutput").ap()
        dbg["stt"] = dt("dbg_stt", [128, 16], F32, kind="ExternalOutput").ap()
        dbg["xn"] = dt("dbg_xn", [128, D], F32, kind="ExternalOutput").ap()
        dbg["s1col"] = dt("dbg_s1col", [128, 8], F32, kind="ExternalOutput").ap()

    with ExitStack() as stack:
        S_ = Sched(nc, stack)
        AR = Arena(nc, stack, 206 * 1024)
        ps = stack.enter_context(nc.psum_tensor("ps", [128, 8 * 512], F32))

        def bank(i):
            return ps[:, i * 512:(i + 1) * 512]

        op, dma = S_.op, S_.dma

        ident_f = AR.alloc(F32, 128)
        cst_b = AR.alloc(BF16, 4 * 128 + 512)
        ident_b = cst_b[:, 0:128]
        triS_b = cst_b[:, 128:256]
        triI_b = cst_b[:, 256:384]
        negU_b = cst_b[:, 384:512]
        dm_b = cst_b[:, 512:1024]
        negones_b = AR.alloc(BF16, 128)
        adacol = AR.alloc(F32, 48)
        s1col = AR.alloc(F32, 8)
        s2col = AR.alloc(F32, 8)
        bgcol = AR.alloc(F32, 16)
        fcncol = AR.alloc(F32, 256)
        one_f = AR.alloc(F32, 128)
        GB = AR.alloc(F32, 2, D)
        epsc = AR.alloc(F32, 1)
        uT_own = AR.alloc(BF16, KC, NOWN * 128)
        m_attnT = AR.mark()
        attnT = AR.alloc(BF16, KC, NOWN * 128)
        op("dve", lambda e: e.memset(epsc, LN_EPS), writes=["epsc"])

        dma("sp", ident_f, consts[:, 0:128], writes=["ident_f"], key="c0")
        dma("pool", cst_b, consts, writes=["cst_b"], key="c1")
        dma("sp", bgcol, b_gate, writes=["bgcol"], key="c2")
        op("dve", lambda e: e.memset(negones_b, -1.0), writes=["negones"])
        op("dve", lambda e: e.memset(one_f, 1.0), writes=["one_f"])

        m_att = AR.mark()
        uT_oth = AR.alloc(BF16, KC, NOWN * 128)

        def uT_blk(g, kc):
            t = uT_own if g % 2 == 1 else uT_oth
            j = g // 2
            return t[:, kc, j * 128:(j + 1) * 128]

        mA = AR.mark()
        ccol = AR.alloc(F32, KC)
        cact = AR.alloc(F32, KC)
        badg = [AR.alloc(F32, 512) for _ in range(2)]
        adag = [AR.alloc(F32, 512) for _ in range(2)]
        wst = [AR.alloc(F32, KC, 512) for _ in range(2)]
        NXB = 6
        xb = [AR.alloc(F32, D) for _ in range(NXB)]
        xn = [AR.alloc(F32, D) for _ in range(3)]
        sttA = [AR.alloc(F32, 16) for _ in range(4)]
        smlA = [AR.alloc(F32, 4) for _ in range(4)]
        dma("sp", ccol, c_col, writes=["ccol"], key="c3")
        op("act", lambda e: e.activation(out=cact, in_=ccol, func=AF.Silu), reads=["ccol"], writes=["cact"])

        def ada_group(ng):
            sl = ng % 2
            dma("pool", wst[sl], w_ada[:, :, ng * 512:(ng + 1) * 512], writes=[("wst", sl)], key=("wst", sl))
            dma("pool", badg[sl][0:1, :], b_ada[:, ng * 512:(ng + 1) * 512], writes=[("badg", sl)], key=("badg", sl))
            pb = bank(sl)
            for kc in range(KC):
                op("pe", lambda e, pb=pb, kc=kc, sl=sl: e.matmul(pb[0:1, :], cact[:, kc:kc + 1], wst[sl][:, kc, :],
                                                              start=(kc == 0), stop=(kc == KC - 1)),
                   reads=["cact", ("wst", sl)], writes=[("ps", sl)], inc=(kc == KC - 1))
            op("dve", lambda e, pb=pb, sl=sl: e.tensor_tensor(out=adag[sl][0:1, :], in0=pb[0:1, :], in1=badg[sl][0:1, :], op=ALU.add),
               reads=[("ps", sl), ("badg", sl)], writes=[("adag", sl)])
            for c4 in range(4):
                c = ng * 4 + c4
                op("pe", lambda e, c=c, c4=c4, sl=sl: e.matmul(bank(2)[:, c:c + 1], adag[sl][0:1, c4 * 128:(c4 + 1) * 128], one_f[0:1, 0:1],
                                                            start=True, stop=True),
                   reads=[("adag", sl), "one_f"], writes=[("ps", 2)], inc=(c4 == 3))
            if ng in (4, 5, 10, 11):
                i, hf = (0 if ng < 6 else 1), ng % 2
                op("pe", lambda e, sl=sl: e.matmul(bank(3), one_f[0:1, :], adag[sl][0:1, :], start=True, stop=True),
                   reads=[("adag", sl), "one_f"], writes=[("ps", 3)])
                op("dve", lambda e, i=i, hf=hf: e.tensor_copy(out=GB[:, i, hf * 512:(hf + 1) * 512], in_=bank(3)),
                   reads=[("ps", 3)], writes=[("GB", i, hf)])

        for ng in range(4):
            ada_group(ng)
        op("dve", lambda e: e.tensor_copy(out=adacol[:, 0:16], in_=bank(2)[:, 0:16]), reads=[("ps", 2)], writes=["adacol"])
        op("dve", lambda e: e.tensor_scalar(out=s1col, in0=adacol[:, 8:16], scalar1=1.0, scalar2=None, op0=ALU.add),
           reads=["adacol"], writes=["s1col"])

        def A0(g):
            kx = g % NXB
            dma("sp", xb[kx], xall[g * 128:(g + 1) * 128, :], writes=[("xb", kx)], key=("xb", kx))

        def A1(g):
            k4 = g % 4
            kx = g % NXB
            xt, st = xb[kx], sttA[k4]
            op("dve", lambda e: e.bn_stats(out=st[:, 0:6], in_=xt[:, 0:512]), reads=[("xb", kx)], writes=[("st", k4, 0)])
            op("dve", lambda e: e.bn_stats(out=st[:, 6:12], in_=xt[:, 512:1024]), reads=[("xb", kx)], writes=[("st", k4, 1)])
            op("dve", lambda e: e.bn_aggr(out=st[:, 12:14], in_=st[:, 0:12]), reads=[("st", k4, 0), ("st", k4, 1)],
               writes=[("mv", k4)])
            sm = smlA[k4]
            op("act", lambda e: e.activation(out=sm[:, 0:1], in_=st[:, 13:14], func=AF.Sqrt, bias=epsc[:, 0:1], scale=1.0),
               reads=[("mv", k4), "epsc"], writes=[("sd", k4)])

        def A3(g):
            k4 = g % 4
            kx = g % NXB
            st, sm, xt = sttA[k4], smlA[k4], xb[kx]
            xo = xn[g % 3]
            op("dve", lambda e: e.reciprocal(out=sm[:, 1:2], in_=sm[:, 0:1]), reads=[("sd", k4)], writes=[("rstd", k4)])
            op("dve", lambda e: e.tensor_scalar(out=sm[:, 2:3], in0=st[:, 12:13], scalar1=-1.0, scalar2=sm[:, 1:2],
                                                op0=ALU.mult, op1=ALU.mult),
               reads=[("mv", k4), ("rstd", k4)], writes=[("nmr", k4)])
            op("act", lambda e: e.activation(out=xo, in_=xt, func=AF.Identity, bias=sm[:, 2:3], scale=sm[:, 1:2]),
               reads=[("xb", kx), ("rstd", k4), ("nmr", k4)], writes=[("xn", g % 3)])

        def A5(g):
            k = g % 2
            xo = xn[g % 3]
            for kc in range(KC):
                bk = 4 + 2 * k + kc // 4
                op("pe", lambda e, bk=bk, kc=kc: e.transpose(bank(bk)[:, (kc % 4) * 128:(kc % 4 + 1) * 128],
                                                         xo[:, kc * 128:(kc + 1) * 128], ident_f),
                   reads=[("xn", g % 3), "ident_f"], writes=[("ps", bk)], inc=(kc % 4 == 3))

        def A6(g):
            k = g % 2
            for kc in range(KC):
                bk = 4 + 2 * k + kc // 4
                src = bank(bk)[:, (kc % 4) * 128:(kc % 4 + 1) * 128]
                if kc // 4 == 0:
                    op("dve", lambda e, src=src, kc=kc: e.tensor_scalar(
                        out=uT_blk(g, kc), in0=src, scalar1=s1col[:, kc:kc + 1], scalar2=adacol[:, kc:kc + 1],
                        op0=ALU.mult, op1=ALU.add),
                       reads=[("ps", bk), "s1col", "adacol"], writes=[("uT", g)])
                else:
                    op("act", lambda e, src=src, kc=kc: e.activation(out=uT_blk(g, kc), in_=src, func=AF.Identity,
                                                                 bias=adacol[:, kc:kc + 1], scale=s1col[:, kc:kc + 1]),
                       reads=[("ps", bk), "s1col", "adacol"], writes=[("uT", g)])

        for g_ in range(4):
            A0(g_)
        for s_ in range(NB + 3):
            if s_ < NB:
                A1(s_)
            if s_ + 4 < NB:
                A0(s_ + 4)
            if 0 <= s_ - 1 < NB:
                A3(s_ - 1)
            if 0 <= s_ - 2 < NB:
                A5(s_ - 2)
            if 0 <= s_ - 3 < NB:
                A6(s_ - 3)
            if s_ % 4 == 1 and s_ // 4 < 8:
                ada_group(4 + s_ // 4)
        op("dve", lambda e: e.tensor_copy(out=adacol[:, 16:48], in_=bank(2)[:, 16:48]), reads=[("ps", 2)], writes=["adacol2"])
        op("dve", lambda e: e.tensor_scalar(out=s2col, in0=adacol[:, 32:40], scalar1=1.0, scalar2=None, op0=ALU.add),
           reads=["adacol2"], writes=["s2col"])
        if debug:
            dma("sp", dbg["ada"], adacol, reads=["adacol", "adacol2"], key="dbg1")
            dma("sp", dbg["uT"][:, :, 0:2048], uT_oth, reads=[("uT", g) for g in range(0, NB, 2)], key="dbg6")
            dma("sp", dbg["uT"][:, :, 2048:4096], uT_own, reads=[("uT", g) for g in range(1, NB, 2)], key="dbg7")
        S_.barrier()
        AR.release(mA)

        mF = AR.mark()
        wf_b = AR.alloc(BF16, KC, 8)
        nbf = AR.alloc(F32, 2)
        spf = AR.alloc(F32, S)
        fcn = AR.alloc(F32, S)
        fc3 = AR.alloc(BF16, 3, S)
        etmp = [AR.alloc(F32, 512) for _ in range(2)]
        dma("pool", wf_b, w_in[:, :, OFF_FG:OFF_FG + 8], writes=["wf_b"], key="c5")
        dma("sp", nbf[0:8, 0:1], b_forget, writes=["nbf0"], key="c6")
        op("dve", lambda e: e.tensor_scalar(out=nbf[0:8, 1:2], in0=nbf[0:8, 0:1], scalar1=-1.0, scalar2=None, op0=ALU.mult),
           reads=["nbf0"], writes=["nbf"])
        for pg in range(8):
            bk = pg % 2
            for blk in range(4):
                g = pg * 4 + blk
                for kc in range(KC):
                    op("pe", lambda e, bk=bk, blk=blk, g=g, kc=kc: e.matmul(
                        bank(bk)[0:8, blk * 128:(blk + 1) * 128], wf_b[:, kc, :], uT_blk(g, kc),
                        start=(kc == 0), stop=(kc == KC - 1)),
                       reads=["wf_b", ("uT", g)], writes=[("ps", bk)], inc=(blk == 3 and kc == KC - 1))
            op("act", lambda e, bk=bk: e.activation(out=etmp[bk][0:8, :], in_=bank(bk)[0:8, :], func=AF.Exp,
                                                    bias=nbf[0:8, 1:2], scale=-1.0),
               reads=[("ps", bk), "nbf"], writes=[("etmp", bk)])
            op("act", lambda e, bk=bk, pg=pg: e.activation(out=spf[0:8, pg * 512:(pg + 1) * 512], in_=etmp[bk][0:8, :],
                                                           func=AF.Ln, bias=1.0, scale=1.0),
               reads=[("etmp", bk)], writes=["spf"])
        op("dve", lambda e: e.tensor_tensor_scan(out=fcn[0:8, :], data0=spf[0:8, :], data1=spf[0:8, :], initial=0.0,
                                                 op0=ALU.add, op1=ALU.max),
           reads=["spf"], writes=["fcn"])
        if debug:
            dma("sp", dbg["fcn"], fcn[0:8, :], reads=["fcn"], key="dbg8")
        for g in range(NB):
            op("pe", lambda e, g=g: e.transpose(bank(2)[:, g * 8:(g + 1) * 8], fcn[0:8, g * 128:(g + 1) * 128], ident_f[0:8, 0:8]),
               reads=["fcn", "ident_f"], writes=[("ps", 2)], inc=(g == NB - 1))
        op("dve", lambda e: e.tensor_copy(out=fcncol, in_=bank(2)[:, 0:256]), reads=[("ps", 2)], writes=["fcncol"])
        op("dve", lambda e: e.tensor_copy(out=fc3[0:8, 0, :], in_=fcn[0:8, :]), reads=["fcn"], writes=[("fc3", 0)])
        op("dve", lambda e: e.tensor_tensor(out=spf[0:8, :], in0=fcn[0:8, :], in1=fc3[0:8, 0, :], op=ALU.subtract),
           reads=["fcn", ("fc3", 0), ("ps", 2)], writes=["spf"])
        op("dve", lambda e: e.tensor_copy(out=fc3[0:8, 1, :], in_=spf[0:8, :]), reads=["spf"], writes=[("fc3", 1)])
        op("dve", lambda e: e.tensor_tensor(out=fcn[0:8, :], in0=spf[0:8, :], in1=fc3[0:8, 1, :], op=ALU.subtract),
           reads=["spf", ("fc3", 1), "fcncol"], writes=["fcn"])
        op("dve", lambda e: e.tensor_copy(out=fc3[0:8, 2, :], in_=fcn[0:8, :]), reads=["fcn"], writes=[("fc3", 2)])
        dma("sp", fc_scr, fc3[0:8, :, :], reads=[("fc3", 0), ("fc3", 1), ("fc3", 2)], writes=["fc_scr"], key="c7")
        S_.barrier()
        AR.release(mF)

        wp = [AR.alloc(BF16, KC, 384) for _ in range(2)]
        KT = AR.alloc(BF16, S)
        KTB = AR.alloc(BF16, S)
        QT = AR.alloc(BF16, NOWN * 128)
        QTB = AR.alloc(BF16, NOWN * 128)
        VT = AR.alloc(BF16, NB, 2, 128)
        eb = [[AR.alloc(F32, 512) for _ in range(2)] for _ in range(2)]
        spb = [[AR.alloc(BF16, 512) for _ in range(2)] for _ in range(2)]
        Ab = [[AR.alloc(BF16, 512) for _ in range(2)] for _ in range(2)]
        Rb = [AR.alloc(BF16, 512) for _ in range(2)]
        rden = [AR.alloc(F32, 512) for _ in range(2)]

        def load_w(pr):
            fox = pr >= 4
            hp = pr % 4
            base = OFF_FOX if fox else OFF_SB
            w = wp[pr % 2]
            wk = ("wp", pr % 2)
            for i in range(3):
                c0 = base + i * 512 + hp * 128
                dma("pool", w[:, :, i * 128:(i + 1) * 128], w_in[:, :, c0:c0 + 128], writes=[(wk, i)], key=(wk, i))

        def sb_set(pr):
            if pr % 2 == 0:
                return KT, QT, 0, "KT", "QT", "VT0"
            return KTB, QTB, 1, "KTB", "QTB", "VT1"

        def proj_pair(pr):
            fox = pr >= 4
            hp = pr % 4
            w = wp[pr % 2]
            wk = ("wp", pr % 2)
            if not fox:
                kt_d, qt_d, vi, kkey, qkey, vkey = sb_set(pr)
            for sg in range(8):
                src = uT_oth if sg < 4 else uT_own
                cs = (sg % 4) * 512
                bk = sg % 2
                rk = [("uT", g) for g in range(NB) if sb_of(g) // 4 == sg]
                for kc in range(KC):
                    op("pe", lambda e, bk=bk, kc=kc, src=src, cs=cs: e.matmul(bank(bk), w[:, kc, 128:256], src[:, kc, cs:cs + 512],
                                                                          start=(kc == 0), stop=(kc == KC - 1)),
                       reads=[(wk, 1)] + rk, writes=[("ps", bk)], inc=(kc == KC - 1))
                if not fox:
                    op("dve", lambda e, bk=bk, sg=sg: e.tensor_copy(out=kt_d[:, sg * 512:(sg + 1) * 512], in_=bank(bk)),
                       reads=[("ps", bk)], writes=[(kkey, sg)])
                else:
                    op("dve", lambda e, bk=bk, sg=sg: e.tensor_copy(out=KT[0:64, sg * 512:(sg + 1) * 512], in_=bank(bk)[0:64, :]),
                       reads=[("ps", bk)], writes=[("KT", sg)])
                    op("dve", lambda e, bk=bk, sg=sg: e.tensor_copy(out=KTB[64:128, sg * 512:(sg + 1) * 512], in_=bank(bk)[64:128, :]),
                       reads=[("ps", bk)], writes=[("KTB", sg)])
                yield
            for G in range(4):
                bk = G % 2
                rk = [("uT", 2 * j + 1) for j in range(4 * G, 4 * G + 4)]
                for kc in range(KC):
                    op("pe", lambda e, bk=bk, kc=kc, G=G: e.matmul(bank(bk), w[:, kc, 0:128], uT_own[:, kc, G * 512:(G + 1) * 512],
                                                                start=(kc == 0), stop=(kc == KC - 1)),
                       reads=[(wk, 0)] + rk, writes=[("ps", bk)], inc=(kc == KC - 1))
                if not fox:
                    op("dve", lambda e, bk=bk, G=G: e.tensor_scalar(out=qt_d[:, G * 512:(G + 1) * 512], in0=bank(bk), scalar1=0.125,
                                                                    scalar2=None, op0=ALU.mult),
                       reads=[("ps", bk)], writes=[(qkey, G)])
                else:
                    op("dve", lambda e, bk=bk, G=G: e.tensor_scalar(out=QT[0:64, G * 512:(G + 1) * 512], in0=bank(bk)[0:64, :],
                                                                    scalar1=0.125, scalar2=None, op0=ALU.mult),
                       reads=[("ps", bk)], writes=[("QT", G)])
                    op("dve", lambda e, bk=bk, G=G: e.tensor_scalar(out=QTB[64:128, G * 512:(G + 1) * 512], in0=bank(bk)[64:128, :],
                                                                    scalar1=0.125, scalar2=None, op0=ALU.mult),
                       reads=[("ps", bk)], writes=[("QTB", G)])
                yield
            for sg in range(8):
                bk = sg % 2
                for blk in range(4):
                    sbk = sg * 4 + blk
                    g = 2 * sbk if sbk < 16 else 2 * (sbk - 16) + 1
                    for kc in range(KC):
                        op("pe", lambda e, bk=bk, blk=blk, g=g, kc=kc: e.matmul(
                            bank(bk)[:, blk * 128:(blk + 1) * 128], uT_blk(g, kc), w[:, kc, 256:384],
                            start=(kc == 0), stop=(kc == KC - 1)),
                           reads=[(wk, 2), ("uT", g)], writes=[("ps", bk)], inc=(blk == 3 and kc == KC - 1))
                    if blk == 1:
                        yield
                if not fox:
                    op("dve", lambda e, bk=bk, sg=sg: e.tensor_copy(
                        out=VT[:, sg * 4:(sg + 1) * 4, vi, :], in_=bank(bk).rearrange("p (a b) -> p a b", a=4, b=128)),
                       reads=[("ps", bk)], writes=[(vkey, sg)])
                else:
                    op("dve", lambda e, bk=bk, sg=sg: e.tensor_copy(
                        out=VT[:, sg * 4:(sg + 1) * 4, :, 0:64],
                        in_=bank(bk).rearrange("p (a h d) -> p a h d", a=4, h=2, d=64)),
                       reads=[("ps", bk)], writes=[("VT0", sg), ("VT1", sg)])
                yield
            if fox:
                for h in range(2):
                    hg = hp * 2 + h
                    dst = QT[64:67, :] if h == 0 else QTB[0:3, :]
                    srcap = fc_scr[hg].rearrange("r (j two c) -> r j two c", two=2, c=128)[:, :, 1, :]
                    dma("sp", dst.rearrange("r (j c) -> r j c", c=128), srcap, reads=["fc_scr"],
                        writes=[("QTaug", h)] + [(("QT" if h == 0 else "QTB"), G_) for G_ in range(4)], key=("qaug", h))

        def grp_cols(G, g):
            d = g - 8 * G
            return (d // 2) * 128 if d >= 0 else 0

        hcount = [0, 0]

        def attn_sb(pr, bg=None):
            kt_d, qt_d, vi, kkey, qkey, vkey = sb_set(pr)
            steps = []
            for G in range(4):
                gs = list(range(8 * G + 7, -1, -1))
                for it, g in enumerate(gs):
                    for h in range(2):
                        par = hcount[h] % 2
                        hcount[h] += 1
                        steps.append(dict(G=G, it=it, g=g, h=h, par=par, last=(g == 0), c0=grp_cols(G, g), b=sb_of(g)))
            N = len(steps)

            def bufs(st):
                h, par = st["h"], st["par"]
                return bank(2 + 2 * h + par), ("ps", 2 + 2 * h + par), bank(6 + h), ("ps", 6 + h)

            def QK(st):
                G, g, h, par, c0, b = st["G"], st["g"], st["h"], st["par"], st["c0"], st["b"]
                zb, zk, ob, ok_ = bufs(st)
                d = g - 8 * G
                rs = slice(h * 64, (h + 1) * 64)
                qsl = slice(G * 512 + c0, G * 512 + 512)
                has_tri = d >= 1 and d % 2 == 1
                has_dm = g == 0
                op("pe", lambda e, st_=not (has_tri or has_dm): e.matmul(
                    zb[:, c0:512], kt_d[rs, b * 128:(b + 1) * 128], qt_d[rs, qsl], start=True, stop=st_),
                   reads=[(kkey, b // 4), (qkey, G)], writes=[zk], inc=not (has_tri or has_dm))
                if has_tri:
                    op("pe", lambda e: e.matmul(zb[:, c0:c0 + 128], ident_b, triS_b, start=False, stop=not has_dm),
                       reads=["cst_b"], writes=[zk], inc=not has_dm)
                if has_dm:
                    op("pe", lambda e: e.matmul(zb[:, c0:512], ident_b, dm_b[:, c0:512], start=False, stop=True),
                       reads=["cst_b"], writes=[zk])

            def EL(st):
                h, par, c0 = st["h"], st["par"], st["c0"]
                zb, zk, ob, ok_ = bufs(st)
                e_, sp_ = eb[h][par], spb[h][par]
                op("act", lambda e: e.activation(out=e_[:, c0:512], in_=zb[:, c0:512], func=AF.Exp),
                   reads=[zk], writes=[("e", h, par)])
                op("act", lambda e: e.activation(out=sp_[:, c0:512], in_=e_[:, c0:512], func=AF.Ln, bias=1.0, scale=1.0),
                   reads=[("e", h, par)], writes=[("sp", h, par)])

            def CUM(st):
                h, par, c0, it = st["h"], st["par"], st["c0"], st["it"]
                zb, zk, ob, ok_ = bufs(st)
                sp_ = spb[h][par]
                op("pe", lambda e: e.matmul(zb[:, c0:512], negU_b, sp_[:, c0:512], start=False, stop=True, skip_group_check=True),
                   reads=["cst_b", ("sp", h, par)], writes=[zk], inc=(it == 0))
                if it > 0:
                    op("pe", lambda e: e.matmul(zb[:, c0:512], negones_b, Rb[h][:, c0:512], start=False, stop=True,
                                                skip_group_check=True),
                       reads=["negones", ("R", h)], writes=[zk])

            def AR_(st):
                h, par, c0, it = st["h"], st["par"], st["c0"], st["it"]
                zb, zk, ob, ok_ = bufs(st)
                A_, sp_ = Ab[h][par], spb[h][par]
                op("act", lambda e: e.activation(out=A_[:, c0:512], in_=zb[:, c0:512], func=AF.Exp),
                   reads=[zk], writes=[("A", h, par)])
                if it == 0:
                    op("pool", lambda e: e.memset(Rb[h], 0.0), writes=[("R", h)])
                if not st["last"]:
                    op("pool", lambda e: e.tensor_tensor(out=Rb[h][:, c0:512], in0=Rb[h][:, c0:512], in1=sp_[:, c0:512], op=ALU.add),
                       reads=[("R", h), ("sp", h, par)], writes=[("R", h)])

            def AV(st):
                G, h, par, c0, it, b = st["G"], st["h"], st["par"], st["c0"], st["it"], st["b"]
                zb, zk, ob, ok_ = bufs(st)
                A_ = Ab[h][par]
                op("pe", lambda e: e.matmul(ob[0:64, c0:512], VT[:, b, vi, h * 64:(h + 1) * 64], A_[:, c0:512],
                                            start=(it == 0), stop=st["last"], skip_group_check=True),
                   reads=[(vkey, b // 4), ("A", h, par)], writes=[ok_], inc=True)
                if st["last"]:
                    op("act", lambda e: e.activation(out=attnT[h * 64:(h + 1) * 64, pr, G * 512:(G + 1) * 512],
                                                     in_=ob[0:64, :], func=AF.Copy),
                       reads=[ok_], writes=[("attnT", pr, G)])

            for s in range(-2, N + 2):
                if 0 <= s - 1 < N:
                    CUM(steps[s - 1])
                if 0 <= s - 2 < N:
                    AV(steps[s - 2])
                if 0 <= s + 2 < N:
                    QK(steps[s + 2])
                if 0 <= s < N:
                    EL(steps[s])
                if 0 <= s - 1 < N:
                    AR_(steps[s - 1])
                if bg is not None and s % 3 == 0:
                    try:
                        next(bg)
                    except StopIteration:
                        bg = None
            if bg is not None:
                for _ in bg:
                    pass

        def attn_fox(pr):
            hp = pr % 4
            steps = []
            for G in range(4):
                gs = list(range(8 * G + 7, -1, -1))
                for it, g in enumerate(gs):
                    for h in range(2):
                        par = hcount[h] % 2
                        hcount[h] += 1
                        steps.append(dict(G=G, it=it, g=g, h=h, par=par, last=(g == 0), c0=grp_cols(G, g), b=sb_of(g)))
            N = len(steps)

            def bufs(st):
                h, par = st["h"], st["par"]
                return bank(2 + 2 * h + par), ("ps", 2 + 2 * h + par), bank(6 + h), ("ps", 6 + h)

            def QK(st):
                G, g, h, par, c0, b = st["G"], st["g"], st["h"], st["par"], st["c0"], st["b"]
                zb, zk, ob, ok_ = bufs(st)
                d = g - 8 * G
                qsl = slice(G * 512 + c0, G * 512 + 512)
                has_tri = d >= 1 and d % 2 == 1
                has_dm = g == 0
                if h == 0:
                    kap, qap = KT[0:67, b * 128:(b + 1) * 128], QT[0:67, qsl]
                    rds = [("KT", b // 4), ("QT", G), ("QTaug", 0), "kaug"]
                else:
                    kap, qap = KTB[:, b * 128:(b + 1) * 128], QTB[:, qsl]
                    rds = [("KTB", b // 4), ("QTB", G), ("QTaug", 1), "kaug"]
                op("pe", lambda e, st_=not (has_tri or has_dm): e.matmul(zb[:, c0:512], kap, qap, start=True, stop=st_),
                   reads=rds, writes=[zk], inc=not (has_tri or has_dm))
                if has_tri:
                    op("pe", lambda e: e.matmul(zb[:, c0:c0 + 128], ident_b, triI_b, start=False, stop=not has_dm),
                       reads=["cst_b"], writes=[zk], inc=not has_dm)
                if has_dm:
                    op("pe", lambda e: e.matmul(zb[:, c0:512], ident_b, dm_b[:, c0:512], start=False, stop=True),
                       reads=["cst_b"], writes=[zk])

            def PX(st):
                g, h, par, c0 = st["g"], st["h"], st["par"], st["c0"]
                zb, zk, ob, ok_ = bufs(st)
                hg = hp * 2 + h
                A_ = Ab[h][par]
                op("act", lambda e: e.activation(out=A_[:, c0:512], in_=zb[:, c0:512], func=AF.Exp,
                                                 bias=fcncol[:, g * 8 + hg:g * 8 + hg + 1], scale=1.0),
                   reads=[zk, "fcncol"], writes=[("A", h, par)])

            def AV(st):
                G, h, par, c0, it, b = st["G"], st["h"], st["par"], st["c0"], st["it"], st["b"]
                zb, zk, ob, ok_ = bufs(st)
                A_ = Ab[h][par]
                op("pe", lambda e: e.matmul(ob[:, c0:512], VT[:, b, h, :], A_[:, c0:512], start=(it == 0), stop=st["last"],
                                            skip_group_check=True),
                   reads=[("VT0", b // 4), ("VT1", b // 4), ("A", h, par), "vones"], writes=[ok_], inc=True)
                if st["last"]:
                    op("dve", lambda e: e.reciprocal(out=rden[h][64:128, :], in_=ob[64:128, :]),
                       reads=[ok_], writes=[("rden", h)])
                    op("dve", lambda e: e.tensor_tensor(out=attnT[h * 64:(h + 1) * 64, pr, G * 512:(G + 1) * 512],
                                                        in0=ob[0:64, :], in1=rden[h][64:128, :], op=ALU.mult),
                       reads=[ok_, ("rden", h)], writes=[("attnT", pr, G)])

            for s in range(-2, N + 2):
                if 0 <= s - 2 < N:
                    AV(steps[s - 2])
                if 0 <= s + 2 < N:
                    QK(steps[s + 2])
                if 0 <= s < N:
                    PX(steps[s])

        load_w(0)
        for _ in proj_pair(0):
            pass
        for pr in range(8):
            if pr + 1 < 8:
                load_w(pr + 1)
            if pr < 4:
                attn_sb(pr, proj_pair(pr + 1) if pr + 1 < 4 else None)
            else:
                if pr == 4:
                    op("pool", lambda e: e.memset(KT[64:67, :], -1.0), writes=[("KT", i) for i in range(8)] + ["kaug"])
                    op("pool", lambda e: e.memset(KTB[0:64, :], 0.0), writes=[("KTB", i) for i in range(8)] + ["kaug0"])
                    op("pool", lambda e: e.memset(KTB[0:3, :], -1.0), reads=["kaug0"], writes=["kaug"])
                    op("pool", lambda e: e.memset(QTB[0:64, :], 0.0), writes=[("QTaug", 1)] + [("QTB", i) for i in range(4)])
                    op("pool", lambda e: e.memset(VT[:, :, :, 64:128], 1.0),
                       writes=[("VT0", i) for i in range(8)] + [("VT1", i) for i in range(8)] + ["vones"])
                for _ in proj_pair(pr):
                    pass
                attn_fox(pr)
        if debug:
            dma("sp", dbg["attnT"], attnT, reads=[("attnT", pr, G) for pr in range(8) for G in range(4)], key="dbg9")
        S_.barrier()
        AR.release(m_att)

        mC1 = AR.mark()
        wbg = AR.alloc(BF16, KC, 2 * D)
        wsb_b = AR.alloc(BF16, 4, D)
        wfx_b = AR.alloc(BF16, 4, D)
        wo_b = AR.alloc(BF16, KC, D)
        lnB1 = AR.alloc(F32, 2, D)
        gT = [AR.alloc(BF16, 512) for _ in range(2)]
        tmpa = [AR.alloc(F32, 512) for _ in range(2)]
        tmpb = [AR.alloc(F32, 512) for _ in range(2)]
        mT = [AR.alloc(BF16, KC, 512) for _ in range(2)]
        xinB = [AR.alloc(F32, D) for _ in range(2)]
        r1 = [AR.alloc(F32, D) for _ in range(2)]
        x1b = [AR.alloc(F32, D) for _ in range(2)]
        sttB = [AR.alloc(F32, 16) for _ in range(4)]
        smlB = [AR.alloc(F32, 4) for _ in range(4)]

        for hf in range(2):
            dma("pool", wbg[:, :, hf * D:(hf + 1) * D], w_in[:, :, OFF_BG + hf * D:OFF_BG + (hf + 1) * D], writes=[("wbg", hf)],
                key=("w0", hf))
        dma("pool", wsb_b, w_sb, writes=["wsb"], key="w1")
        dma("pool", wfx_b, w_fx, writes=["wfx"], key="w2")
        for i in range(2):
            dma("sp", lnB1[:, i, :], lnp[i:i + 1, :].partition_broadcast(128), writes=[("lnB1", i)], key=("lnB1", i))
        for kc in range(KC):
            st_ = xinB[kc % 2]
            dma("sp", st_, w_o[:, kc, :], writes=[("xinB", kc % 2)], key=("xinB", kc % 2))
            op("dve", lambda e, st_=st_, kc=kc: e.tensor_tensor(out=wo_b[:, kc, :], in0=st_, in1=GB[:, 0, :], op=ALU.mult),
               reads=[("xinB", kc % 2), ("GB", 0, 0), ("GB", 0, 1)], writes=[("wo", kc)])

        def own_rows(j):
            g = 2 * j + 1
            return slice(g * 128, (g + 1) * 128)

        def nloop_thread(G):
            ukeys = [("uT", 2 * j + 1) for j in range(4 * G, 4 * G + 4)]
            mt = mT[G % 2]
            for n in range(8):
                par = n % 2
                for w_ in range(2):
                    col = w_ * D + n * 128
                    for kc in range(KC):
                        op("pe", lambda e, w_=w_, kc=kc, col=col: e.matmul(bank(w_), wbg[:, kc, col:col + 128],
                                                                        uT_own[:, kc, G * 512:(G + 1) * 512],
                                                                        start=(kc == 0), stop=(kc == KC - 1)),
                           reads=[("wbg", w_)] + ukeys, writes=[("ps", w_)], inc=(kc == KC - 1))
                    op("act", lambda e, w_=w_, n=n: e.activation(out=gT[w_], in_=bank(w_), func=AF.Sigmoid,
                                                                bias=bgcol[:, w_ * 8 + n:w_ * 8 + n + 1], scale=1.0),
                       reads=[("ps", w_), "bgcol"], writes=[("gT", w_)])
                for w_, wt in enumerate((wsb_b, wfx_b)):
                    for kc in range(4):
                        op("pe", lambda e, kc=kc, wt=wt, w_=w_, n=n: e.matmul(
                            bank(2 + w_), wt[:, kc, n * 128:(n + 1) * 128], attnT[:, w_ * 4 + kc, G * 512:(G + 1) * 512],
                            start=(kc == 0), stop=(kc == 3)),
                           reads=["wsb", "wfx"] + [("attnT", w_ * 4 + kc, G)], writes=[("ps", 2 + w_)], inc=(kc == 3))
                op("dve", lambda e, par=par: e.tensor_tensor(out=tmpa[par], in0=bank(2), in1=gT[0], op=ALU.mult),
                   reads=[("ps", 2), ("gT", 0)], writes=[("tmpa", par)])
                op("dve", lambda e, par=par: e.tensor_tensor(out=tmpb[par], in0=bank(3), in1=gT[1], op=ALU.mult),
                   reads=[("ps", 3), ("gT", 1)], writes=[("tmpb", par)])
                op("pool", lambda e, par=par, n=n: e.tensor_tensor(out=mt[:, n, :], in0=tmpa[par], in1=tmpb[par], op=ALU.add),
                   reads=[("tmpa", par), ("tmpb", par)], writes=[("mT", G % 2, n)])
                yield

        def stats_ops(src, rks, k4):
            st, sm = sttB[k4], smlB[k4]
            op("dve", lambda e: e.bn_stats(out=st[:, 0:6], in_=src[:, 0:512]), reads=rks, writes=[("st", k4, 0)])
            op("dve", lambda e: e.bn_stats(out=st[:, 6:12], in_=src[:, 512:1024]), reads=rks, writes=[("st", k4, 1)])
            op("dve", lambda e: e.bn_aggr(out=st[:, 12:14], in_=st[:, 0:12]), reads=[("st", k4, 0), ("st", k4, 1)],
               writes=[("mv", k4)])
            op("act", lambda e: e.activation(out=sm[:, 0:1], in_=st[:, 13:14], func=AF.Sqrt, bias=epsc[:, 0:1], scale=1.0),
               reads=[("mv", k4), "epsc"], writes=[("sd", k4)])

        def rstd_ops(k4):
            st, sm = sttB[k4], smlB[k4]
            op("dve", lambda e: e.reciprocal(out=sm[:, 1:2], in_=sm[:, 0:1]), reads=[("sd", k4)], writes=[("rstd", k4)])
            op("dve", lambda e: e.tensor_scalar(out=sm[:, 2:3], in0=st[:, 12:13], scalar1=-1.0, scalar2=sm[:, 1:2],
                                                op0=ALU.mult, op1=ALU.mult),
               reads=[("mv", k4), ("rstd", k4)], writes=[("nmr", k4)])

        def tile_thread(j):
            G, i, k = j // 4, j % 4, j % 2
            ka, kb = (2 * j) % 4, (2 * j + 1) % 4
            mt = mT[G % 2]
            dma("sp", xinB[k], xall[own_rows(j), :], writes=[("xinB", k)], key=("xinB", k))
            for hf in range(2):
                for kc in range(KC):
                    op("pe", lambda e, kc=kc, hf=hf: e.matmul(bank(4 + 2 * k + hf), mt[:, kc, i * 128:(i + 1) * 128],
                                                           wo_b[:, kc, hf * 512:(hf + 1) * 512],
                                                           start=(kc == 0), stop=(kc == KC - 1)),
                       reads=[("mT", G % 2, kc), ("wo", kc)], writes=[("ps", 4 + 2 * k + hf)], inc=(kc == KC - 1))
            yield
            for hf in range(2):
                op("dve", lambda e, hf=hf: e.scalar_tensor_tensor(
                    out=r1[k][:, hf * 512:(hf + 1) * 512], in0=xinB[k][:, hf * 512:(hf + 1) * 512], scalar=ALPHA,
                    in1=bank(4 + 2 * k + hf), op0=ALU.mult, op1=ALU.add),
                   reads=[("xinB", k), ("ps", 4 + 2 * k + hf)], writes=[("r1", k)])
            stats_ops(r1[k], [("r1", k)], ka)
            yield
            rstd_ops(ka)
            sm = smlB[ka]
            op("act", lambda e: e.activation(out=r1[k], in_=r1[k], func=AF.Identity, bias=sm[:, 2:3], scale=sm[:, 1:2]),
               reads=[("r1", k), ("rstd", ka), ("nmr", ka)], writes=[("r1", k)])
            yield
            op("pool", lambda e: e.tensor_tensor(out=x1b[k], in0=r1[k], in1=lnB1[:, 0, :], op=ALU.mult),
               reads=[("r1", k), ("lnB1", 0)], writes=[("x1", k)])
            op("pool", lambda e: e.tensor_tensor(out=x1b[k], in0=x1b[k], in1=lnB1[:, 1, :], op=ALU.add),
               reads=[("x1", k), ("lnB1", 1)], writes=[("x1", k)])
            yield
            dma("sp", x1_scr[j * 128:(j + 1) * 128, :], x1b[k], reads=[("x1", k)], writes=[("x1scr", j)], key=("x1o", k))
            if debug:
                dma("sp", dbg["x1"][j * 128:(j + 1) * 128, :], x1b[k], reads=[("x1", k)], key=("dbgx1", k))
            stats_ops(x1b[k], [("x1", k)], kb)
            yield
            rstd_ops(kb)
            sm2 = smlB[kb]
            op("act", lambda e: e.activation(out=r1[k], in_=x1b[k], func=AF.Identity, bias=sm2[:, 2:3], scale=sm2[:, 1:2]),
               reads=[("x1", k), ("rstd", kb), ("nmr", kb)], writes=[("r1", k)])
            yield
            for kc in range(KC):
                bk = 4 + 2 * k + kc // 4
                op("pe", lambda e, bk=bk, kc=kc: e.transpose(bank(bk)[:, (kc % 4) * 128:(kc % 4 + 1) * 128],
                                                         r1[k][:, kc * 128:(kc + 1) * 128], ident_f),
                   reads=[("r1", k), "ident_f"], writes=[("ps", bk)], inc=(kc % 4 == 3))
            yield
            for kc in range(KC):
                bk = 4 + 2 * k + kc // 4
                src = bank(bk)[:, (kc % 4) * 128:(kc % 4 + 1) * 128]
                if kc // 4 == 0:
                    op("dve", lambda e, src=src, kc=kc: e.tensor_scalar(
                        out=uT_own[:, kc, j * 128:(j + 1) * 128], in0=src,
                        scalar1=s2col[:, kc:kc + 1], scalar2=adacol[:, 24 + kc:25 + kc], op0=ALU.mult, op1=ALU.add),
                       reads=[("ps", bk), "s2col", "adacol2"], writes=[("uT", 2 * j + 1)])
                else:
                    op("act", lambda e, src=src, kc=kc: e.activation(out=uT_own[:, kc, j * 128:(j + 1) * 128], in_=src, func=AF.Identity,
                                                                 bias=adacol[:, 24 + kc:25 + kc], scale=s2col[:, kc:kc + 1]),
                       reads=[("ps", bk), "s2col", "adacol2"], writes=[("uT", 2 * j + 1)])
            yield

        for _ in nloop_thread(0):
            pass
        nl_done = 1
        rnd = 0
        nl = None
        next_tile = 0
        tiles = []
        while next_tile < NOWN or tiles or nl is not None:
            while len(tiles) < 2 and next_tile < NOWN and next_tile // 4 < nl_done:
                tiles.append(tile_thread(next_tile))
                next_tile += 1
            if nl is None and nl_done < 4 and next_tile >= 4 * (nl_done - 1):
                nl = nloop_thread(nl_done)
            progressed = False
            for t in list(tiles):
                try:
                    next(t)
                    progressed = True
                except StopIteration:
                    tiles.remove(t)
            rnd += 1
            if nl is not None and (rnd % 2 == 0 or not tiles):
                try:
                    next(nl)
                    progressed = True
                except StopIteration:
                    nl = None
                    nl_done += 1
            if not progressed and not tiles and nl is None and next_tile >= NOWN:
                break
        S_.barrier()
        AR.release(m_attnT)

        wg_b = AR.alloc(BF16, KC, DFF)
        wu_b = AR.alloc(BF16, KC, DFF)
        wd_b = AR.alloc(BF16, FC, D)
        hT = AR.alloc(BF16, FC, 256)
        sgt = [AR.alloc(F32, 256) for _ in range(2)]
        xin = [AR.alloc(F32, D) for _ in range(2)]
        r2 = xin
        lnB = GB
        stt = [AR.alloc(F32, 16) for _ in range(2)]
        sml = [AR.alloc(F32, 4) for _ in range(2)]
        CG = [(0, 768), (768, 1536), (1536, 2304), (2304, 2816)]
        for ci, (a, b_) in enumerate(CG):
            dma("pool", wg_b[:, :, a:b_], w_g[:, :, a:b_], writes=[("wg", ci)], key=("w3", ci))
            dma("pool", wu_b[:, :, a:b_], w_u[:, :, a:b_], writes=[("wu", ci)], key=("w4", ci))
        for fc in range(FC):
            st_ = xin[fc % 2]
            dma("sp", st_, w_d[:, fc, :], writes=[("xin2", fc % 2)], key=("xin", fc % 2))
            op("dve", lambda e, st_=st_, fc=fc: e.tensor_tensor(out=wd_b[:, fc, :], in0=st_, in1=GB[:, 1, :], op=ALU.mult),
               reads=[("xin2", fc % 2), ("GB", 1, 0), ("GB", 1, 1)], writes=[("wd", fc)])
        for i in range(2):
            dma("sp", lnB[:, i, :], lnp[2 + i:3 + i, :].partition_broadcast(128), writes=[("GB", i, 0), ("GB", i, 1)],
                key=("lnB", i))
        for sub in range(8):
            cs = sub * 256
            ukeys = [("uT", 2 * j + 1) for j in (2 * sub, 2 * sub + 1)]
            for fc in range(FC):
                par = fc % 2
                bg, bu = bank(par), bank(2 + par)
                for kc in range(KC):
                    op("pe", lambda e, bg=bg, kc=kc, fc=fc, cs=cs: e.matmul(bg[:, 0:256], wg_b[:, kc, fc * 128:(fc + 1) * 128],
                                                                       uT_own[:, kc, cs:cs + 256], start=(kc == 0), stop=(kc == KC - 1)),
                       reads=[("wg", min(fc // 6, 3))] + ukeys, writes=[("ps", par)], inc=(kc == KC - 1))
                for kc in range(KC):
                    op("pe", lambda e, bu=bu, kc=kc, fc=fc, cs=cs: e.matmul(bu[:, 0:256], wu_b[:, kc, fc * 128:(fc + 1) * 128],
                                                                       uT_own[:, kc, cs:cs + 256], start=(kc == 0), stop=(kc == KC - 1)),
                       reads=[("wu", min(fc // 6, 3))] + ukeys, writes=[("ps", 2 + par)], inc=(kc == KC - 1))
                op("act", lambda e, bg=bg, par=par: e.activation(out=sgt[par], in_=bg[:, 0:256], func=AF.Silu),
                   reads=[("ps", par)], writes=[("sgt", par)])
                op("dve", lambda e, bu=bu, par=par, fc=fc: e.tensor_tensor(out=hT[:, fc, :], in0=bu[:, 0:256], in1=sgt[par], op=ALU.mult),
                   reads=[("ps", 2 + par), ("sgt", par)], writes=[("hT", fc)])
            for i in range(2):
                j = 2 * sub + i
                k = j % 2
                dma("sp", xin[k], x1_scr[j * 128:(j + 1) * 128, :], reads=[("x1scr", j)], writes=[("xin2", k)], key=("xin", k))
                for hf in range(2):
                    bk = 4 + 2 * k + hf
                    for fc in range(FC):
                        op("pe", lambda e, bk=bk, fc=fc, i=i, hf=hf: e.matmul(bank(bk), hT[:, fc, i * 128:(i + 1) * 128],
                                                                           wd_b[:, fc, hf * 512:(hf + 1) * 512],
                                                                           start=(fc == 0), stop=(fc == FC - 1)),
                           reads=[("hT", fc), ("wd", fc)], writes=[("ps", bk)], inc=(fc == FC - 1))
                    op("dve", lambda e, bk=bk, k=k, hf=hf: e.scalar_tensor_tensor(
                        out=r2[k][:, hf * 512:(hf + 1) * 512], in0=xin[k][:, hf * 512:(hf + 1) * 512], scalar=ALPHA,
                        in1=bank(bk), op0=ALU.mult, op1=ALU.add),
                       reads=[("xin2", k), ("ps", bk)], writes=[("xin2", k)])
                st, sm = stt[k], sml[k]
                op("dve", lambda e, st=st, k=k: e.bn_stats(out=st[:, 0:6], in_=r2[k][:, 0:512]), reads=[("xin2", k)], writes=[("st", k, 0)])
                op("dve", lambda e, st=st, k=k: e.bn_stats(out=st[:, 6:12], in_=r2[k][:, 512:1024]), reads=[("xin2", k)], writes=[("st", k, 1)])
                op("dve", lambda e, st=st: e.bn_aggr(out=st[:, 12:14], in_=st[:, 0:12]), reads=[("st", k, 0), ("st", k, 1)],
                   writes=[("mv", k)])
                op("act", lambda e, st=st, sm=sm: e.activation(out=sm[:, 0:1], in_=st[:, 13:14], func=AF.Sqrt, bias=epsc[:, 0:1], scale=1.0),
                   reads=[("mv", k), "epsc"], writes=[("sd", k)])
                op("dve", lambda e, sm=sm: e.reciprocal(out=sm[:, 1:2], in_=sm[:, 0:1]), reads=[("sd", k)], writes=[("rstd", k)])
                op("dve", lambda e, st=st, sm=sm: e.tensor_scalar(out=sm[:, 2:3], in0=st[:, 12:13], scalar1=-1.0, scalar2=sm[:, 1:2],
                                                                  op0=ALU.mult, op1=ALU.mult),
                   reads=[("mv", k), ("rstd", k)], writes=[("nmr", k)])
                op("act", lambda e, k=k, sm=sm: e.activation(out=r2[k], in_=r2[k], func=AF.Identity, bias=sm[:, 2:3], scale=sm[:, 1:2]),
                   reads=[("xin2", k), ("rstd", k), ("nmr", k)], writes=[("xin2", k)])
                op("pool", lambda e, k=k: e.tensor_tensor(out=r2[k], in0=r2[k], in1=lnB[:, 0, :], op=ALU.mult),
                   reads=[("xin2", k), ("GB", 0, 0), ("GB", 0, 1)], writes=[("xin2", k)])
                op("pool", lambda e, k=k: e.tensor_tensor(out=r2[k], in0=r2[k], in1=lnB[:, 1, :], op=ALU.add),
                   reads=[("xin2", k), ("GB", 1, 0), ("GB", 1, 1)], writes=[("xin2", k)])
                dma("sp", out[j * 128:(j + 1) * 128, :], r2[k], reads=[("xin2", k)], writes=[("out", j)], key=("outd", k))
        S_.wait_all("sp", [("outd", 0), ("outd", 1)] + ([k for k in S_.dma_cnt if str(k).startswith("dbg") or (isinstance(k, tuple) and k[0] == "dbgx1")] if debug else []))
        print("arena peak bytes", AR.peak, "pe", S_.cnt["pe"], "act", S_.cnt["act"], "dve", S_.cnt["dve"], "pool", S_.cnt["pool"],
              "instr", {k: len(v) for k, v in S_.prog.items()})

        with nc.Block() as block:
            @block.tensor
            def _(e):
                S_.emit("pe", e)

            @block.scalar
            def _(e):
                S_.emit("act", e)

            @block.vector
            def _(e):
                S_.emit("dve", e)

            @block.gpsimd
            def _(e):
                S_.emit("pool", e)

            @block.sync
            def _(e):
                S_.emit("sp", e)
    return nc


_CONSTS = None


def _consts():
    global _CONSTS
    if _CONSTS is None:
        s = np.arange(128)[:, None]
        t = np.arange(128)[None, :]
        ident = (s == t).astype(np.float32)
        triS = np.where(s < t, 0.0, NEG).astype(np.float32)
        triI = np.where(s <= t, 0.0, NEG).astype(np.float32)
        negU = np.where(s >= t, -1.0, 0.0).astype(np.float32)
        _CONSTS = np.concatenate([ident, triS, triI, negU], axis=1)
    return _CONSTS


def _rk(w, kc):
    n = w.shape[1]
    return np.ascontiguousarray(w.reshape(kc, 128, n).transpose(1, 0, 2))


def make_in_maps(x, c, w_ada, b_ada, w_in, b_gate, b_forget, w_sb_out, w_fox_out, w_o,
                 ln1_g, ln1_b, w_ffn_gate, w_ffn_up, w_ffn_down, ln2_g, ln2_b):
    f = np.float32
    x = np.asarray(x, f)
    shared = {
        "w_ada": _rk(np.asarray(w_ada, f)[0], KC),
        "b_ada": np.ascontiguousarray(np.asarray(b_ada, f)[0][None, :]),
        "w_in": _rk(np.asarray(w_in, f)[0], KC),
        "b_gate": np.ascontiguousarray(np.asarray(b_gate, f)[0].reshape(16, 128).T),
        "b_forget": np.ascontiguousarray(np.asarray(b_forget, f)[0].reshape(8, 1)),
        "w_sb": _rk(np.asarray(w_sb_out, f)[0], 4),
        "w_fx": _rk(np.asarray(w_fox_out, f)[0], 4),
        "w_o": _rk(np.asarray(w_o, f)[0], KC),
        "lnp": np.ascontiguousarray(np.stack([np.asarray(a, f)[0] for a in (ln1_g, ln1_b, ln2_g, ln2_b)])),
        "w_g": _rk(np.asarray(w_ffn_gate, f)[0], KC),
        "w_u": _rk(np.asarray(w_ffn_up, f)[0], KC),
        "w_d": _rk(np.asarray(w_ffn_down, f)[0], FC),
    }
    cb = _consts()
    maps = []
    for core in range(8):
        b, p = core // 2, core % 2
        if p == 1:
            xa = x[b]
        else:
            xa = np.concatenate([np.zeros((128, D), f), x[b][:S - 128]], axis=0)
        dmask = np.full((128, 512), NEG if p == 0 else 0.0, f)
        m = dict(shared)
        m["xall"] = np.ascontiguousarray(xa)
        m["c_col"] = np.ascontiguousarray(np.asarray(c, f)[b].reshape(KC, 128).T)
        m["consts"] = np.ascontiguousarray(np.concatenate([cb, dmask], axis=1))
        maps.append(m)
    return maps


def assemble(results):
    out = np.zeros((4, S, D), np.float32)
    for core in range(8):
        b, p = core // 2, core % 2
        o = np.asarray(results[core]["out"], np.float32).reshape(NOWN, 128, D)
        ov = out[b].reshape(NB, 128, D)
        for j in range(NOWN):
            ov[2 * j + p] = o[j]
    return out


def kernel(**inputs):
    nc = build_nc(debug=False)
    maps = make_in_maps(**inputs)
    res = run_bass_kernel_spmd(nc, maps, core_ids=list(range(8)))
    return assemble(res.results)
```

```python
import numpy as np
from contextlib import ExitStack
import concourse.bass as bass
import concourse.mybir as mybir
from concourse.bass_utils import run_bass_kernel_spmd

F32 = mybir.dt.float32
BF16 = mybir.dt.bfloat16
AF = mybir.ActivationFunctionType
ALU = mybir.AluOpType

D = 1024
KC = 8
S = 4096
NB = 32
NOWN = 16
DFF = 2816
FC = 22
IN_COLS = 5128
OFF_SB = 0
OFF_FOX = 1536
OFF_FG = 3072
OFF_BG = 3080
LN_EPS = 1e-5
ALPHA = 2.0 ** 0.25
NEG = -30000.0

COMPUTE = ("pe", "act", "dve", "pool")


class Sched:
    def __init__(self, nc, stack, n_dma_sems=64):
        self.nc = nc
        self.prog = {e: [] for e in ("pe", "act", "dve", "pool", "sp")}
        self.cnt = {e: 0 for e in COMPUTE}
        self.sem = {e: stack.enter_context(nc.semaphore("s_" + e)) for e in COMPUTE}
        self.dma_pool = [stack.enter_context(nc.semaphore("d%d" % i)) for i in range(n_dma_sems)]
        self.dma_key = {}
        self.dma_cnt = {}
        self.known = {e: {} for e in self.prog}
        self.lastw = {}
        self.reads = {}

    def _semh(self, k):
        return self.sem[k] if k in self.sem else self.dma_pool[self.dma_key[k]]

    def _deps(self, eng, reads, writes):
        deps = set()
        for b in reads:
            ev = self.lastw.get(b)
            if ev:
                deps.add(ev)
        for b in writes:
            ev = self.lastw.get(b)
            if ev and not (ev[0] == eng and eng in COMPUTE):
                deps.add(ev)
            for ev in self.reads.get(b, ()):
                if not (ev[0] == eng and eng in COMPUTE):
                    deps.add(ev)
        waits = {}
        kn = self.known[eng]
        for (k, v) in deps:
            if k == eng and eng == "pe":
                continue
            if kn.get(k, 0) >= v:
                continue
            if waits.get(k, 0) < v:
                waits[k] = v
        for k, v in waits.items():
            kn[k] = v
        return [(self._semh(k), v) for k, v in waits.items()]

    def _record(self, ev, reads, writes):
        for b in reads:
            self.reads.setdefault(b, []).append(ev)
        for b in writes:
            self.lastw[b] = ev
            self.reads[b] = []

    def op(self, eng, fn, reads=(), writes=(), inc=True):
        waits = self._deps(eng, reads, writes)
        if inc:
            self.cnt[eng] += 1
            ev = (eng, self.cnt[eng])
        else:
            ev = (eng, self.cnt[eng] + 1)
        self._record(ev, reads, writes)
        self.prog[eng].append((waits, fn, (self.sem[eng], 1) if inc else None))

    def dma(self, queue, out, in_, reads=(), writes=(), key=None, **kw):
        assert key is not None
        if key not in self.dma_key:
            idx = len(self.dma_key)
            assert idx < len(self.dma_pool), "out of dma semaphores"
            self.dma_key[key] = idx
            self.dma_cnt[key] = 0
        waits = self._deps(queue, reads, writes)
        self.dma_cnt[key] += 16
        ev = (key, self.dma_cnt[key])
        self._record(ev, reads, writes)
        fn = lambda e, out=out, in_=in_, kw=kw: e.dma_start(out=out, in_=in_, **kw)
        self.prog[queue].append((waits, fn, (self._semh(key), 16)))

    def group_final(self, key, bufs):
        for b in bufs:
            self.lastw[b] = (key, self.dma_cnt[key])

    def wait_all(self, eng, keys):
        waits = []
        for k in keys:
            v = self.dma_cnt[k] if k in self.dma_cnt else self.cnt[k]
            if v > 0:
                waits.append((self._semh(k), v))
        self.prog[eng].append((waits, None, None))

    def barrier(self):
        keys = list(COMPUTE) + list(self.dma_cnt.keys())
        for eng in self.prog:
            waits = []
            for k in keys:
                v = self.dma_cnt[k] if k in self.dma_cnt else self.cnt[k]
                if k == eng or v == 0:
                    continue
                if self.known[eng].get(k, 0) >= v:
                    continue
                self.known[eng][k] = v
                waits.append((self._semh(k), v))
            if waits:
                self.prog[eng].append((waits, None, None))

    def emit(self, eng, e):
        for waits, fn, inc in self.prog[eng]:
            for (h, v) in waits:
                e.wait_ge(h, v)
            if fn is None:
                continue
            ins = fn(e)
            if inc is not None:
                ins.then_inc(inc[0], inc[1])


class Arena:
    def __init__(self, nc, stack, nbytes):
        self.h32 = stack.enter_context(nc.sbuf_tensor("arena", [128, nbytes // 4], F32))
        self.h16 = self.h32.bitcast(BF16)
        self.top = 0
        self.limit = nbytes
        self.peak = 0

    def alloc(self, dtype, *free):
        n = 1
        for f in free:
            n *= f
        sz = 4 if dtype == F32 else 2
        nb = (n * sz + 63) // 64 * 64
        off = self.top
        self.top += nb
        self.peak = max(self.peak, self.top)
        assert self.top <= self.limit, "arena overflow %d > %d" % (self.top, self.limit)
        base = self.h32 if dtype == F32 else self.h16
        e0 = off // sz
        ap = base[:, e0:e0 + n]
        if len(free) == 2:
            ap = ap.rearrange("p (a b) -> p a b", a=free[0], b=free[1])
        elif len(free) == 3:
            ap = ap.rearrange("p (a b c) -> p a b c", a=free[0], b=free[1], c=free[2])
        return ap

    def mark(self):
        return self.top

    def release(self, m):
        self.top = m


def sb_of(g):
    return g // 2 if g % 2 == 0 else 16 + g // 2


def build_nc(debug=False):
    nc = bass.Bass("TRN2", target_bir_lowering=False)
    dt = nc.dram_tensor
    xall = dt("xall", [S, D], F32, kind="ExternalInput").ap()
    c_col = dt("c_col", [128, KC], F32, kind="ExternalInput").ap()
    w_ada = dt("w_ada", [128, KC, 6 * D], F32, kind="ExternalInput").ap()
    b_ada = dt("b_ada", [1, 6 * D], F32, kind="ExternalInput").ap()
    w_in = dt("w_in", [128, KC, IN_COLS], F32, kind="ExternalInput").ap()
    b_gate = dt("b_gate", [128, 16], F32, kind="ExternalInput").ap()
    b_forget = dt("b_forget", [8, 1], F32, kind="ExternalInput").ap()
    w_sb = dt("w_sb", [128, 4, D], F32, kind="ExternalInput").ap()
    w_fx = dt("w_fx", [128, 4, D], F32, kind="ExternalInput").ap()
    w_o = dt("w_o", [128, KC, D], F32, kind="ExternalInput").ap()
    lnp = dt("lnp", [4, D], F32, kind="ExternalInput").ap()
    w_g = dt("w_g", [128, KC, DFF], F32, kind="ExternalInput").ap()
    w_u = dt("w_u", [128, KC, DFF], F32, kind="ExternalInput").ap()
    w_d = dt("w_d", [128, FC, D], F32, kind="ExternalInput").ap()
    consts = dt("consts", [128, 4 * 128 + 512], F32, kind="ExternalInput").ap()
    out = dt("out", [NOWN * 128, D], F32, kind="ExternalOutput").ap()
    fc_scr = dt("fc_scr", [8, 3, S], BF16, kind="Internal").ap()
    x1_scr = dt("x1_scr", [NOWN * 128, D], F32, kind="Internal").ap()
    wg_s = dt("wg_s", [128, KC, DFF], BF16, kind="Internal").ap()
    wu_s = dt("wu_s", [128, KC, DFF], BF16, kind="Internal").ap()
    wbg_s = dt("wbg_s", [128, KC, 2 * D], BF16, kind="Internal").ap()
    wsb_s = dt("wsb_s", [128, 4, D], BF16, kind="Internal").ap()
    wfx_s = dt("wfx_s", [128, 4, D], BF16, kind="Internal").ap()
    dbg = {}
    if debug:
        dbg["uT"] = dt("dbg_uT", [128, KC, S], BF16, kind="ExternalOutput").ap()
        dbg["attnT"] = dt("dbg_attnT", [128, KC, NOWN * 128], BF16, kind="ExternalOutput").ap()
        dbg["ada"] = dt("dbg_ada", [128, 48], F32, kind="ExternalOutput").ap()
        dbg["fcn"] = dt("dbg_fcn", [8, S], F32, kind="ExternalOutput").ap()
        dbg["x1"] = dt("dbg_x1", [NOWN * 128, D], F32, kind="ExternalOutput").ap()
        dbg["sml"] = dt("dbg_sml", [128, 4], F32, kind="ExternalOutput").ap()
        dbg["stt"] = dt("dbg_stt", [128, 16], F32, kind="ExternalOutput").ap()
        dbg["xn"] = dt("dbg_xn", [128, D], F32, kind="ExternalOutput").ap()
        dbg["s1col"] = dt("dbg_s1col", [128, 8], F32, kind="ExternalOutput").ap()

    with ExitStack() as stack:
        S_ = Sched(nc, stack)
        AR = Arena(nc, stack, 206 * 1024)
        ps = stack.enter_context(nc.psum_tensor("ps", [128, 8 * 512], F32))

        def bank(i):
            return ps[:, i * 512:(i + 1) * 512]

        op, dma = S_.op, S_.dma

        ident_f = AR.alloc(F32, 128)
        cst_b = AR.alloc(BF16, 4 * 128 + 512)
        ident_b = cst_b[:, 0:128]
        triS_b = cst_b[:, 128:256]
        triI_b = cst_b[:, 256:384]
        negU_b = cst_b[:, 384:512]
        dm_b = cst_b[:, 512:1024]
        negones_b = AR.alloc(BF16, 128)
        adacol = AR.alloc(F32, 48)
        s1col = AR.alloc(F32, 8)
        s2col = AR.alloc(F32, 8)
        bgcol = AR.alloc(F32, 16)
        fcncol = AR.alloc(F32, 256)
        one_f = AR.alloc(F32, 128)
        GB = AR.alloc(F32, 2, D)
        epsc = AR.alloc(F32, 1)
        uT_own = AR.alloc(BF16, KC, NOWN * 128)
        m_attnT = AR.mark()
        attnT = AR.alloc(BF16, KC, NOWN * 128)
        op("dve", lambda e: e.memset(epsc, LN_EPS), writes=["epsc"])

        dma("sp", ident_f, consts[:, 0:128], writes=["ident_f"], key="c0")
        dma("pool", cst_b, consts, writes=["cst_b"], key="c1")
        dma("sp", bgcol, b_gate, writes=["bgcol"], key="c2")
        op("dve", lambda e: e.memset(negones_b, -1.0), writes=["negones"])
        op("dve", lambda e: e.memset(one_f, 1.0), writes=["one_f"])

        m_att = AR.mark()
        uT_oth = AR.alloc(BF16, KC, NOWN * 128)

        def uT_blk(g, kc):
            t = uT_own if g % 2 == 1 else uT_oth
            j = g // 2
            return t[:, kc, j * 128:(j + 1) * 128]

        mA = AR.mark()
        ccol = AR.alloc(F32, KC)
        cact = AR.alloc(F32, KC)
        badg = [AR.alloc(F32, 512) for _ in range(2)]
        adag = [AR.alloc(F32, 512) for _ in range(2)]
        wst = [AR.alloc(F32, KC, 512) for _ in range(2)]
        NXB = 6
        xb = [AR.alloc(F32, D) for _ in range(NXB)]
        xn = [AR.alloc(F32, D) for _ in range(3)]
        sttA = [AR.alloc(F32, 16) for _ in range(4)]
        smlA = [AR.alloc(F32, 4) for _ in range(4)]
        dma("sp", ccol, c_col, writes=["ccol"], key="c3")
        op("act", lambda e: e.activation(out=cact, in_=ccol, func=AF.Silu), reads=["ccol"], writes=["cact"])

        def ada_group(ng):
            sl = ng % 2
            dma("pool", wst[sl], w_ada[:, :, ng * 512:(ng + 1) * 512], writes=[("wst", sl)], key=("wst", sl))
            dma("pool", badg[sl][0:1, :], b_ada[:, ng * 512:(ng + 1) * 512], writes=[("badg", sl)], key=("badg", sl))
            pb = bank(sl)
            for kc in range(KC):
                op("pe", lambda e, pb=pb, kc=kc, sl=sl: e.matmul(pb[0:1, :], cact[:, kc:kc + 1], wst[sl][:, kc, :],
                                                              start=(kc == 0), stop=(kc == KC - 1)),
                   reads=["cact", ("wst", sl)], writes=[("ps", sl)], inc=(kc == KC - 1))
            op("dve", lambda e, pb=pb, sl=sl: e.tensor_tensor(out=adag[sl][0:1, :], in0=pb[0:1, :], in1=badg[sl][0:1, :], op=ALU.add),
               reads=[("ps", sl), ("badg", sl)], writes=[("adag", sl)])
            for c4 in range(4):
                c = ng * 4 + c4
                op("pe", lambda e, c=c, c4=c4, sl=sl: e.matmul(bank(2)[:, c:c + 1], adag[sl][0:1, c4 * 128:(c4 + 1) * 128], one_f[0:1, 0:1],
                                                            start=True, stop=True),
                   reads=[("adag", sl), "one_f"], writes=[("ps", 2)], inc=(c4 == 3))
            if ng in (4, 5, 10, 11):
                i, hf = (0 if ng < 6 else 1), ng % 2
                op("pe", lambda e, sl=sl: e.matmul(bank(3), one_f[0:1, :], adag[sl][0:1, :], start=True, stop=True),
                   reads=[("adag", sl), "one_f"], writes=[("ps", 3)])
                op("dve", lambda e, i=i, hf=hf: e.tensor_copy(out=GB[:, i, hf * 512:(hf + 1) * 512], in_=bank(3)),
                   reads=[("ps", 3)], writes=[("GB", i, hf)])

        for ng in range(4):
            ada_group(ng)
        op("dve", lambda e: e.tensor_copy(out=adacol[:, 0:16], in_=bank(2)[:, 0:16]), reads=[("ps", 2)], writes=["adacol"])
        op("dve", lambda e: e.tensor_scalar(out=s1col, in0=adacol[:, 8:16], scalar1=1.0, scalar2=None, op0=ALU.add),
           reads=["adacol"], writes=["s1col"])

        def A0(g):
            kx = g % NXB
            dma("sp", xb[kx], xall[g * 128:(g + 1) * 128, :], writes=[("xb", kx)], key=("xb", kx))

        def A1(g):
            k4 = g % 4
            kx = g % NXB
            xt, st = xb[kx], sttA[k4]
            op("dve", lambda e: e.bn_stats(out=st[:, 0:6], in_=xt[:, 0:512]), reads=[("xb", kx)], writes=[("st", k4, 0)])
            op("dve", lambda e: e.bn_stats(out=st[:, 6:12], in_=xt[:, 512:1024]), reads=[("xb", kx)], writes=[("st", k4, 1)])
            op("dve", lambda e: e.bn_aggr(out=st[:, 12:14], in_=st[:, 0:12]), reads=[("st", k4, 0), ("st", k4, 1)],
               writes=[("mv", k4)])
            sm = smlA[k4]
            op("act", lambda e: e.activation(out=sm[:, 0:1], in_=st[:, 13:14], func=AF.Sqrt, bias=epsc[:, 0:1], scale=1.0),
               reads=[("mv", k4), "epsc"], writes=[("sd", k4)])

        def A3(g):
            k4 = g % 4
            kx = g % NXB
            st, sm, xt = sttA[k4], smlA[k4], xb[kx]
            xo = xn[g % 3]
            op("dve", lambda e: e.reciprocal(out=sm[:, 1:2], in_=sm[:, 0:1]), reads=[("sd", k4)], writes=[("rstd", k4)])
            op("dve", lambda e: e.tensor_scalar(out=sm[:, 2:3], in0=st[:, 12:13], scalar1=-1.0, scalar2=sm[:, 1:2],
                                                op0=ALU.mult, op1=ALU.mult),
               reads=[("mv", k4), ("rstd", k4)], writes=[("nmr", k4)])
            op("act", lambda e: e.activation(out=xo, in_=xt, func=AF.Identity, bias=sm[:, 2:3], scale=sm[:, 1:2]),
               reads=[("xb", kx), ("rstd", k4), ("nmr", k4)], writes=[("xn", g % 3)])

        def A5(g):
            k = g % 2
            xo = xn[g % 3]
            for kc in range(KC):
                bk = 4 + 2 * k + kc // 4
                op("pe", lambda e, bk=bk, kc=kc: e.transpose(bank(bk)[:, (kc % 4) * 128:(kc % 4 + 1) * 128],
                                                         xo[:, kc * 128:(kc + 1) * 128], ident_f),
                   reads=[("xn", g % 3), "ident_f"], writes=[("ps", bk)], inc=(kc % 4 == 3))

        def A6(g):
            k = g % 2
            for kc in range(KC):
                bk = 4 + 2 * k + kc // 4
                src = bank(bk)[:, (kc % 4) * 128:(kc % 4 + 1) * 128]
                if kc // 4 == 0:
                    op("dve", lambda e, src=src, kc=kc: e.tensor_scalar(
                        out=uT_blk(g, kc), in0=src, scalar1=s1col[:, kc:kc + 1], scalar2=adacol[:, kc:kc + 1],
                        op0=ALU.mult, op1=ALU.add),
                       reads=[("ps", bk), "s1col", "adacol"], writes=[("uT", g)])
                else:
                    op("act", lambda e, src=src, kc=kc: e.activation(out=uT_blk(g, kc), in_=src, func=AF.Identity,
                                                                 bias=adacol[:, kc:kc + 1], scale=s1col[:, kc:kc + 1]),
                       reads=[("ps", bk), "s1col", "adacol"], writes=[("uT", g)])

        for g_ in range(4):
            A0(g_)
        for s_ in range(NB + 3):
            if s_ < NB:
                A1(s_)
            if s_ + 4 < NB:
                A0(s_ + 4)
            if 0 <= s_ - 1 < NB:
                A3(s_ - 1)
            if 0 <= s_ - 2 < NB:
                A5(s_ - 2)
            if 0 <= s_ - 3 < NB:
                A6(s_ - 3)
            if s_ % 4 == 1 and s_ // 4 < 8:
                ada_group(4 + s_ // 4)
        op("dve", lambda e: e.tensor_copy(out=adacol[:, 16:48], in_=bank(2)[:, 16:48]), reads=[("ps", 2)], writes=["adacol2"])
        op("dve", lambda e: e.tensor_scalar(out=s2col, in0=adacol[:, 32:40], scalar1=1.0, scalar2=None, op0=ALU.add),
           reads=["adacol2"], writes=["s2col"])
        if debug:
            dma("sp", dbg["ada"], adacol, reads=["adacol", "adacol2"], key="dbg1")
            dma("sp", dbg["uT"][:, :, 0:2048], uT_oth, reads=[("uT", g) for g in range(0, NB, 2)], key="dbg6")
            dma("sp", dbg["uT"][:, :, 2048:4096], uT_own, reads=[("uT", g) for g in range(1, NB, 2)], key="dbg7")
        S_.barrier()
        AR.release(mA)

        mF = AR.mark()
        wf_b = AR.alloc(BF16, KC, 8)
        nbf = AR.alloc(F32, 2)
        spf = AR.alloc(F32, S)
        fcn = AR.alloc(F32, S)
        fc3 = AR.alloc(BF16, 3, S)
        etmp = [AR.alloc(F32, 512) for _ in range(2)]
        dma("pool", wf_b, w_in[:, :, OFF_FG:OFF_FG + 8], writes=["wf_b"], key="c5")
        dma("sp", nbf[0:8, 0:1], b_forget, writes=["nbf0"], key="c6")
        op("dve", lambda e: e.tensor_scalar(out=nbf[0:8, 1:2], in0=nbf[0:8, 0:1], scalar1=-1.0, scalar2=None, op0=ALU.mult),
           reads=["nbf0"], writes=["nbf"])
        for pg in range(8):
            bk = pg % 2
            for blk in range(4):
                g = pg * 4 + blk
                for kc in range(KC):
                    op("pe", lambda e, bk=bk, blk=blk, g=g, kc=kc: e.matmul(
                        bank(bk)[0:8, blk * 128:(blk + 1) * 128], wf_b[:, kc, :], uT_blk(g, kc),
                        start=(kc == 0), stop=(kc == KC - 1)),
                       reads=["wf_b", ("uT", g)], writes=[("ps", bk)], inc=(blk == 3 and kc == KC - 1))
            op("act", lambda e, bk=bk: e.activation(out=etmp[bk][0:8, :], in_=bank(bk)[0:8, :], func=AF.Exp,
                                                    bias=nbf[0:8, 1:2], scale=-1.0),
               reads=[("ps", bk), "nbf"], writes=[("etmp", bk)])
            op("act", lambda e, bk=bk, pg=pg: e.activation(out=spf[0:8, pg * 512:(pg + 1) * 512], in_=etmp[bk][0:8, :],
                                                           func=AF.Ln, bias=1.0, scale=1.0),
               reads=[("etmp", bk)], writes=["spf"])
        op("dve", lambda e: e.tensor_tensor_scan(out=fcn[0:8, :], data0=spf[0:8, :], data1=spf[0:8, :], initial=0.0,
                                                 op0=ALU.add, op1=ALU.max),
           reads=["spf"], writes=["fcn"])
        if debug:
            dma("sp", dbg["fcn"], fcn[0:8, :], reads=["fcn"], key="dbg8")
        for g in range(NB):
            op("pe", lambda e, g=g: e.transpose(bank(2)[:, g * 8:(g + 1) * 8], fcn[0:8, g * 128:(g + 1) * 128], ident_f[0:8, 0:8]),
               reads=["fcn", "ident_f"], writes=[("ps", 2)], inc=(g == NB - 1))
        op("dve", lambda e: e.tensor_copy(out=fcncol, in_=bank(2)[:, 0:256]), reads=[("ps", 2)], writes=["fcncol"])
        op("dve", lambda e: e.tensor_copy(out=fc3[0:8, 0, :], in_=fcn[0:8, :]), reads=["fcn"], writes=[("fc3", 0)])
        op("dve", lambda e: e.tensor_tensor(out=spf[0:8, :], in0=fcn[0:8, :], in1=fc3[0:8, 0, :], op=ALU.subtract),
           reads=["fcn", ("fc3", 0), ("ps", 2)], writes=["spf"])
        op("dve", lambda e: e.tensor_copy(out=fc3[0:8, 1, :], in_=spf[0:8, :]), reads=["spf"], writes=[("fc3", 1)])
        op("dve", lambda e: e.tensor_tensor(out=fcn[0:8, :], in0=spf[0:8, :], in1=fc3[0:8, 1, :], op=ALU.subtract),
           reads=["spf", ("fc3", 1), "fcncol"], writes=["fcn"])
        op("dve", lambda e: e.tensor_copy(out=fc3[0:8, 2, :], in_=fcn[0:8, :]), reads=["fcn"], writes=[("fc3", 2)])
        dma("sp", fc_scr, fc3[0:8, :, :], reads=[("fc3", 0), ("fc3", 1), ("fc3", 2)], writes=["fc_scr"], key="c7")
        S_.barrier()
        AR.release(mF)

        wp = [AR.alloc(BF16, KC, 384) for _ in range(2)]
        KT = AR.alloc(BF16, S)
        KTB = AR.alloc(BF16, S)
        QT = AR.alloc(BF16, NOWN * 128)
        QTB = AR.alloc(BF16, NOWN * 128)
        VT = AR.alloc(BF16, NB, 2, 128)
        eb = [[AR.alloc(F32, 512) for _ in range(2)] for _ in range(2)]
        spb = [[AR.alloc(BF16, 512) for _ in range(2)] for _ in range(2)]
        Ab = [[AR.alloc(BF16, 512) for _ in range(2)] for _ in range(2)]
        Rb = [AR.alloc(BF16, 512) for _ in range(2)]
        rden = [AR.alloc(F32, 512) for _ in range(2)]

        def load_w(pr):
            fox = pr >= 4
            hp = pr % 4
            base = OFF_FOX if fox else OFF_SB
            w = wp[pr % 2]
            wk = ("wp", pr % 2)
            for i in range(3):
                c0 = base + i * 512 + hp * 128
                dma("pool", w[:, :, i * 128:(i + 1) * 128], w_in[:, :, c0:c0 + 128], writes=[(wk, i)], key=(wk, i))

        def sb_set(pr):
            if pr % 2 == 0:
                return KT, QT, 0, "KT", "QT", "VT0"
            return KTB, QTB, 1, "KTB", "QTB", "VT1"

        def proj_pair(pr):
            fox = pr >= 4
            hp = pr % 4
            w = wp[pr % 2]
            wk = ("wp", pr % 2)
            if not fox:
                kt_d, qt_d, vi, kkey, qkey, vkey = sb_set(pr)
            for sg in range(8):
                src = uT_oth if sg < 4 else uT_own
                cs = (sg % 4) * 512
                bk = sg % 2
                rk = [("uT", g) for g in range(NB) if sb_of(g) // 4 == sg]
                for kc in range(KC):
                    op("pe", lambda e, bk=bk, kc=kc, src=src, cs=cs: e.matmul(bank(bk), w[:, kc, 128:256], src[:, kc, cs:cs + 512],
                                                                          start=(kc == 0), stop=(kc == KC - 1)),
                       reads=[(wk, 1)] + rk, writes=[("ps", bk)], inc=(kc == KC - 1))
                if not fox:
                    op("dve", lambda e, bk=bk, sg=sg: e.tensor_copy(out=kt_d[:, sg * 512:(sg + 1) * 512], in_=bank(bk)),
                       reads=[("ps", bk)], writes=[(kkey, sg)])
                else:
                    op("dve", lambda e, bk=bk, sg=sg: e.tensor_copy(out=KT[0:64, sg * 512:(sg + 1) * 512], in_=bank(bk)[0:64, :]),
                       reads=[("ps", bk)], writes=[("KT", sg)])
                    op("dve", lambda e, bk=bk, sg=sg: e.tensor_copy(out=KTB[64:128, sg * 512:(sg + 1) * 512], in_=bank(bk)[64:128, :]),
                       reads=[("ps", bk)], writes=[("KTB", sg)])
                yield
            for G in range(4):
                bk = G % 2
                rk = [("uT", 2 * j + 1) for j in range(4 * G, 4 * G + 4)]
                for kc in range(KC):
                    op("pe", lambda e, bk=bk, kc=kc, G=G: e.matmul(bank(bk), w[:, kc, 0:128], uT_own[:, kc, G * 512:(G + 1) * 512],
                                                                start=(kc == 0), stop=(kc == KC - 1)),
                       reads=[(wk, 0)] + rk, writes=[("ps", bk)], inc=(kc == KC - 1))
                if not fox:
                    op("dve", lambda e, bk=bk, G=G: e.tensor_scalar(out=qt_d[:, G * 512:(G + 1) * 512], in0=bank(bk), scalar1=0.125,
                                                                    scalar2=None, op0=ALU.mult),
                       reads=[("ps", bk)], writes=[(qkey, G)])
                else:
                    op("dve", lambda e, bk=bk, G=G: e.tensor_scalar(out=QT[0:64, G * 512:(G + 1) * 512], in0=bank(bk)[0:64, :],
                                                                    scalar1=0.125, scalar2=None, op0=ALU.mult),
                       reads=[("ps", bk)], writes=[("QT", G)])
                    op("dve", lambda e, bk=bk, G=G: e.tensor_scalar(out=QTB[64:128, G * 512:(G + 1) * 512], in0=bank(bk)[64:128, :],
                                                                    scalar1=0.125, scalar2=None, op0=ALU.mult),
                       reads=[("ps", bk)], writes=[("QTB", G)])
                yield
            for sg in range(8):
                bk = sg % 2
                for blk in range(4):
                    sbk = sg * 4 + blk
                    g = 2 * sbk if sbk < 16 else 2 * (sbk - 16) + 1
                    for kc in range(KC):
                        op("pe", lambda e, bk=bk, blk=blk, g=g, kc=kc: e.matmul(
                            bank(bk)[:, blk * 128:(blk + 1) * 128], uT_blk(g, kc), w[:, kc, 256:384],
                            start=(kc == 0), stop=(kc == KC - 1)),
                           reads=[(wk, 2), ("uT", g)], writes=[("ps", bk)], inc=(blk == 3 and kc == KC - 1))
                    if blk == 1:
                        yield
                if not fox:
                    op("dve", lambda e, bk=bk, sg=sg: e.tensor_copy(
                        out=VT[:, sg * 4:(sg + 1) * 4, vi, :], in_=bank(bk).rearrange("p (a b) -> p a b", a=4, b=128)),
                       reads=[("ps", bk)], writes=[(vkey, sg)])
                else:
                    op("dve", lambda e, bk=bk, sg=sg: e.tensor_copy(
                        out=VT[:, sg * 4:(sg + 1) * 4, :, 0:64],
                        in_=bank(bk).rearrange("p (a h d) -> p a h d", a=4, h=2, d=64)),
                       reads=[("ps", bk)], writes=[("VT0", sg), ("VT1", sg)])
                yield
            if fox:
                for h in range(2):
                    hg = hp * 2 + h
                    dst = QT[64:67, :] if h == 0 else QTB[0:3, :]
                    srcap = fc_scr[hg].rearrange("r (j two c) -> r j two c", two=2, c=128)[:, :, 1, :]
                    dma("sp", dst.rearrange("r (j c) -> r j c", c=128), srcap, reads=["fc_scr"],
                        writes=[("QTaug", h)] + [(("QT" if h == 0 else "QTB"), G_) for G_ in range(4)], key=("qaug", h))

        def grp_cols(G, g):
            d = g - 8 * G
            return (d // 2) * 128 if d >= 0 else 0

        hcount = [0, 0]

        def attn_sb(pr, bg=None):
            kt_d, qt_d, vi, kkey, qkey, vkey = sb_set(pr)
            steps = []
            for G in range(4):
                gs = list(range(8 * G + 7, -1, -1))
                for it, g in enumerate(gs):
                    for h in range(2):
                        par = hcount[h] % 2
                        hcount[h] += 1
                        steps.append(dict(G=G, it=it, g=g, h=h, par=par, last=(g == 0), c0=grp_cols(G, g), b=sb_of(g)))
            N = len(steps)

            def bufs(st):
                h, par = st["h"], st["par"]
                return bank(2 + 2 * h + par), ("ps", 2 + 2 * h + par), bank(6 + h), ("ps", 6 + h)

            def QK(st):
                G, g, h, par, c0, b = st["G"], st["g"], st["h"], st["par"], st["c0"], st["b"]
                zb, zk, ob, ok_ = bufs(st)
                d = g - 8 * G
                rs = slice(h * 64, (h + 1) * 64)
                qsl = slice(G * 512 + c0, G * 512 + 512)
                has_tri = d >= 1 and d % 2 == 1
                has_dm = g == 0
                op("pe", lambda e, st_=not (has_tri or has_dm): e.matmul(
                    zb[:, c0:512], kt_d[rs, b * 128:(b + 1) * 128], qt_d[rs, qsl], start=True, stop=st_),
                   reads=[(kkey, b // 4), (qkey, G)], writes=[zk], inc=not (has_tri or has_dm))
                if has_tri:
                    op("pe", lambda e: e.matmul(zb[:, c0:c0 + 128], ident_b, triS_b, start=False, stop=not has_dm),
                       reads=["cst_b"], writes=[zk], inc=not has_dm)
                if has_dm:
                    op("pe", lambda e: e.matmul(zb[:, c0:512], ident_b, dm_b[:, c0:512], start=False, stop=True),
                       reads=["cst_b"], writes=[zk])

            def EL(st):
                h, par, c0 = st["h"], st["par"], st["c0"]
                zb, zk, ob, ok_ = bufs(st)
                e_, sp_ = eb[h][par], spb[h][par]
                op("act", lambda e: e.activation(out=e_[:, c0:512], in_=zb[:, c0:512], func=AF.Exp),
                   reads=[zk], writes=[("e", h, par)])
                op("act", lambda e: e.activation(out=sp_[:, c0:512], in_=e_[:, c0:512], func=AF.Ln, bias=1.0, scale=1.0),
                   reads=[("e", h, par)], writes=[("sp", h, par)])

            def CUM(st):
                h, par, c0, it = st["h"], st["par"], st["c0"], st["it"]
                zb, zk, ob, ok_ = bufs(st)
                sp_ = spb[h][par]
                op("pe", lambda e: e.matmul(zb[:, c0:512], negU_b, sp_[:, c0:512], start=False, stop=True, skip_group_check=True),
                   reads=["cst_b", ("sp", h, par)], writes=[zk], inc=(it == 0))
                if it > 0:
                    op("pe", lambda e: e.matmul(zb[:, c0:512], negones_b, Rb[h][:, c0:512], start=False, stop=True,
                                                skip_group_check=True),
                       reads=["negones", ("R", h)], writes=[zk])

            def AR_(st):
                h, par, c0, it = st["h"], st["par"], st["c0"], st["it"]
                zb, zk, ob, ok_ = bufs(st)
                A_, sp_ = Ab[h][par], spb[h][par]
                op("act", lambda e: e.activation(out=A_[:, c0:512], in_=zb[:, c0:512], func=AF.Exp),
                   reads=[zk], writes=[("A", h, par)])
                if it == 0:
                    op("pool", lambda e: e.memset(Rb[h], 0.0), writes=[("R", h)])
                if not st["last"]:
                    op("pool", lambda e: e.tensor_tensor(out=Rb[h][:, c0:512], in0=Rb[h][:, c0:512], in1=sp_[:, c0:512], op=ALU.add),
                       reads=[("R", h), ("sp", h, par)], writes=[("R", h)])

            def AV(st):
                G, h, par, c0, it, b = st["G"], st["h"], st["par"], st["c0"], st["it"], st["b"]
                zb, zk, ob, ok_ = bufs(st)
                A_ = Ab[h][par]
                op("pe", lambda e: e.matmul(ob[0:64, c0:512], VT[:, b, vi, h * 64:(h + 1) * 64], A_[:, c0:512],
                                            start=(it == 0), stop=st["last"], skip_group_check=True),
                   reads=[(vkey, b // 4), ("A", h, par)], writes=[ok_], inc=True)
                if st["last"]:
                    op("act", lambda e: e.activation(out=attnT[h * 64:(h + 1) * 64, pr, G * 512:(G + 1) * 512],
                                                     in_=ob[0:64, :], func=AF.Copy),
                       reads=[ok_], writes=[("attnT", pr, G)])

            for s in range(-2, N + 2):
                if 0 <= s - 1 < N:
                    CUM(steps[s - 1])
                if 0 <= s - 2 < N:
                    AV(steps[s - 2])
                if 0 <= s + 2 < N:
                    QK(steps[s + 2])
                if 0 <= s < N:
                    EL(steps[s])
                if 0 <= s - 1 < N:
                    AR_(steps[s - 1])
                if bg is not None and s % 3 == 0:
                    try:
                        next(bg)
                    except StopIteration:
                        bg = None
            if bg is not None:
                for _ in bg:
                    pass

        def attn_fox(pr):
            hp = pr % 4
            steps = []
            for G in range(4):
                gs = list(range(8 * G + 7, -1, -1))
                for it, g in enumerate(gs):
                    for h in range(2):
                        par = hcount[h] % 2
                        hcount[h] += 1
                        steps.append(dict(G=G, it=it, g=g, h=h, par=par, last=(g == 0), c0=grp_cols(G, g), b=sb_of(g)))
            N = len(steps)

            def bufs(st):
                h, par = st["h"], st["par"]
                return bank(2 + 2 * h + par), ("ps", 2 + 2 * h + par), bank(6 + h), ("ps", 6 + h)

            def QK(st):
                G, g, h, par, c0, b = st["G"], st["g"], st["h"], st["par"], st["c0"], st["b"]
                zb, zk, ob, ok_ = bufs(st)
                d = g - 8 * G
                qsl = slice(G * 512 + c0, G * 512 + 512)
                has_tri = d >= 1 and d % 2 == 1
                has_dm = g == 0
                if h == 0:
                    kap, qap = KT[0:67, b * 128:(b + 1) * 128], QT[0:67, qsl]
                    rds = [("KT", b // 4), ("QT", G), ("QTaug", 0), "kaug"]
                else:
                    kap, qap = KTB[:, b * 128:(b + 1) * 128], QTB[:, qsl]
                    rds = [("KTB", b // 4), ("QTB", G), ("QTaug", 1), "kaug"]
                op("pe", lambda e, st_=not (has_tri or has_dm): e.matmul(zb[:, c0:512], kap, qap, start=True, stop=st_),
                   reads=rds, writes=[zk], inc=not (has_tri or has_dm))
                if has_tri:
                    op("pe", lambda e: e.matmul(zb[:, c0:c0 + 128], ident_b, triI_b, start=False, stop=not has_dm),
                       reads=["cst_b"], writes=[zk], inc=not has_dm)
                if has_dm:
                    op("pe", lambda e: e.matmul(zb[:, c0:512], ident_b, dm_b[:, c0:512], start=False, stop=True),
                       reads=["cst_b"], writes=[zk])

            def PX(st):
                g, h, par, c0 = st["g"], st["h"], st["par"], st["c0"]
                zb, zk, ob, ok_ = bufs(st)
                hg = hp * 2 + h
                A_ = Ab[h][par]
                op("act", lambda e: e.activation(out=A_[:, c0:512], in_=zb[:, c0:512], func=AF.Exp,
                                                 bias=fcncol[:, g * 8 + hg:g * 8 + hg + 1], scale=1.0),
                   reads=[zk, "fcncol"], writes=[("A", h, par)])

            def AV(st):
                G, h, par, c0, it, b = st["G"], st["h"], st["par"], st["c0"], st["it"], st["b"]
                zb, zk, ob, ok_ = bufs(st)
                A_ = Ab[h][par]
                op("pe", lambda e: e.matmul(ob[:, c0:512], VT[:, b, h, :], A_[:, c0:512], start=(it == 0), stop=st["last"],
                                            skip_group_check=True),
                   reads=[("VT0", b // 4), ("VT1", b // 4), ("A", h, par), "vones"], writes=[ok_], inc=True)
                if st["last"]:
                    op("dve", lambda e: e.reciprocal(out=rden[h][64:128, :], in_=ob[64:128, :]),
                       reads=[ok_], writes=[("rden", h)])
                    op("dve", lambda e: e.tensor_tensor(out=attnT[h * 64:(h + 1) * 64, pr, G * 512:(G + 1) * 512],
                                                        in0=ob[0:64, :], in1=rden[h][64:128, :], op=ALU.mult),
                       reads=[ok_, ("rden", h)], writes=[("attnT", pr, G)])

            for s in range(-2, N + 2):
                if 0 <= s - 2 < N:
                    AV(steps[s - 2])
                if 0 <= s + 2 < N:
                    QK(steps[s + 2])
                if 0 <= s < N:
                    PX(steps[s])

        load_w(0)
        for _ in proj_pair(0):
            pass
        for pr in range(8):
            if pr + 1 < 8:
                load_w(pr + 1)
            if pr < 4:
                attn_sb(pr, proj_pair(pr + 1) if pr + 1 < 4 else None)
            else:
                if pr == 4:
                    op("pool", lambda e: e.memset(KT[64:67, :], -1.0), writes=[("KT", i) for i in range(8)] + ["kaug"])
                    op("pool", lambda e: e.memset(KTB[0:64, :], 0.0), writes=[("KTB", i) for i in range(8)] + ["kaug0"])
                    op("pool", lambda e: e.memset(KTB[0:3, :], -1.0), reads=["kaug0"], writes=["kaug"])
                    op("pool", lambda e: e.memset(QTB[0:64, :], 0.0), writes=[("QTaug", 1)] + [("QTB", i) for i in range(4)])
                    op("pool", lambda e: e.memset(VT[:, :, :, 64:128], 1.0),
                       writes=[("VT0", i) for i in range(8)] + [("VT1", i) for i in range(8)] + ["vones"])
                for _ in proj_pair(pr):
                    pass
                if pr == 4:
                    for hf in range(2):
                        dma("pool", wbg_s[:, :, hf * D:(hf + 1) * D], w_in[:, :, OFF_BG + hf * D:OFF_BG + (hf + 1) * D],
                            writes=[("wbg_s", hf)], key=("cv0", hf))
                    dma("pool", wsb_s, w_sb, writes=["wsb_s"], key="cv1")
                    dma("pool", wfx_s, w_fx, writes=["wfx_s"], key="cv2")
                if pr == 5:
                    for hf in range(2):
                        dma("pool", wg_s[:, :, hf * 1408:(hf + 1) * 1408], w_g[:, :, hf * 1408:(hf + 1) * 1408],
                            writes=[("wg_s", hf)], key=("cv3", hf))
                if pr == 6:
                    for hf in range(2):
                        dma("pool", wu_s[:, :, hf * 1408:(hf + 1) * 1408], w_u[:, :, hf * 1408:(hf + 1) * 1408],
                            writes=[("wu_s", hf)], key=("cv4", hf))
                if pr == 7:
                    for hf in range(2):
                        dma("sp", uT_oth[:, :, hf * D:(hf + 1) * D], wbg_s[:, :, hf * D:(hf + 1) * D], reads=[("wbg_s", hf)],
                            writes=[("wbg", hf)] + [("uT", g_) for g_ in range(0, NB, 2)], key=("w0", hf))
                attn_fox(pr)
        if debug:
            dma("sp", dbg["attnT"], attnT, reads=[("attnT", pr, G) for pr in range(8) for G in range(4)], key="dbg9")
        S_.barrier()
        AR.release(m_att)

        mC1 = AR.mark()
        wbg = AR.alloc(BF16, KC, 2 * D)
        wsb_b = AR.alloc(BF16, 4, D)
        wfx_b = AR.alloc(BF16, 4, D)
        wo_b = AR.alloc(BF16, KC, D)
        lnB1 = AR.alloc(F32, 2, D)
        gT = [AR.alloc(BF16, 512) for _ in range(2)]
        tmpa = [AR.alloc(F32, 512) for _ in range(2)]
        tmpb = [AR.alloc(F32, 512) for _ in range(2)]
        mT = [AR.alloc(BF16, KC, 512) for _ in range(2)]
        xinB = [AR.alloc(F32, D) for _ in range(2)]
        r1 = [AR.alloc(F32, D) for _ in range(2)]
        x1b = [AR.alloc(F32, D) for _ in range(2)]
        sttB = [AR.alloc(F32, 16) for _ in range(4)]
        smlB = [AR.alloc(F32, 4) for _ in range(4)]

        assert wbg.offset == uT_oth.offset, (wbg.offset, uT_oth.offset)
        dma("act", wsb_b, wsb_s, reads=["wsb_s"], writes=["wsb"], key="w1")
        dma("act", wfx_b, wfx_s, reads=["wfx_s"], writes=["wfx"], key="w2")
        for i in range(2):
            dma("sp", lnB1[:, i, :], lnp[i:i + 1, :].partition_broadcast(128), writes=[("lnB1", i)], key=("lnB1", i))
        for kc in range(KC):
            st_ = xinB[kc % 2]
            dma("sp", st_, w_o[:, kc, :], writes=[("xinB", kc % 2)], key=("xinB", kc % 2))
            op("dve", lambda e, st_=st_, kc=kc: e.tensor_tensor(out=wo_b[:, kc, :], in0=st_, in1=GB[:, 0, :], op=ALU.mult),
               reads=[("xinB", kc % 2), ("GB", 0, 0), ("GB", 0, 1)], writes=[("wo", kc)])

        def own_rows(j):
            g = 2 * j + 1
            return slice(g * 128, (g + 1) * 128)

        def nloop_thread(G):
            ukeys = [("uT", 2 * j + 1) for j in range(4 * G, 4 * G + 4)]
            mt = mT[G % 2]
            for n in range(8):
                par = n % 2
                for w_ in range(2):
                    col = w_ * D + n * 128
                    for kc in range(KC):
                        op("pe", lambda e, w_=w_, kc=kc, col=col: e.matmul(bank(w_), wbg[:, kc, col:col + 128],
                                                                        uT_own[:, kc, G * 512:(G + 1) * 512],
                                                                        start=(kc == 0), stop=(kc == KC - 1)),
                           reads=[("wbg", w_)] + ukeys, writes=[("ps", w_)], inc=(kc == KC - 1))
                    op("act", lambda e, w_=w_, n=n: e.activation(out=gT[w_], in_=bank(w_), func=AF.Sigmoid,
                                                                bias=bgcol[:, w_ * 8 + n:w_ * 8 + n + 1], scale=1.0),
                       reads=[("ps", w_), "bgcol"], writes=[("gT", w_)])
                for w_, wt in enumerate((wsb_b, wfx_b)):
                    for kc in range(4):
                        op("pe", lambda e, kc=kc, wt=wt, w_=w_, n=n: e.matmul(
                            bank(2 + w_), wt[:, kc, n * 128:(n + 1) * 128], attnT[:, w_ * 4 + kc, G * 512:(G + 1) * 512],
                            start=(kc == 0), stop=(kc == 3)),
                           reads=["wsb", "wfx"] + [("attnT", w_ * 4 + kc, G)], writes=[("ps", 2 + w_)], inc=(kc == 3))
                op("dve", lambda e, par=par: e.tensor_tensor(out=tmpa[par], in0=bank(2), in1=gT[0], op=ALU.mult),
                   reads=[("ps", 2), ("gT", 0)], writes=[("tmpa", par)])
                op("dve", lambda e, par=par: e.tensor_tensor(out=tmpb[par], in0=bank(3), in1=gT[1], op=ALU.mult),
                   reads=[("ps", 3), ("gT", 1)], writes=[("tmpb", par)])
                op("pool", lambda e, par=par, n=n: e.tensor_tensor(out=mt[:, n, :], in0=tmpa[par], in1=tmpb[par], op=ALU.add),
                   reads=[("tmpa", par), ("tmpb", par)], writes=[("mT", G % 2, n)])
                yield

        def stats_ops(src, rks, k4):
            st, sm = sttB[k4], smlB[k4]
            op("dve", lambda e: e.bn_stats(out=st[:, 0:6], in_=src[:, 0:512]), reads=rks, writes=[("st", k4, 0)])
            op("dve", lambda e: e.bn_stats(out=st[:, 6:12], in_=src[:, 512:1024]), reads=rks, writes=[("st", k4, 1)])
            op("dve", lambda e: e.bn_aggr(out=st[:, 12:14], in_=st[:, 0:12]), reads=[("st", k4, 0), ("st", k4, 1)],
               writes=[("mv", k4)])
            op("act", lambda e: e.activation(out=sm[:, 0:1], in_=st[:, 13:14], func=AF.Sqrt, bias=epsc[:, 0:1], scale=1.0),
               reads=[("mv", k4), "epsc"], writes=[("sd", k4)])

        def rstd_ops(k4):
            st, sm = sttB[k4], smlB[k4]
            op("dve", lambda e: e.reciprocal(out=sm[:, 1:2], in_=sm[:, 0:1]), reads=[("sd", k4)], writes=[("rstd", k4)])
            op("dve", lambda e: e.tensor_scalar(out=sm[:, 2:3], in0=st[:, 12:13], scalar1=-1.0, scalar2=sm[:, 1:2],
                                                op0=ALU.mult, op1=ALU.mult),
               reads=[("mv", k4), ("rstd", k4)], writes=[("nmr", k4)])

        def tile_thread(j):
            G, i, k = j // 4, j % 4, j % 2
            ka, kb = (2 * j) % 4, (2 * j + 1) % 4
            mt = mT[G % 2]
            dma("sp", xinB[k], xall[own_rows(j), :], writes=[("xinB", k)], key=("xinB", k))
            for hf in range(2):
                for kc in range(KC):
                    op("pe", lambda e, kc=kc, hf=hf: e.matmul(bank(4 + 2 * k + hf), mt[:, kc, i * 128:(i + 1) * 128],
                                                           wo_b[:, kc, hf * 512:(hf + 1) * 512],
                                                           start=(kc == 0), stop=(kc == KC - 1)),
                       reads=[("mT", G % 2, kc), ("wo", kc)], writes=[("ps", 4 + 2 * k + hf)], inc=(kc == KC - 1))
            yield
            for hf in range(2):
                op("dve", lambda e, hf=hf: e.scalar_tensor_tensor(
                    out=r1[k][:, hf * 512:(hf + 1) * 512], in0=xinB[k][:, hf * 512:(hf + 1) * 512], scalar=ALPHA,
                    in1=bank(4 + 2 * k + hf), op0=ALU.mult, op1=ALU.add),
                   reads=[("xinB", k), ("ps", 4 + 2 * k + hf)], writes=[("r1", k)])
            stats_ops(r1[k], [("r1", k)], ka)
            yield
            rstd_ops(ka)
            sm = smlB[ka]
            op("act", lambda e: e.activation(out=r1[k], in_=r1[k], func=AF.Identity, bias=sm[:, 2:3], scale=sm[:, 1:2]),
               reads=[("r1", k), ("rstd", ka), ("nmr", ka)], writes=[("r1", k)])
            yield
            op("pool", lambda e: e.tensor_tensor(out=x1b[k], in0=r1[k], in1=lnB1[:, 0, :], op=ALU.mult),
               reads=[("r1", k), ("lnB1", 0)], writes=[("x1", k)])
            op("pool", lambda e: e.tensor_tensor(out=x1b[k], in0=x1b[k], in1=lnB1[:, 1, :], op=ALU.add),
               reads=[("x1", k), ("lnB1", 1)], writes=[("x1", k)])
            yield
            dma("sp", x1_scr[j * 128:(j + 1) * 128, :], x1b[k], reads=[("x1", k)], writes=[("x1scr", j)], key=("x1o", k))
            if debug:
                dma("sp", dbg["x1"][j * 128:(j + 1) * 128, :], x1b[k], reads=[("x1", k)], key=("dbgx1", k))
            stats_ops(x1b[k], [("x1", k)], kb)
            yield
            rstd_ops(kb)
            sm2 = smlB[kb]
            op("act", lambda e: e.activation(out=r1[k], in_=x1b[k], func=AF.Identity, bias=sm2[:, 2:3], scale=sm2[:, 1:2]),
               reads=[("x1", k), ("rstd", kb), ("nmr", kb)], writes=[("r1", k)])
            yield
            for kc in range(KC):
                bk = 4 + 2 * k + kc // 4
                op("pe", lambda e, bk=bk, kc=kc: e.transpose(bank(bk)[:, (kc % 4) * 128:(kc % 4 + 1) * 128],
                                                         r1[k][:, kc * 128:(kc + 1) * 128], ident_f),
                   reads=[("r1", k), "ident_f"], writes=[("ps", bk)], inc=(kc % 4 == 3))
            yield
            for kc in range(KC):
                bk = 4 + 2 * k + kc // 4
                src = bank(bk)[:, (kc % 4) * 128:(kc % 4 + 1) * 128]
                if kc // 4 == 0:
                    op("dve", lambda e, src=src, kc=kc: e.tensor_scalar(
                        out=uT_own[:, kc, j * 128:(j + 1) * 128], in0=src,
                        scalar1=s2col[:, kc:kc + 1], scalar2=adacol[:, 24 + kc:25 + kc], op0=ALU.mult, op1=ALU.add),
                       reads=[("ps", bk), "s2col", "adacol2"], writes=[("uT", 2 * j + 1)])
                else:
                    op("act", lambda e, src=src, kc=kc: e.activation(out=uT_own[:, kc, j * 128:(j + 1) * 128], in_=src, func=AF.Identity,
                                                                 bias=adacol[:, 24 + kc:25 + kc], scale=s2col[:, kc:kc + 1]),
                       reads=[("ps", bk), "s2col", "adacol2"], writes=[("uT", 2 * j + 1)])
            yield

        for _ in nloop_thread(0):
            pass
        nl_done = 1
        rnd = 0
        nl = None
        next_tile = 0
        tiles = []
        while next_tile < NOWN or tiles or nl is not None:
            while len(tiles) < 2 and next_tile < NOWN and next_tile // 4 < nl_done:
                tiles.append(tile_thread(next_tile))
                next_tile += 1
            if nl is None and nl_done < 4 and next_tile >= 4 * (nl_done - 1):
                nl = nloop_thread(nl_done)
            progressed = False
            for t in list(tiles):
                try:
                    next(t)
                    progressed = True
                except StopIteration:
                    tiles.remove(t)
            rnd += 1
            if nl is not None and (rnd % 2 == 0 or not tiles):
                try:
                    next(nl)
                    progressed = True
                except StopIteration:
                    nl = None
                    nl_done += 1
            if not progressed and not tiles and nl is None and next_tile >= NOWN:
                break
        S_.barrier()
        AR.release(m_attnT)

        wg_b = AR.alloc(BF16, KC, DFF)
        wu_b = AR.alloc(BF16, KC, DFF)
        wd_b = AR.alloc(BF16, FC, D)
        hT = AR.alloc(BF16, FC, 256)
        sgt = [AR.alloc(F32, 256) for _ in range(2)]
        xin = [AR.alloc(F32, D) for _ in range(2)]
        r2 = xin
        lnB = GB
        stt = [AR.alloc(F32, 16) for _ in range(2)]
        sml = [AR.alloc(F32, 4) for _ in range(2)]
        CG = [(0, 768), (768, 1536), (1536, 2304), (2304, 2816)]
        for ci, (a, b_) in enumerate(CG):
            dma("act", wg_b[:, :, a:b_], wg_s[:, :, a:b_], reads=[("wg_s", 0), ("wg_s", 1)], writes=[("wg", ci)], key=("w3", ci))
            dma("act", wu_b[:, :, a:b_], wu_s[:, :, a:b_], reads=[("wu_s", 0), ("wu_s", 1)], writes=[("wu", ci)], key=("w4", ci))
        for fc in range(FC):
            st_ = xin[fc % 2]
            dma("sp", st_, w_d[:, fc, :], writes=[("xin2", fc % 2)], key=("xin", fc % 2))
            op("dve", lambda e, st_=st_, fc=fc: e.tensor_tensor(out=wd_b[:, fc, :], in0=st_, in1=GB[:, 1, :], op=ALU.mult),
               reads=[("xin2", fc % 2), ("GB", 1, 0), ("GB", 1, 1)], writes=[("wd", fc)])
        for i in range(2):
            dma("sp", lnB[:, i, :], lnp[2 + i:3 + i, :].partition_broadcast(128), writes=[("GB", i, 0), ("GB", i, 1)],
                key=("lnB", i))
        for sub in range(8):
            cs = sub * 256
            ukeys = [("uT", 2 * j + 1) for j in (2 * sub, 2 * sub + 1)]
            for fc in range(FC):
                par = fc % 2
                bg, bu = bank(par), bank(2 + par)
                for kc in range(KC):
                    op("pe", lambda e, bg=bg, kc=kc, fc=fc, cs=cs: e.matmul(bg[:, 0:256], wg_b[:, kc, fc * 128:(fc + 1) * 128],
                                                                       uT_own[:, kc, cs:cs + 256], start=(kc == 0), stop=(kc == KC - 1)),
                       reads=[("wg", min(fc // 6, 3))] + ukeys, writes=[("ps", par)], inc=(kc == KC - 1))
                for kc in range(KC):
                    op("pe", lambda e, bu=bu, kc=kc, fc=fc, cs=cs: e.matmul(bu[:, 0:256], wu_b[:, kc, fc * 128:(fc + 1) * 128],
                                                                       uT_own[:, kc, cs:cs + 256], start=(kc == 0), stop=(kc == KC - 1)),
                       reads=[("wu", min(fc // 6, 3))] + ukeys, writes=[("ps", 2 + par)], inc=(kc == KC - 1))
                op("act", lambda e, bg=bg, par=par: e.activation(out=sgt[par], in_=bg[:, 0:256], func=AF.Silu),
                   reads=[("ps", par)], writes=[("sgt", par)])
                op("dve", lambda e, bu=bu, par=par, fc=fc: e.tensor_tensor(out=hT[:, fc, :], in0=bu[:, 0:256], in1=sgt[par], op=ALU.mult),
                   reads=[("ps", 2 + par), ("sgt", par)], writes=[("hT", fc)])
            for i in range(2):
                j = 2 * sub + i
                k = j % 2
                dma("sp", xin[k], x1_scr[j * 128:(j + 1) * 128, :], reads=[("x1scr", j)], writes=[("xin2", k)], key=("xin", k))
                for hf in range(2):
                    bk = 4 + 2 * k + hf
                    for fc in range(FC):
                        op("pe", lambda e, bk=bk, fc=fc, i=i, hf=hf: e.matmul(bank(bk), hT[:, fc, i * 128:(i + 1) * 128],
                                                                           wd_b[:, fc, hf * 512:(hf + 1) * 512],
                                                                           start=(fc == 0), stop=(fc == FC - 1)),
                           reads=[("hT", fc), ("wd", fc)], writes=[("ps", bk)], inc=(fc == FC - 1))
                    op("dve", lambda e, bk=bk, k=k, hf=hf: e.scalar_tensor_tensor(
                        out=r2[k][:, hf * 512:(hf + 1) * 512], in0=xin[k][:, hf * 512:(hf + 1) * 512], scalar=ALPHA,
                        in1=bank(bk), op0=ALU.mult, op1=ALU.add),
                       reads=[("xin2", k), ("ps", bk)], writes=[("xin2", k)])
                st, sm = stt[k], sml[k]
                op("dve", lambda e, st=st, k=k: e.bn_stats(out=st[:, 0:6], in_=r2[k][:, 0:512]), reads=[("xin2", k)], writes=[("st", k, 0)])
                op("dve", lambda e, st=st, k=k: e.bn_stats(out=st[:, 6:12], in_=r2[k][:, 512:1024]), reads=[("xin2", k)], writes=[("st", k, 1)])
                op("dve", lambda e, st=st: e.bn_aggr(out=st[:, 12:14], in_=st[:, 0:12]), reads=[("st", k, 0), ("st", k, 1)],
                   writes=[("mv", k)])
                op("act", lambda e, st=st, sm=sm: e.activation(out=sm[:, 0:1], in_=st[:, 13:14], func=AF.Sqrt, bias=epsc[:, 0:1], scale=1.0),
                   reads=[("mv", k), "epsc"], writes=[("sd", k)])
                op("dve", lambda e, sm=sm: e.reciprocal(out=sm[:, 1:2], in_=sm[:, 0:1]), reads=[("sd", k)], writes=[("rstd", k)])
                op("dve", lambda e, st=st, sm=sm: e.tensor_scalar(out=sm[:, 2:3], in0=st[:, 12:13], scalar1=-1.0, scalar2=sm[:, 1:2],
                                                                  op0=ALU.mult, op1=ALU.mult),
                   reads=[("mv", k), ("rstd", k)], writes=[("nmr", k)])
                op("act", lambda e, k=k, sm=sm: e.activation(out=r2[k], in_=r2[k], func=AF.Identity, bias=sm[:, 2:3], scale=sm[:, 1:2]),
                   reads=[("xin2", k), ("rstd", k), ("nmr", k)], writes=[("xin2", k)])
                op("pool", lambda e, k=k: e.tensor_tensor(out=r2[k], in0=r2[k], in1=lnB[:, 0, :], op=ALU.mult),
                   reads=[("xin2", k), ("GB", 0, 0), ("GB", 0, 1)], writes=[("xin2", k)])
                op("pool", lambda e, k=k: e.tensor_tensor(out=r2[k], in0=r2[k], in1=lnB[:, 1, :], op=ALU.add),
                   reads=[("xin2", k), ("GB", 1, 0), ("GB", 1, 1)], writes=[("xin2", k)])
                dma("sp", out[j * 128:(j + 1) * 128, :], r2[k], reads=[("xin2", k)], writes=[("out", j)], key=("outd", k))
        S_.wait_all("sp", [("outd", 0), ("outd", 1)] + ([k for k in S_.dma_cnt if str(k).startswith("dbg") or (isinstance(k, tuple) and k[0] == "dbgx1")] if debug else []))
        print("arena peak bytes", AR.peak, "pe", S_.cnt["pe"], "act", S_.cnt["act"], "dve", S_.cnt["dve"], "pool", S_.cnt["pool"],
              "instr", {k: len(v) for k, v in S_.prog.items()})

        with nc.Block() as block:
            @block.tensor
            def _(e):
                S_.emit("pe", e)

            @block.scalar
            def _(e):
                S_.emit("act", e)

            @block.vector
            def _(e):
                S_.emit("dve", e)

            @block.gpsimd
            def _(e):
                S_.emit("pool", e)

            @block.sync
            def _(e):
                S_.emit("sp", e)
    return nc


_CONSTS = None


def _consts():
    global _CONSTS
    if _CONSTS is None:
        s = np.arange(128)[:, None]
        t = np.arange(128)[None, :]
        ident = (s == t).astype(np.float32)
        triS = np.where(s < t, 0.0, NEG).astype(np.float32)
        triI = np.where(s <= t, 0.0, NEG).astype(np.float32)
        negU = np.where(s >= t, -1.0, 0.0).astype(np.float32)
        _CONSTS = np.concatenate([ident, triS, triI, negU], axis=1)
    return _CONSTS


def _rk(w, kc):
    n = w.shape[1]
    return np.ascontiguousarray(w.reshape(kc, 128, n).transpose(1, 0, 2))


def make_in_maps(x, c, w_ada, b_ada, w_in, b_gate, b_forget, w_sb_out, w_fox_out, w_o,
                 ln1_g, ln1_b, w_ffn_gate, w_ffn_up, w_ffn_down, ln2_g, ln2_b):
    f = np.float32
    x = np.asarray(x, f)
    shared = {
        "w_ada": _rk(np.asarray(w_ada, f)[0], KC),
        "b_ada": np.ascontiguousarray(np.asarray(b_ada, f)[0][None, :]),
        "w_in": _rk(np.asarray(w_in, f)[0], KC),
        "b_gate": np.ascontiguousarray(np.asarray(b_gate, f)[0].reshape(16, 128).T),
        "b_forget": np.ascontiguousarray(np.asarray(b_forget, f)[0].reshape(8, 1)),
        "w_sb": _rk(np.asarray(w_sb_out, f)[0], 4),
        "w_fx": _rk(np.asarray(w_fox_out, f)[0], 4),
        "w_o": _rk(np.asarray(w_o, f)[0], KC),
        "lnp": np.ascontiguousarray(np.stack([np.asarray(a, f)[0] for a in (ln1_g, ln1_b, ln2_g, ln2_b)])),
        "w_g": _rk(np.asarray(w_ffn_gate, f)[0], KC),
        "w_u": _rk(np.asarray(w_ffn_up, f)[0], KC),
        "w_d": _rk(np.asarray(w_ffn_down, f)[0], FC),
    }
    cb = _consts()
    maps = []
    for core in range(8):
        b, p = core // 2, core % 2
        if p == 1:
            xa = x[b]
        else:
            xa = np.concatenate([np.zeros((128, D), f), x[b][:S - 128]], axis=0)
        dmask = np.full((128, 512), NEG if p == 0 else 0.0, f)
        m = dict(shared)
        m["xall"] = np.ascontiguousarray(xa)
        m["c_col"] = np.ascontiguousarray(np.asarray(c, f)[b].reshape(KC, 128).T)
        m["consts"] = np.ascontiguousarray(np.concatenate([cb, dmask], axis=1))
        maps.append(m)
    return maps


def assemble(results):
    out = np.zeros((4, S, D), np.float32)
    for core in range(8):
        b, p = core // 2, core % 2
        o = np.asarray(results[core]["out"], np.float32).reshape(NOWN, 128, D)
        ov = out[b].reshape(NB, 128, D)
        for j in range(NOWN):
            ov[2 * j + p] = o[j]
    return out


def kernel(**inputs):
    nc = build_nc(debug=False)
    maps = make_in_maps(**inputs)
    res = run_bass_kernel_spmd(nc, maps, core_ids=list(range(8)))
    return assemble(res.results)
```
